# Optimizing a Trainium2 kernel written in Bass

```python
import math
import jax, jax.numpy as jnp
from jax import lax
import numpy as np

D_MODEL = 1024
BATCH = 8
SEQ = 4096
DEPTH = 2

MEM_LEN = 256
D_MIX = D_MODEL
ATT_HEADS = 8
ATT_KV_HEADS = 2
ATT_HEAD_DIM = D_MIX // (2 * ATT_HEADS)
WINDOW = 128
ATT_BLOCK = 128
GDN_HEADS = 4
GDN_HEAD_DIM = D_MIX // (2 * GDN_HEADS)
CONV_K = 4
GDN_CHUNK = 64
X_HEADS = 4
X_HEAD_DIM = D_MODEL // X_HEADS
D_FF = 4 * D_MODEL
ALPHA = (2 * DEPTH) ** 0.25
BETA_INIT = (8 * DEPTH) ** -0.25
LN_EPS = 1e-5
RMS_EPS = 1e-6
NEG_INF = -1e30

ATT_Q = ATT_HEADS * ATT_HEAD_DIM
ATT_KV = ATT_KV_HEADS * ATT_HEAD_DIM
GDN_W = GDN_HEADS * GDN_HEAD_DIM
IN_SPLITS = (ATT_Q, ATT_KV, ATT_KV, 3 * GDN_W, GDN_HEADS, GDN_HEADS, GDN_W)
D_IN = ATT_Q + 2 * ATT_KV + 4 * GDN_W + 2 * GDN_HEADS

kernel_name = "hybrid_swa_sink_gdn_deepnorm"


def _split_points(sizes):
    pts, acc = [], 0
    for s in sizes[:-1]:
        acc += s
        pts.append(acc)
    return pts


def alibi_slopes(n):
    return jnp.exp2(-8.0 * (jnp.arange(n, dtype=jnp.float32) + 1.0) / n)


def layer_norm(x, g, b):
    xf = x.astype(jnp.float32)
    mu = xf.mean(-1, keepdims=True)
    var = jnp.square(xf - mu).mean(-1, keepdims=True)
    y = (xf - mu) * lax.rsqrt(var + LN_EPS) * g.astype(jnp.float32) + b.astype(jnp.float32)
    return y.astype(x.dtype)


def l2norm(t):
    return t * lax.rsqrt(jnp.sum(t * t, axis=-1, keepdims=True) + RMS_EPS)


def sliding_window_attention(q, k, v, sinks):
    B, S = q.shape[0], q.shape[1]
    nb = S // ATT_BLOCK
    G = ATT_HEADS // ATT_KV_HEADS
    qb = q.reshape(B, nb, ATT_BLOCK, ATT_KV_HEADS, G, ATT_HEAD_DIM)

    def band(t):
        tp = jnp.pad(t, ((0, 0), (ATT_BLOCK, 0), (0, 0), (0, 0)))
        tp = tp.reshape(B, nb + 1, ATT_BLOCK, ATT_KV_HEADS, ATT_HEAD_DIM)
        return jnp.concatenate([tp[:, :-1], tp[:, 1:]], axis=2)

    kb, vb = band(k), band(v)
    s = jnp.einsum('bnqhgd,bnkhd->bnhgqk', qb, kb).astype(jnp.float32) * (ATT_HEAD_DIM ** -0.5)
    qi = jnp.arange(ATT_BLOCK)[:, None]
    kj = jnp.arange(2 * ATT_BLOCK)[None, :]
    dist = qi + ATT_BLOCK - kj
    key_pos = jnp.arange(nb)[:, None, None] * ATT_BLOCK + kj[None] - ATT_BLOCK
    valid = (dist >= 0) & (dist < WINDOW) & (key_pos >= 0)
    slopes = alibi_slopes(ATT_HEADS).reshape(ATT_KV_HEADS, G)
    bias = -slopes[:, :, None, None] * dist.astype(jnp.float32)
    s = jnp.where(valid[None, :, None, None], s + bias, NEG_INF)
    sink = jnp.broadcast_to(sinks.astype(jnp.float32).reshape(ATT_KV_HEADS, G, 1, 1),
                            s.shape[:-1] + (1,))
    p = jax.nn.softmax(jnp.concatenate([s, sink], axis=-1), axis=-1)[..., :-1]
    o = jnp.einsum('bnhgqk,bnkhd->bnqhgd', p.astype(v.dtype), vb)
    return o.reshape(B, S, ATT_Q)


def causal_depthwise_conv(x, w):
    C = x.shape[-1]
    return lax.conv_general_dilated(
        x, w[:, None, :].astype(x.dtype), window_strides=(1,), padding=[(CONV_K - 1, 0)],
        dimension_numbers=('NWC', 'WIO', 'NWC'), feature_group_count=C)


def gated_delta_rule(q, k, v, g, beta):
    B, S, H, Dk = q.shape
    Dv = v.shape[-1]
    C = GDN_CHUNK
    n = S // C

    def chunks(t):
        t = t.reshape((B, n, C, H) + t.shape[3:])
        return jnp.moveaxis(t, 3, 1)

    qc, kc, vc = chunks(q * (Dk ** -0.5)), chunks(k), chunks(v)
    gc, bc = chunks(g), chunks(beta)
    decay = jnp.cumsum(gc, axis=-1)
    causal = jnp.tril(jnp.ones((C, C), dtype=bool))
    strict = jnp.tril(jnp.ones((C, C), dtype=bool), -1)
    diff = decay[..., :, None] - decay[..., None, :]
    lmask = jnp.where(causal, jnp.exp(jnp.where(causal, diff, 0.0)), 0.0)
    kbeta = kc * bc[..., None]
    A = jnp.where(strict, jnp.einsum('bhnid,bhnjd->bhnij', kbeta, kc) * lmask, 0.0)
    eye = jnp.eye(C, dtype=jnp.float32)
    T = lax.linalg.triangular_solve(eye + A, jnp.broadcast_to(eye, A.shape),
                                    left_side=True, lower=True)
    u = jnp.einsum('bhnij,bhnjd->bhnid', T, vc * bc[..., None])
    w = jnp.einsum('bhnij,bhnjd->bhnid', T, kbeta * jnp.exp(decay)[..., None])
    qd = qc * jnp.exp(decay)[..., None]
    kd = kc * jnp.exp(decay[..., -1:] - decay)[..., None]
    intra = jnp.where(causal, jnp.einsum('bhnid,bhnjd->bhnij', qc, kc) * lmask, 0.0)
    cd = jnp.exp(decay[..., -1])

    def step(state, inp):
        qd_i, kd_i, u_i, w_i, a_i, cd_i = inp
        v_new = u_i - jnp.einsum('bhik,bhkv->bhiv', w_i, state)
        o_i = jnp.einsum('bhik,bhkv->bhiv', qd_i, state) + jnp.einsum('bhij,bhjv->bhiv', a_i, v_new)
        state = state * cd_i[..., None, None] + jnp.einsum('bhik,bhiv->bhkv', kd_i, v_new)
        return state, o_i

    xs = (jnp.moveaxis(qd, 2, 0), jnp.moveaxis(kd, 2, 0), jnp.moveaxis(u, 2, 0),
          jnp.moveaxis(w, 2, 0), jnp.moveaxis(intra, 2, 0), jnp.moveaxis(cd, 2, 0))
    state0 = jnp.zeros((B, H, Dk, Dv), jnp.float32)
    _, o = lax.scan(step, state0, xs)
    return jnp.moveaxis(o, 0, 2).transpose(0, 2, 3, 1, 4).reshape(B, S, H, Dv)


def hybrid_mixer(x, w_in, conv_w, sinks, a_log, dt_bias, norm_g, w_out):
    B, S, _ = x.shape
    h = x @ w_in
    aq, ak, av, gqkv, ga, gb, gz = jnp.split(h, _split_points(IN_SPLITS), axis=-1)
    att = sliding_window_attention(aq.reshape(B, S, ATT_HEADS, ATT_HEAD_DIM),
                                   ak.reshape(B, S, ATT_KV_HEADS, ATT_HEAD_DIM),
                                   av.reshape(B, S, ATT_KV_HEADS, ATT_HEAD_DIM), sinks)
    qkv = jax.nn.silu(causal_depthwise_conv(gqkv, conv_w)).astype(jnp.float32)
    q, k, v = jnp.split(qkv, 3, axis=-1)
    q = l2norm(q.reshape(B, S, GDN_HEADS, GDN_HEAD_DIM))
    k = l2norm(k.reshape(B, S, GDN_HEADS, GDN_HEAD_DIM))
    v = v.reshape(B, S, GDN_HEADS, GDN_HEAD_DIM)
    beta = jax.nn.sigmoid(gb.astype(jnp.float32))
    g = -jnp.exp(a_log.astype(jnp.float32)) * jax.nn.softplus(ga.astype(jnp.float32) + dt_bias.astype(jnp.float32))
    o = gated_delta_rule(q, k, v, g, beta)
    z = gz.astype(jnp.float32).reshape(B, S, GDN_HEADS, GDN_HEAD_DIM)
    o = (o * lax.rsqrt(jnp.mean(o * o, axis=-1, keepdims=True) + RMS_EPS)
         * norm_g.astype(jnp.float32) * jax.nn.silu(z))
    mix = jnp.concatenate([att, o.reshape(B, S, GDN_W).astype(x.dtype)], axis=-1)
    return mix @ w_out


def memory_cross_attention(x, mem, wq, wk, wv, wo):
    B, S, _ = x.shape
    M = mem.shape[1]
    q = (x @ wq).reshape(B, S, X_HEADS, X_HEAD_DIM)
    k = (mem @ wk).reshape(B, M, X_HEADS, X_HEAD_DIM)
    v = (mem @ wv).reshape(B, M, X_HEADS, X_HEAD_DIM)
    s = jnp.einsum('bshd,bmhd->bhsm', q, k).astype(jnp.float32) * (X_HEAD_DIM ** -0.5)
    p = jax.nn.softmax(s, axis=-1).astype(v.dtype)
    o = jnp.einsum('bhsm,bmhd->bshd', p, v).reshape(B, S, D_MODEL)
    return o @ wo


def squared_relu_mlp(x, w1, w2):
    return jnp.square(jax.nn.relu(x @ w1)) @ w2


def setup_inputs(seed: int = 0) -> dict:
    key = jax.random.key(seed)
    ks = jax.random.split(key, 20)
    nrm = jax.random.normal
    f32 = jnp.float32
    x = nrm(ks[0], (BATCH, SEQ, D_MODEL), f32)
    mem = nrm(ks[1], (BATCH, MEM_LEN, D_MODEL), f32)
    w_in = nrm(ks[2], (DEPTH, D_MODEL, D_IN), f32) * D_MODEL ** -0.5
    conv_w = nrm(ks[3], (DEPTH, CONV_K, 3 * GDN_W), f32) * CONV_K ** -0.5
    attn_sinks = nrm(ks[4], (DEPTH, ATT_HEADS), f32) * 0.5
    a_log = jnp.log(jax.random.uniform(ks[5], (DEPTH, GDN_HEADS), f32, 1.0, 16.0))
    dt = jnp.exp(jax.random.uniform(ks[6], (DEPTH, GDN_HEADS), f32, math.log(1e-3), math.log(1e-1)))
    dt_bias = dt + jnp.log(-jnp.expm1(-dt))
    gdn_norm_g = 1.0 + 0.02 * nrm(ks[7], (DEPTH, GDN_HEAD_DIM), f32)
    w_mix_out = nrm(ks[8], (DEPTH, D_MIX, D_MODEL), f32) * (D_MIX ** -0.5) * BETA_INIT
    wq_mem = nrm(ks[9], (DEPTH, D_MODEL, D_MODEL), f32) * D_MODEL ** -0.5
    wk_mem = nrm(ks[10], (DEPTH, D_MODEL, D_MODEL), f32) * D_MODEL ** -0.5
    wv_mem = nrm(ks[11], (DEPTH, D_MODEL, D_MODEL), f32) * D_MODEL ** -0.5
    wo_mem = nrm(ks[12], (DEPTH, D_MODEL, D_MODEL), f32) * (D_MODEL ** -0.5) * BETA_INIT
    w_ff1 = nrm(ks[13], (DEPTH, D_MODEL, D_FF), f32) * D_MODEL ** -0.5
    w_ff2 = nrm(ks[14], (DEPTH, D_FF, D_MODEL), f32) * (D_FF ** -0.5) * BETA_INIT
    ln_g = 1.0 + 0.02 * nrm(ks[15], (DEPTH, 3, D_MODEL), f32)
    ln_b = 0.02 * nrm(ks[16], (DEPTH, 3, D_MODEL), f32)
    return {"x": x, "mem": mem, "w_in": w_in, "conv_w": conv_w, "attn_sinks": attn_sinks,
            "a_log": a_log, "dt_bias": dt_bias, "gdn_norm_g": gdn_norm_g, "w_mix_out": w_mix_out,
            "wq_mem": wq_mem, "wk_mem": wk_mem, "wv_mem": wv_mem, "wo_mem": wo_mem,
            "w_ff1": w_ff1, "w_ff2": w_ff2, "ln_g": ln_g, "ln_b": ln_b}


def reference(x, mem, w_in, conv_w, attn_sinks, a_log, dt_bias, gdn_norm_g, w_mix_out,
              wq_mem, wk_mem, wv_mem, wo_mem, w_ff1, w_ff2, ln_g, ln_b):
    for l in range(DEPTH):
        y = hybrid_mixer(x, w_in[l], conv_w[l], attn_sinks[l], a_log[l], dt_bias[l],
                         gdn_norm_g[l], w_mix_out[l])
        x = layer_norm(ALPHA * x + y, ln_g[l, 0], ln_b[l, 0])
        y = memory_cross_attention(x, mem, wq_mem[l], wk_mem[l], wv_mem[l], wo_mem[l])
        x = layer_norm(ALPHA * x + y, ln_g[l, 1], ln_b[l, 1])
        y = squared_relu_mlp(x, w_ff1[l], w_ff2[l])
        x = layer_norm(ALPHA * x + y, ln_g[l, 2], ln_b[l, 2])
    return x
```

```python
import contextlib
import numpy as np
import concourse.bass as bass
import concourse.mybir as mybir
from concourse.bass_utils import run_bass_kernel_spmd

F32 = mybir.dt.float32
BF16 = mybir.dt.bfloat16
AF = mybir.ActivationFunctionType
ALU = mybir.AluOpType

D = 1024
SEQ = 4096
DEPTH = 2
MEM = 256
DIN = 2824
DFF = 4096
ALPHA = float((2 * DEPTH) ** 0.25)
LN_EPS = 1e-5
RMS_EPS = 1e-6
NCORES = 8

ENGS = ("pe", "act", "dve", "pool", "sp")
SEM_ROLL = 30000


class _Op:
    __slots__ = ("eng", "fn", "deps", "is_dma", "sem", "val", "signal", "key", "prefetch")


class Prog:
    def __init__(self, nc):
        self.nc = nc
        self.ops = {e: [] for e in ENGS}
        self.last_w = {}
        self.readers = {}
        self.dma_cnt = {}
        self.all_ops = []

    def _deps(self, op, reads, writes, extra=()):
        pr = [r for r in reads if isinstance(r, str) and r.startswith("psum")]
        if pr:
            reads = [r for r in reads if r not in pr]
            writes = list(writes) + [r for r in pr if r not in writes]
        deps = []
        for r in reads:
            w = self.last_w.get(r)
            if w is not None:
                deps.append((w, "raw"))
        for w_ in writes:
            w = self.last_w.get(w_)
            if w is not None:
                deps.append((w, "waw"))
            for rd in self.readers.get(w_, ()):
                deps.append((rd, "war"))
        for d in extra:
            deps.append((d, "raw"))
        for r in reads:
            self.readers.setdefault(r, []).append(op)
        for w_ in writes:
            self.last_w[w_] = op
            self.readers[w_] = []
        out = []
        seen = set()
        for d, kind in deps:
            if d is op or id(d) in seen:
                continue
            if (not d.is_dma) and (not op.is_dma) and d.eng == op.eng:
                if op.eng == "pe" or (kind != "raw" and op.eng != "pool"):
                    continue
            seen.add(id(d))
            out.append(d)
        op.deps = out

    def op(self, eng, fn, reads=(), writes=(), extra=()):
        o = _Op()
        o.eng = eng
        o.fn = fn
        o.is_dma = False
        o.signal = False
        o.sem = None
        o.val = 0
        o.key = None
        o.prefetch = False
        self._deps(o, reads, writes, extra)
        self.ops[eng].append(o)
        self.all_ops.append(o)
        return o

    def dma(self, eng, out, in_, reads=(), writes=(), key=None, prefetch=False, extra=()):
        o = _Op()
        o.eng = eng
        o.is_dma = True
        o.fn = (out, in_)
        o.signal = True
        o.prefetch = prefetch
        o.key = key if key is not None else tuple(writes)[0]
        self.dma_cnt[o.key] = self.dma_cnt.get(o.key, 0) + 1
        o.val = 16 * self.dma_cnt[o.key]
        o.sem = None
        self._deps(o, reads, writes, extra)
        self.ops[eng].append(o)
        self.all_ops.append(o)
        return o

    def fence(self):
        lasts = []
        for e in ENGS:
            last = None
            for o in reversed(self.ops[e]):
                if not o.is_dma and o.fn is not None:
                    last = o
                    break
            if last is not None:
                lasts.append(last)
        dmas = {}
        for o in self.all_ops:
            if o.is_dma and not o.prefetch:
                dmas[o.key] = o
        deps = lasts + list(dmas.values())
        for e in ENGS:
            o = _Op()
            o.eng = e
            o.fn = None
            o.is_dma = False
            o.signal = False
            o.sem = None
            o.val = 0
            o.key = None
            o.prefetch = False
            o.deps = [d for d in deps if d.is_dma or d.eng != e]
            self.ops[e].append(o)
            self.all_ops.append(o)

    def emit(self, final_wait_ops=()):
        nc = self.nc
        for o in self.all_ops:
            for d in o.deps:
                d.signal = True
        for o in final_wait_ops:
            o.signal = True
        eng_sems = {}
        for e in ENGS:
            cnt = 0
            k = 0
            for o in self.ops[e]:
                if o.is_dma or not o.signal:
                    continue
                if cnt >= SEM_ROLL:
                    k += 1
                    cnt = 0
                cnt += 1
                o.sem = ("eng", e, k)
                o.val = cnt
                eng_sems[(e, k)] = True
        for o in self.all_ops:
            if o.is_dma:
                o.sem = ("dma", o.key)
        all_sem_keys = [("eng", e, k) for (e, k) in eng_sems] + [("dma", k) for k in self.dma_cnt]
        self.n_sems = len(all_sem_keys)
        stats = {"waits": 0, "ops": 0}
        with contextlib.ExitStack() as st:
            semh = {}
            for i, sk in enumerate(all_sem_keys):
                semh[sk] = st.enter_context(nc.semaphore("s%d" % i))
            block = st.enter_context(nc.Block())
            engmap = {"pe": block.tensor, "act": block.scalar, "dve": block.vector,
                      "pool": block.gpsimd, "sp": block.sync}
            for e in ENGS:
                ops = self.ops[e]
                finals = list(final_wait_ops) if e == "sp" else []
                if not ops and not finals:
                    continue

                def body(engine, ops=ops, finals=finals):
                    waited = {}

                    def wait_for(d):
                        if waited.get(d.sem, 0) >= d.val:
                            return
                        engine.wait_ge(semh[d.sem], d.val)
                        waited[d.sem] = d.val
                        stats["waits"] += 1

                    for o in ops:
                        for d in o.deps:
                            wait_for(d)
                        if o.is_dma:
                            out, in_ = o.fn
                            ins = engine.dma_start(out=out, in_=in_)
                        elif o.fn is None:
                            continue
                        else:
                            ins = o.fn(engine)
                        if o.signal:
                            ins.then_inc(semh[o.sem], 16 if o.is_dma else 1)
                        stats["ops"] += 1
                    for d in finals:
                        wait_for(d)

                engmap[e](body)
        self.stats = stats


class Region:
    def __init__(self, big, base, size):
        self.big, self.base, self.size, self.off = big, base, size, 0

    def reset(self):
        self.off = 0

    def f32(self, n):
        assert self.off + n <= self.size, ("arena overflow", self.off, n, self.size)
        v = self.big[:, self.base + self.off: self.base + self.off + n]
        self.off += n
        return v

    def bf16(self, n):
        assert n % 2 == 0
        return self.f32(n // 2).bitcast(BF16)


KB = 256


class Builder:
    def __init__(self, seq=SEQ, phases=("A", "B", "C"), depth=DEPTH, debug=()):
        self.seq = seq
        self.depth = depth
        self.phases = phases
        self.debug = debug

    def build(self):
        nc = bass.Bass("TRN2", target_bir_lowering=False)
        self.nc = nc
        S = self.seq
        dr = {}

        def din(name, shape):
            dr[name] = nc.dram_tensor(name, list(shape), F32, kind="ExternalInput").ap()

        din("x", [S, D])
        din("mem", [MEM, D])
        din("attn_sinks", [DEPTH, 8])
        din("a_log", [DEPTH, 4])
        din("dt_bias", [DEPTH, 4])
        din("gdn_norm_g", [DEPTH, 128])
        din("wq_mem", [DEPTH, D, D])
        din("wk_mem", [DEPTH, D, D])
        din("wv_mem", [DEPTH, D, D])
        din("wo_mem", [DEPTH, D, D])
        din("w_ff1", [DEPTH, D, DFF])
        din("w_ff2", [DEPTH, DFF, D])
        din("ln_g", [DEPTH, 3, D])
        din("ln_b", [DEPTH, 3, D])
        din("c_ident", [128, 128])
        din("c_emask", [128, 2, 2, 512])
        din("c_gdn", [128, 6, 128])
        din("w_in_p", [DEPTH, D, DIN])
        din("w_out_p", [DEPTH, D, D])
        din("conv_wT", [DEPTH, 128, 12, 4])
        dr["out"] = nc.dram_tensor("out", [S, D], F32, kind="ExternalOutput").ap()
        for nm in ("xs0", "xs1", "ypart"):
            dr[nm] = nc.dram_tensor(nm, [S, D], F32).ap()
        for nm, shp in self.debug:
            dr[nm] = nc.dram_tensor(nm, list(shp), F32, kind="ExternalOutput").ap()
        self.dr = dr

        with contextlib.ExitStack() as st:
            self.st = st
            NF = 207 * KB
            big = st.enter_context(nc.sbuf_tensor("big", [128, NF], F32))
            self.CONST = Region(big, 0, 21 * KB)
            self.WA = Region(big, 21 * KB, 64 * KB)
            self.WB = Region(big, 85 * KB, 64 * KB)
            self.WORK = Region(big, 149 * KB, NF - 149 * KB)
            self.psum = [st.enter_context(nc.psum_tensor("pb%d" % i, [128, 512], F32)) for i in range(8)]
            P = Prog(nc)
            self.P = P
            self.finals = []
            self.setup_consts()
            self.program()
            P.emit(final_wait_ops=self.finals)
            self.stats = dict(P.stats, sems=P.n_sems)
        return nc

    def setup_consts(self):
        P, dr = self.P, self.dr
        C = self.CONST
        idf = C.f32(128)
        self.ident = C.bf16(128)
        P.dma("sp", idf, dr["c_ident"], writes=["c_idf"])
        P.op("dve", lambda e: e.tensor_copy(out=self.ident, in_=idf), reads=["c_idf"], writes=["c_ident"])
        self.gb = [C.f32(1024), C.f32(1024)]
        self.ones = C.bf16(128)
        P.op("pool", lambda e: e.memset(self.ones, 1.0), writes=["c_ones"])
        self.identF = idf
        self.emask = C.f32(2048).rearrange("p (a b q) -> p a b q", a=2, b=2)
        P.dma("sp", self.emask, dr["c_emask"], writes=["c_emask"])
        self.gdnc = C.f32(768).rearrange("p (a q) -> p a q", a=6)
        P.dma("sp", self.gdnc, dr["c_gdn"], writes=["c_gdn"])
        self.rot_ex = 0
        self.tap_mix = None

    def tap(self, name, ap, key):
        if name not in [d[0] for d in self.debug]:
            return
        d = self.P.dma("pool", self.dr[name], ap, reads=list(key) if isinstance(key, list) else [key], writes=[("tap", name)])
        self.finals.append(d)

    def mm(self, out, lhsT, rhs, start, stop, reads, writes):
        return self.P.op("pe", lambda e: e.matmul(out, lhsT=lhsT, rhs=rhs, start=start, stop=stop), reads=reads, writes=writes)

    def tr(self, out, in_, reads, writes):
        return self.P.op("pe", lambda e: e.transpose(out=out, in_=in_, identity=self.ident), reads=list(reads) + ["c_ident"], writes=writes)

    def act(self, out, in_, func, reads, writes, bias=None, scale=None, accum_out=None):
        kw = {}
        if bias is not None:
            kw["bias"] = bias
        if scale is not None:
            kw["scale"] = scale
        if accum_out is not None:
            kw["accum_out"] = accum_out
        return self.P.op("act", lambda e: e.activation(out=out, in_=in_, func=func, **kw), reads=reads, writes=writes)

    def cp(self, eng, out, in_, reads, writes):
        if eng == "act":
            return self.P.op("act", lambda e: e.copy(out=out, in_=in_), reads=reads, writes=writes)
        return self.P.op(eng, lambda e: e.tensor_copy(out=out, in_=in_), reads=reads, writes=writes)

    def tt(self, eng, out, in0, in1, op, reads, writes):
        return self.P.op(eng, lambda e: e.tensor_tensor(out=out, in0=in0, in1=in1, op=op), reads=reads, writes=writes)

    def ts(self, eng, out, in0, s1, s2, op0, op1, reads, writes, accum_out=None):
        if op1 is None:
            return self.P.op(eng, lambda e: e.tensor_scalar(out=out, in0=in0, scalar1=s1, scalar2=None, op0=op0), reads=reads, writes=writes)
        if accum_out is not None:
            return self.P.op(eng, lambda e: e.tensor_scalar(out=out, in0=in0, scalar1=s1, scalar2=s2, op0=op0, op1=op1, accum_out=accum_out), reads=reads, writes=writes)
        return self.P.op(eng, lambda e: e.tensor_scalar(out=out, in0=in0, scalar1=s1, scalar2=s2, op0=op0, op1=op1), reads=reads, writes=writes)

    def stt(self, out, in0, scalar, in1, op0, op1, reads, writes):
        return self.P.op("dve", lambda e: e.scalar_tensor_tensor(out=out, in0=in0, scalar=scalar, in1=in1, op0=op0, op1=op1), reads=reads, writes=writes)

    def load_ln(self, l, i):
        P, dr = self.P, self.dr
        P.dma("sp", self.gb[0], dr["ln_g"][l, i:i + 1, :].broadcast_to([128, D]), writes=["ln_g"])
        P.dma("sp", self.gb[1], dr["ln_b"][l, i:i + 1, :].broadcast_to([128, D]), writes=["ln_b"])

    def load_weight(self, dst, src, key, rows_per_part_chunk=128, prefetch=True):
        P = self.P
        kc = dst.shape[1]
        h = max(1, kc // 2)
        srcv = src.rearrange("(k p) n -> p k n", p=128)
        for j, (a, b) in enumerate(((0, h), (h, kc))):
            if a == b:
                continue
            P.dma("pool", dst[:, a:b, :], srcv[:, a:b, :], writes=[(key, j)], prefetch=prefetch)
        return [(key, 0), (key, 1)] if kc > 1 else [(key, 0)]

    def x_to_xT(self, xin, xin_key, xb, xT, nsub, tag):
        self.cp("act", xb, xin, [xin_key], [tag + "xb"])
        for s in range(nsub):
            pb = self.psum[self.rot_pt % 2]
            pkey = "psum%d" % (self.rot_pt % 2)
            self.rot_pt += 1
            pT = pb[:].bitcast(BF16).rearrange("p (k t) -> p k t", k=8)
            for k in range(8):
                self.tr(pT[:, k, :], xb[:, s, k * 128:(k + 1) * 128], [tag + "xb"], [pkey])
            self.cp("dve", xT[:, :, s * 128:(s + 1) * 128], pT, [pkey], [tag + "xT"])

    def ln_tail(self, z, zkey, dst_rows):
        P = self.P
        r = self.ln_rot % 2
        self.ln_rot += 1
        sm = self.ln_small[r]
        st6 = sm[:, 0:12]
        mv = sm[:, 12:14]
        rstd = sm[:, 14:15]
        nmr = sm[:, 15:16]
        lnv = sm[:, 16:17]
        sk = "ln_small%d" % r
        P.op("dve", lambda e: e.bn_stats(out=st6[:, 0:6], in_=z[:, 0:512]), reads=[zkey], writes=[sk + "a"])
        P.op("dve", lambda e: e.bn_stats(out=st6[:, 6:12], in_=z[:, 512:1024]), reads=[zkey], writes=[sk + "b"])
        P.op("dve", lambda e: e.bn_aggr(out=mv, in_=st6.rearrange("p (a b) -> p a b", a=2)), reads=[sk + "a", sk + "b"], writes=[sk + "mv"])
        self.ts("dve", lnv, mv[:, 1:2], LN_EPS, None, ALU.add, None, [sk + "mv"], [sk + "lnv"])
        self.act(lnv, lnv, AF.Ln, [sk + "lnv"], [sk + "lnv"])
        self.act(rstd, lnv, AF.Exp, [sk + "lnv"], [sk + "rstd"], scale=-0.5)
        self.stt(nmr, mv[:, 0:1], -1.0, rstd, ALU.mult, ALU.mult, [sk + "mv", sk + "rstd"], [sk + "nmr"])
        self.act(z, z, AF.Identity, [zkey, sk + "rstd", sk + "nmr"], [zkey], bias=nmr, scale=rstd)
        self.tt("dve", z, z, self.gb[0], ALU.mult, [zkey, "ln_g"], [zkey])
        self.tt("dve", z, z, self.gb[1], ALU.add, [zkey, "ln_b"], [zkey])
        d = P.dma("sp", dst_rows, z, reads=[zkey], writes=[("st", zkey)])
        return d

    def program(self):
        P, dr = self.P, self.dr
        self.rot_pt = 0
        self.ln_rot = 0
        self.rot_ps = 0
        cur = dr["x"]
        bufs = [dr["xs0"], dr["xs1"]]
        bi = 0
        nl = self.depth
        order = [(l, ph) for l in range(nl) for ph in ("A", "B", "C1", "C2") if ph[0] in self.phases]
        wts = {}

        def load(i):
            if i < len(order) and i not in wts:
                l, ph = order[i]
                wts[i] = getattr(self, "weights_" + ph)(l)

        load(0)
        for i, (l, ph) in enumerate(order):
            last = i == len(order) - 1
            if ph == "A":
                dst = dr["out"] if last else bufs[bi]
                self.compute_A(l, wts[i], cur, dst, last)
                P.fence()
                load(i + 1)
                load(i + 2)
            else:
                if ph == "C1":
                    dst = None
                else:
                    dst = dr["out"] if last else bufs[bi]
                getattr(self, "compute_" + ph)(l, wts[i], cur, dst, last)
                P.fence()
                load(i + 1)
                if i + 1 < len(order) and order[i + 1][1] != "A":
                    load(i + 2)
            if dst is not None:
                cur = dst
                bi ^= 1

    def weights_C1(self, l):
        return self._weights_C(l, 0, self.WA, "WA")

    def weights_C2(self, l):
        return self._weights_C(l, 1, self.WB, "WB")

    def _weights_C(self, l, half, slot, slotname):
        dr = self.dr
        slot.reset()
        w1 = slot.bf16(8 * 2048).rearrange("p (k n) -> p k n", k=8)
        w2 = slot.bf16(16 * 1024).rearrange("p (k n) -> p k n", k=16)
        k1 = self.load_weight(w1, dr["w_ff1"][l][:, half * 2048:(half + 1) * 2048], slotname + "a")
        k2 = self.load_weight(w2, dr["w_ff2"][l][half * 2048:(half + 1) * 2048, :], slotname + "b")
        return (w1, w2, k1, k2)

    def compute_C1(self, l, wts, src, dst, last):
        self._compute_C(l, wts, src, dst, last, 0)

    def compute_C2(self, l, wts, src, dst, last):
        self._compute_C(l, wts, src, dst, last, 1)

    def _compute_C(self, l, wts, src, dst, last, half):
        P, dr = self.P, self.dr
        w1, w2, k1, k2 = wts
        S = self.seq
        TM = 256
        NS = TM // 128
        nmt = S // TM
        W = self.WORK
        W.reset()
        if half == 1:
            self.load_ln(l, 2)
        xin = [W.f32(NS * 1024).rearrange("p (s d) -> p s d", s=NS) for _ in range(2)]
        xb = W.bf16(NS * 1024).rearrange("p (s d) -> p s d", s=NS)
        xTs = [W.bf16(8 * TM).rearrange("p (k t) -> p k t", k=8) for _ in range(2)]
        hTs = [W.bf16(16 * TM).rearrange("p (f t) -> p f t", f=16) for _ in range(2)]
        rtmp = [W.f32(TM) for _ in range(2)]
        zb = [W.f32(1024) for _ in range(2)]
        self.ln_small = [W.f32(32) for _ in range(2)]
        srcv = src.rearrange("(m s p) d -> m p s d", s=NS, p=128)
        deferred = []

        def prep(m):
            self.x_to_xT(xin[m % 2], "xin%d" % (m % 2), xb, xTs[m % 2], NS, "C%d" % (m % 2))

        P.dma("sp", xin[0], srcv[0], writes=["xin0"])
        prep(0)
        for m in range(nmt):
            xi = xin[m % 2]
            xk = "xin%d" % (m % 2)
            xT = xTs[m % 2]
            xtag = "C%d" % (m % 2)
            hT = hTs[m % 2]
            hk = "hT%d" % (m % 2)
            if m + 1 < nmt:
                P.dma("sp", xin[(m + 1) % 2], srcv[m + 1], writes=["xin%d" % ((m + 1) % 2)])
            for f in range(16):
                pb = self.psum[2 + f % 2]
                pk = "psum%d" % (2 + f % 2)
                for k in range(8):
                    self.mm(pb[:, 0:TM], w1[:, k, f * 128:(f + 1) * 128], xT[:, k, :], k == 0, k == 7, [xtag + "xT"] + k1, [pk])
                rt = rtmp[f % 2]
                rk = "rtmp%d" % (f % 2)
                self.act(rt, pb[:, 0:TM], AF.Relu, [pk], [rk])
                self.tt("dve", hT[:, f, :], rt, rt, ALU.mult, [rk], [hk])
            if m + 1 < nmt:
                prep(m + 1)
            for fn in deferred:
                fn()
            deferred = []
            for s in range(NS):
                r = (m * NS + s) % 2
                pys = [self.psum[4 + 2 * r], self.psum[5 + 2 * r]]
                pyk = ["psum%d" % (4 + 2 * r), "psum%d" % (5 + 2 * r)]
                rows = slice(m * TM + s * 128, m * TM + (s + 1) * 128)
                z = zb[r]
                zk = "zb%d" % r
                if half == 1:
                    P.dma("sp", z, dr["ypart"][rows, :], writes=[zk])
                for n in range(2):
                    for f in range(16):
                        self.mm(pys[n][:], hT[:, f, s * 128:(s + 1) * 128], w2[:, f, n * 512:(n + 1) * 512],
                                f == 0, f == 15, [hk] + k2, [pyk[n]])
                if half == 0:
                    for n in range(2):
                        self.stt(z[:, n * 512:(n + 1) * 512], xi[:, s, n * 512:(n + 1) * 512], ALPHA, pys[n][:], ALU.mult, ALU.add, [xk, pyk[n]], [zk])
                    P.dma("sp", dr["ypart"][rows, :], z, reads=[zk], writes=[("st", zk)])
                else:
                    for n in range(2):
                        self.tt("dve", z[:, n * 512:(n + 1) * 512], z[:, n * 512:(n + 1) * 512], pys[n][:], ALU.add, [pyk[n], zk], [zk])

                    def tail(z=z, zk=zk, rows=rows):
                        d = self.ln_tail(z, zk, dst[rows, :])
                        if last:
                            self.finals.append(d)
                    deferred.append(tail)
        for fn in deferred:
            fn()

    def weights_B(self, l):
        dr = self.dr
        slot = self.WB
        slot.reset()
        ws = []
        ks = []
        for nm in ("wq_mem", "wk_mem", "wv_mem", "wo_mem"):
            w = slot.bf16(8 * 1024).rearrange("p (k n) -> p k n", k=8)
            ks.append(self.load_weight(w, dr[nm][l], "WB" + nm[1]))
            ws.append(w)
        return ws, ks

    def bank(self):
        i = 2 + self.rot_ps % 4
        self.rot_ps += 1
        return self.psum[i], "psum%d" % i

    def compute_B(self, l, wts, src, dst, last):
        P, dr = self.P, self.dr
        (wq, wk, wv, wo), (kq, kk, kv, ko) = wts
        S = self.seq
        TM = 256
        NS = 2
        nmt = S // TM
        W = self.WORK
        W.reset()
        self.load_ln(l, 1)
        xb = W.bf16(NS * 1024).rearrange("p (s d) -> p s d", s=NS)
        xT = W.bf16(8 * TM).rearrange("p (k t) -> p k t", k=8)
        kTm = W.bf16(8 * MEM).rearrange("p (c m) -> p c m", c=8)
        vm = W.bf16(2 * 1024).rearrange("p (c n) -> p c n", c=2)
        qTs = [W.bf16(8 * TM).rearrange("p (c t) -> p c t", c=8) for _ in range(2)]
        pT = [W.bf16(2 * TM).rearrange("p (c t) -> p c t", c=2) for _ in range(4)]
        rden = [W.f32(TM) for _ in range(4)]
        oTn = W.bf16(8 * TM).rearrange("p (c t) -> p c t", c=8)
        zb = [W.f32(1024) for _ in range(4)]
        self.ln_small = [W.f32(32) for _ in range(2)]
        ones = self.ones
        deferred = []

        def transposes(tag):
            for s in range(NS):
                pb = self.psum[self.rot_pt % 2]
                pkey = "psum%d" % (self.rot_pt % 2)
                self.rot_pt += 1
                pTt = pb[:].bitcast(BF16).rearrange("p (k t) -> p k t", k=8)
                for k in range(8):
                    self.tr(pTt[:, k, :], xb[:, s, k * 128:(k + 1) * 128], ["Bxb"], [pkey])
                self.cp("dve", xT[:, :, s * 128:(s + 1) * 128], pTt, [pkey], ["BxT"])

        P.dma("pool", xb, dr["mem"].rearrange("(s p) d -> p s d", p=128), writes=["Bxb"])
        transposes("B")
        for c in range(8):
            pb, pk = self.bank()
            for k in range(8):
                self.mm(pb[:, 0:MEM], wk[:, k, c * 128:(c + 1) * 128], xT[:, k, :], k == 0, k == 7, ["BxT"] + kk, [pk])
            self.cp("act" if c % 2 else "dve", kTm[:, c, :], pb[:, 0:MEM], [pk], ["kTm"])
        for mc in range(2):
            for n in range(2):
                pb, pk = self.bank()
                for k in range(8):
                    self.mm(pb[:], xT[:, k, mc * 128:(mc + 1) * 128], wv[:, k, n * 512:(n + 1) * 512], k == 0, k == 7, ["BxT"] + kv, [pk])
                self.cp("act" if n % 2 else "dve", vm[:, mc, n * 512:(n + 1) * 512], pb[:], [pk], ["vm"])
        srcv = src.rearrange("(m s p) d -> m p s d", s=NS, p=128)

        def X_task(m):
            qT = qTs[m % 2]
            qk = "qT%d" % (m % 2)
            if m == 0:
                P.dma("pool", xb, srcv[0], writes=["Bxb"])
            transposes("B")
            if m + 1 < nmt:
                P.dma("pool", xb, srcv[m + 1], writes=["Bxb"])
            yield
            for c in range(8):
                pb, pk = self.bank()
                for k in range(8):
                    self.mm(pb[:, 0:TM], wq[:, k, c * 128:(c + 1) * 128], xT[:, k, :], k == 0, k == 7, ["BxT"] + kq, [pk])
                self.act(qT[:, c, :], pb[:, 0:TM], AF.Copy, [pk], [qk], scale=1.0 / 16.0)
                yield

        def head_task(m, h):
            qT = qTs[m % 2]
            qk = "qT%d" % (m % 2)
            pb, pk = self.bank()
            sT = pb[:].rearrange("p (c t) -> p c t", c=2)
            for mc in range(2):
                for dc in range(2):
                    self.mm(sT[:, mc, :], kTm[:, 2 * h + dc, mc * 128:(mc + 1) * 128], qT[:, 2 * h + dc, :], dc == 0, dc == 1, ["kTm", qk], [pk])
            pt = pT[h]
            ptk = "pT%d" % h
            self.act(pt, sT, AF.Exp, [pk], [ptk])
            yield
            pbo, pko = self.bank()
            oT = pbo[:].rearrange("p (c t) -> p c t", c=2)
            for dc in range(2):
                for mc in range(2):
                    self.mm(oT[:, dc, :], vm[:, mc, h * 256 + dc * 128: h * 256 + (dc + 1) * 128], pt[:, mc, :], mc == 0, mc == 1, ["vm", ptk], [pko])
            pbd, pkd = self.bank()
            for mc in range(2):
                self.mm(pbd[:, 0:TM], ones, pt[:, mc, :], mc == 0, mc == 1, ["c_ones", ptk], [pkd])
            rd = rden[h]
            rdk = "rden%d" % h
            self.act(rd, pbd[:, 0:TM], AF.Ln, [pkd], [rdk])
            self.act(rd, rd, AF.Exp, [rdk], [rdk], scale=-1.0)
            for dc in range(2):
                self.tt("dve", oTn[:, 2 * h + dc, :], oT[:, dc, :], rd, ALU.mult, [pko, rdk], ["oTn%d" % h])
            yield

        def load_res(m):
            for s in range(NS):
                rows = slice(m * TM + s * 128, m * TM + (s + 1) * 128)
                r = (m * NS + s) % 4
                P.dma("sp", zb[r], src[rows, :], writes=["zb%d" % r])

        def post_task(m):
            for s in range(NS):
                rows = slice(m * TM + s * 128, m * TM + (s + 1) * 128)
                r = (m * NS + s) % 4
                z = zb[r]
                zk = "zb%d" % r
                for n in range(2):
                    pbn = self.psum[6 + n]
                    pkn = "psum%d" % (6 + n)
                    for c in range(8):
                        self.mm(pbn[:], oTn[:, c, s * 128:(s + 1) * 128], wo[:, c, n * 512:(n + 1) * 512], c == 0, c == 7,
                                ["oTn%d" % (c // 2)] + ko, [pkn])
                    self.stt(z[:, n * 512:(n + 1) * 512], z[:, n * 512:(n + 1) * 512], ALPHA, pbn[:], ALU.mult, ALU.add, [zk, pkn], [zk])
                    yield

                def tail(z=z, zk=zk, rows=rows):
                    d = self.ln_tail(z, zk, dst[rows, :])
                    if last:
                        self.finals.append(d)
                deferred.append(tail)

        def run(groups):
            bg = groups.pop(0)
            for grp in groups:
                alive = list(grp)
                while alive:
                    for t in list(alive):
                        try:
                            next(t)
                        except StopIteration:
                            alive.remove(t)
                    if bg is not None:
                        try:
                            next(bg)
                        except StopIteration:
                            bg = None
            while bg is not None:
                try:
                    next(bg)
                except StopIteration:
                    bg = None

        run([X_task(0), []])
        for m in range(nmt):
            bg = X_task(m + 1) if m + 1 < nmt else None
            load_res(m)
            fl = list(deferred)
            del deferred[:]

            def flush(fl=fl):
                for fn in fl:
                    fn()
                    yield

            run([bg, [head_task(m, h) for h in range(4)] + [flush()], [post_task(m)]])
        for fn in deferred:
            fn()

    def weights_A(self, l):
        dr = self.dr
        slot = self.WA
        slot.reset()
        win = dr["w_in_p"][l]
        spec = (("q", 0, 512), ("k", 512, 128), ("v", 640, 128), ("g", 768, 1536), ("ab", 2304, 8), ("z", 2312, 512))
        w = {}
        ks = {}
        for nm, c0, n in spec:
            w[nm] = slot.bf16(8 * n).rearrange("p (k n) -> p k n", k=8)
            ks[nm] = self.load_weight(w[nm], win[:, c0:c0 + n], "WA" + nm)
        w["o"] = slot.bf16(8 * 1024).rearrange("p (k n) -> p k n", k=8)
        ks["o"] = self.load_weight(w["o"], dr["w_out_p"][l], "WAo")
        return w, ks

    def compute_A(self, l, wts, src, dst, last):
        P, dr = self.P, self.dr
        w, ks = wts
        S = self.seq
        TM = 256
        NS = 2
        nmt = S // TM
        R = Region(self.WB.big, self.WB.base, self.WB.size + self.WORK.size)
        ones, identF = self.ones, self.identF
        Ublk, Lblk, Csel0, Csel1, NEGc, SM = [self.gdnc[:, i, :] for i in range(6)]
        self.rotA = 0

        def bank():
            i = self.rotA % 8
            self.rotA += 1
            return self.psum[i], "psum%d" % i

        def h4(ap):
            return ap.rearrange("p (h d) -> p h d", h=4)

        xb = R.bf16(NS * 1024).rearrange("p (s d) -> p s d", s=NS)
        xT = R.bf16(8 * TM).rearrange("p (k t) -> p k t", k=8)
        aqT = [R.bf16(4 * TM).rearrange("p (c t) -> p c t", c=4) for _ in range(2)]
        akT = [R.bf16(TM) for _ in range(2)]
        vtok = [R.bf16(TM).rearrange("p (b d) -> p b d", b=2) for _ in range(2)]
        qTn = [R.bf16(4 * TM).rearrange("p (h t) -> p h t", h=4) for _ in range(2)]
        kTn = [R.bf16(4 * TM).rearrange("p (h t) -> p h t", h=4) for _ in range(2)]
        vT = [R.bf16(4 * TM).rearrange("p (h t) -> p h t", h=4) for _ in range(2)]
        ab = [R.f32(16).rearrange("p (s c) -> p s c", s=2) for _ in range(2)]
        zg = [[R.bf16(512) for _ in range(2)] for _ in range(2)]
        NG = 4
        gbuf = [R.f32(260) for _ in range(NG)]
        cacc = [R.f32(TM) for _ in range(NG)]
        chalo = R.f32(48).rearrange("p (c j) -> p c j", c=12)
        sq = [R.bf16(TM) for _ in range(NG)]
        zgf = R.f32(512)
        exb = [R.f32(512) for _ in range(2)]
        ptb = [R.bf16(512) for _ in range(4)]
        dtot = [R.f32(512) for _ in range(2)]
        mixT = [R.bf16(8 * 128).rearrange("p (c t) -> p c t", c=8) for _ in range(2)]
        oall = [h4(R.f32(512)) for _ in range(2)]
        og = h4(R.bf16(512))
        Sst = h4(R.f32(512))
        Sb = h4(R.bf16(512))
        zb = [R.f32(1024) for _ in range(2)]
        self.ln_small = [R.f32(32) for _ in range(2)]
        convw = R.f32(48).rearrange("p (c j) -> p c j", c=12)
        esk = R.f32(4)
        esink_b = R.f32(512)
        nA = R.f32(4)
        dtb = R.f32(4)
        ng4 = R.f32(512)
        smalls = [[R.f32(64) for _ in range(2)] for _ in range(2)]
        Hs = []
        for s_ in range(2):
            d = {}
            for nm in ("fA", "Ec", "Nf", "Pf"):
                d[nm] = h4(R.f32(512))
            d["Nb"] = [h4(R.bf16(512)) for _ in range(2)]
            d["NTb"] = [h4(R.bf16(512)) for _ in range(2)]
            for nm in ("Pb", "intraT", "kd", "r", "vnew", "vtf"):
                d[nm] = h4(R.bf16(512))
            Hs.append(d)
        self.arenaA = R.off

        P.dma("sp", convw, dr["conv_wT"][l], writes=["convw"])
        P.dma("sp", esk[0:64, :], dr["attn_sinks"][l:l + 1, 0:4].broadcast_to([64, 4]), writes=["esk0"])
        P.dma("sp", esk[64:128, :], dr["attn_sinks"][l:l + 1, 4:8].broadcast_to([64, 4]), writes=["esk1"])
        P.dma("sp", nA, dr["a_log"][l:l + 1, :].broadcast_to([128, 4]), writes=["nA"])
        P.dma("sp", dtb, dr["dt_bias"][l:l + 1, :].broadcast_to([128, 4]), writes=["dtb"])
        ngcol = ng4[:, 0:1]
        P.dma("sp", ngcol, dr["gdn_norm_g"][l].rearrange("(p o) -> p o", o=1), writes=["ngcol"])
        self.load_ln(l, 0)
        for c in range(4):
            self.act(esink_b[:, c * 128:(c + 1) * 128], identF, AF.Exp, ["esk0", "esk1", "c_idf"], ["esink_b"], bias=esk[:, c:c + 1], scale=0.0)
        self.act(nA, nA, AF.Exp, ["nA"], ["nA"])
        self.ts("dve", nA, nA, -1.0, None, ALU.mult, None, ["nA"], ["nA"])
        P.op("pool", lambda e: e.memset(Sst, 0.0), writes=["S"])
        P.op("pool", lambda e: e.memset(Sb, 0.0), writes=["Sb"])
        P.op("pool", lambda e: e.memset(chalo, 0.0), writes=["chalo%d" % c for c in range(12)])

        emask = self.emask
        srcv = src.rearrange("(m s p) d -> m p s d", s=NS, p=128)
        kq, kk, kv, kg, kab, kz, ko = ks["q"], ks["k"], ks["v"], ks["g"], ks["ab"], ks["z"], ks["o"]
        wq, wk, wv, wg, wab, wz, wo = w["q"], w["k"], w["v"], w["g"], w["ab"], w["z"], w["o"]
        deferredA = []

        def bc_h(ap4, n=128, rows=slice(0, 128)):
            a = ap4[rows, :]
            return a.unsqueeze(2).to_broadcast([a.shape[0], 4, n])

        def bc_m(ap, rows=slice(0, 128)):
            a = ap[rows, :]
            return a.unsqueeze(1).to_broadcast([a.shape[0], 4, a.shape[1]])

        def small_task(p, s):
            pk_ = "p%d" % p
            sm = smalls[p][s]
            smk = "sm%d" % s + pk_
            abk = "ab%d" % s + pk_
            x4, ax, e4, l4, sp4, g4, eb4, beta = [sm[:, 4 * i:4 * i + 4] for i in range(8)]
            edec = sm[:, 32:48]
            necum = sm[:, 48:52]
            self.tt("dve", x4, ab[p][:, s, 0:4], dtb, ALU.add, [abk, "dtb"], [smk + "x"])
            self.stt(ax, x4, -1.0, x4, ALU.mult, ALU.max, [smk + "x"], [smk + "ax"])
            self.act(e4, ax, AF.Exp, [smk + "ax"], [smk + "e"], scale=-1.0)
            self.act(l4, e4, AF.Ln, [smk + "e"], [smk + "l"], bias=1.0)
            self.stt(sp4, x4, 0.0, l4, ALU.max, ALU.add, [smk + "x", smk + "l"], [smk + "sp"])
            self.tt("dve", g4, sp4, nA, ALU.mult, [smk + "sp", "nA"], [smk + "g"])
            self.act(eb4, ab[p][:, s, 4:8], AF.Exp, [abk], [smk + "eb"], scale=-1.0)
            self.ts("dve", eb4, eb4, 1.0, None, ALU.add, None, [smk + "eb"], [smk + "eb"])
            P.op("dve", lambda e, beta=beta, eb4=eb4: e.reciprocal(out=beta, in_=eb4), reads=[smk + "eb"], writes=[smk + "beta"])
            pb, pk = bank()
            for i, msk in enumerate((Ublk, Lblk, Csel0, Csel1)):
                self.mm(pb[:, 4 * i:4 * i + 4], msk, g4, True, True, ["c_gdn", smk + "g"], [pk])
            self.act(edec, pb[:, 0:16], AF.Exp, [pk], [smk + "edec"])
            self.ts("dve", necum, edec[:, 0:4], -1.0, None, ALU.mult, None, [smk + "edec"], [smk + "necum"])

        def XT_task(m):
            if m == 0:
                P.dma("pool", xb, srcv[0], writes=["Axb"])
            for s in range(NS):
                pbk, pkk = bank()
                pT = pbk[:].bitcast(BF16).rearrange("p (k t) -> p k t", k=8)
                for k in range(8):
                    self.tr(pT[:, k, :], xb[:, s, k * 128:(k + 1) * 128], ["Axb"], [pkk])
                self.cp("act", xT[:, :, s * 128:(s + 1) * 128], pT, [pkk], ["AxT"])
                yield
            if m + 1 < nmt:
                P.dma("pool", xb, srcv[m + 1], writes=["Axb"])

        def XP_task(m):
            p = m % 2
            pk_ = "p%d" % p
            for c in range(4):
                pb, pk = bank()
                for k in range(8):
                    self.mm(pb[:, 0:TM], wq[:, k, c * 128:(c + 1) * 128], xT[:, k, :], k == 0, k == 7, ["AxT"] + kq, [pk])
                self.act(aqT[p][:, c, :], pb[:, 0:TM], AF.Copy, [pk], ["aqT" + pk_], scale=0.125)
                yield
            pb, pk = bank()
            for k in range(8):
                self.mm(pb[:, 0:TM], wk[:, k, :], xT[:, k, :], k == 0, k == 7, ["AxT"] + kk, [pk])
            self.cp("act", akT[p], pb[:, 0:TM], [pk], ["akT" + pk_])
            yield
            for s in range(NS):
                pb, pk = bank()
                for k in range(8):
                    self.mm(pb[:, 0:128], xT[:, k, s * 128:(s + 1) * 128], wv[:, k, :], k == 0, k == 7, ["AxT"] + kv, [pk])
                self.cp("act", vtok[p][:, s, :], pb[:, 0:128], [pk], ["vt" + pk_])
                yield
            for s in range(NS):
                pb, pk = bank()
                for k in range(8):
                    self.mm(pb[:, 0:8], xT[:, k, s * 128:(s + 1) * 128], wab[:, k, :], k == 0, k == 7, ["AxT"] + kab, [pk])
                self.cp("dve", ab[p][:, s, :], pb[:, 0:8], [pk], ["ab%d" % s + pk_])
                yield
                pb, pk = bank()
                for k in range(8):
                    self.mm(pb[:], xT[:, k, s * 128:(s + 1) * 128], wz[:, k, :], k == 0, k == 7, ["AxT"] + kz, [pk])
                self.act(zg[p][s], pb[:], AF.Silu, [pk], ["zg%d" % s + pk_])
                yield
                small_task(p, s)
                yield

        def XC_task(m):
            p = m % 2
            pk_ = "p%d" % p

            def conv_task(ch, slot):
                gb_ = gbuf[slot]
                gk = "gbuf%d" % slot
                hkk = "chalo%d" % ch
                acc = cacc[slot]
                ak = "cacc%d" % slot
                pb, pk = bank()
                for k in range(8):
                    self.mm(pb[:, 0:TM], wg[:, k, ch * 128:(ch + 1) * 128], xT[:, k, :], k == 0, k == 7, ["AxT"] + kg, [pk])
                self.cp("pool", gb_[:, 0:3], chalo[:, ch, 0:3], [hkk], [gk + "h"])
                self.cp("act", gb_[:, 3:259], pb[:, 0:TM], [pk], [gk])
                self.cp("pool", chalo[:, ch, 0:3], gb_[:, 256:259], [gk], [hkk])
                yield
                self.ts("dve", acc, gb_[:, 3:259], convw[:, ch, 3:4], None, ALU.mult, None, [gk, "convw"], [ak])
                self.stt(acc, gb_[:, 2:2 + TM], convw[:, ch, 2:3], acc, ALU.mult, ALU.add, [gk, gk + "h", "convw", ak], [ak])
                yield
                for j in (1, 0):
                    self.stt(acc, gb_[:, j:j + TM], convw[:, ch, j:j + 1], acc, ALU.mult, ALU.add, [gk, gk + "h", "convw", ak], [ak])
                yield
                if ch < 8:
                    self.act(acc, acc, AF.Silu, [ak], [ak])
                    sqb = sq[slot]
                    sqk = "sq%d" % slot
                    self.act(sqb, acc, AF.Square, [ak], [sqk])
                    yield
                    pb2, pk2 = bank()
                    self.mm(pb2[:, 0:TM], ones, sqb, True, True, ["c_ones", sqk], [pk2])
                    lr = gb_[:, 0:TM]
                    gkk = [gk, gk + "h"]
                    self.ts("dve", lr, pb2[:, 0:TM], RMS_EPS, None, ALU.add, None, [pk2], gkk)
                    yield
                    self.act(lr, lr, AF.Ln, gkk, gkk)
                    self.act(lr, lr, AF.Exp, gkk, gkk, scale=-0.5)
                    yield
                    if ch < 4:
                        self.stt(qTn[p][:, ch, :], acc, float(128 ** -0.5), lr, ALU.mult, ALU.mult, [ak] + gkk, ["qTn" + pk_])
                    else:
                        self.tt("dve", kTn[p][:, ch - 4, :], acc, lr, ALU.mult, [ak] + gkk, ["kTn" + pk_])
                else:
                    self.act(vT[p][:, ch - 8, :], acc, AF.Silu, [ak], ["vT" + pk_])
                yield

            for grp in range(3):
                alive = [conv_task(grp * NG + i, i) for i in range(NG)]
                while alive:
                    for t in list(alive):
                        try:
                            next(t)
                        except StopIteration:
                            alive.remove(t)
                    yield

        def make_Y(m):
            p = m % 2
            pk_ = "p%d" % p
            q_ = "p%d" % (1 - p)

            def attn_task(s):
                blk = m * NS + s
                tok = slice(s * 128, (s + 1) * 128)
                mx = mixT[blk % 2]
                mka = "mixTa%d" % (blk % 2)
                dt = dtot[s]
                for g in range(2):
                    gp = slice(g * 64, (g + 1) * 64)
                    dk = "dtot%d_%d" % (s, g)
                    kbs = ([0] if blk > 0 else []) + [1]
                    pts = []
                    for kb in kbs:
                        pb, pk = bank()
                        if kb == 1:
                            kap, kkey = akT[p][gp, s * 128:(s + 1) * 128], "akT" + pk_
                        elif s == 1:
                            kap, kkey = akT[p][gp, 0:128], "akT" + pk_
                        else:
                            kap, kkey = akT[1 - p][gp, 128:256], "akT" + q_
                        self.mm(pb[:], kap, aqT[p][gp, :, tok], True, True, [kkey, "aqT" + pk_], [pk])
                        ei = self.rot_ex % 2
                        pi = self.rot_ex % 4
                        self.rot_ex += 1
                        self.act(exb[ei], pb[:], AF.Exp, [pk], ["exb%d" % ei])
                        self.tt("dve", ptb[pi], exb[ei], emask[:, kb, g, :], ALU.mult, ["exb%d" % ei, "c_emask"], ["ptb%d" % pi])
                        pts.append((kb, ptb[pi], "ptb%d" % pi))
                        yield
                    pbo, pko = bank()
                    pbd, pkd = bank()
                    for idx, (kb, pt, ptk) in enumerate(pts):
                        if kb == 1:
                            vap, vkey = vtok[p][:, s, gp], "vt" + pk_
                        elif s == 1:
                            vap, vkey = vtok[p][:, 0, gp], "vt" + pk_
                        else:
                            vap, vkey = vtok[1 - p][:, 1, gp], "vt" + q_
                        self.mm(pbo[gp, :], vap, pt, idx == 0, idx == len(pts) - 1, [vkey, ptk], [pko])
                    for idx, (kb, pt, ptk) in enumerate(pts):
                        self.mm(pbd[gp, :], ones[:, 0:64], pt, idx == 0, idx == len(pts) - 1, ["c_ones", ptk], [pkd])
                    self.tt("dve", dt[gp, :], pbd[gp, :], esink_b[gp, :], ALU.add, [pkd, "esink_b"], [dk])
                    self.act(dt[gp, :], dt[gp, :], AF.Ln, [dk], [dk])
                    self.act(dt[gp, :], dt[gp, :], AF.Exp, [dk], [dk], scale=-1.0)
                    self.tt("dve", mx[gp, 0:4, :], pbo[gp, :].rearrange("p (c q) -> p c q", c=4), dt[gp, :].rearrange("p (c q) -> p c q", c=4),
                            ALU.mult, [pko, dk], [mka + "_%d" % g])
                    yield

            def small_task_unused(s):
                sm = smalls[p][s]
                smk = "sm%d" % s + pk_
                abk = "ab%d" % s + pk_
                x4, ax, e4, l4, sp4, g4, eb4, beta = [sm[:, 4 * i:4 * i + 4] for i in range(8)]
                edec = sm[:, 32:48]
                necum = sm[:, 48:52]
                self.tt("dve", x4, ab[p][:, s, 0:4], dtb, ALU.add, [abk, "dtb"], [smk + "x"])
                self.stt(ax, x4, -1.0, x4, ALU.mult, ALU.max, [smk + "x"], [smk + "ax"])
                self.act(e4, ax, AF.Exp, [smk + "ax"], [smk + "e"], scale=-1.0)
                self.act(l4, e4, AF.Ln, [smk + "e"], [smk + "l"], bias=1.0)
                self.stt(sp4, x4, 0.0, l4, ALU.max, ALU.add, [smk + "x", smk + "l"], [smk + "sp"])
                self.tt("dve", g4, sp4, nA, ALU.mult, [smk + "sp", "nA"], [smk + "g"])
                self.act(eb4, ab[p][:, s, 4:8], AF.Exp, [abk], [smk + "eb"], scale=-1.0)
                self.ts("dve", eb4, eb4, 1.0, None, ALU.add, None, [smk + "eb"], [smk + "eb"])
                P.op("dve", lambda e, beta=beta, eb4=eb4: e.reciprocal(out=beta, in_=eb4), reads=[smk + "eb"], writes=[smk + "beta"])
                pb, pk = bank()
                for i, msk in enumerate((Ublk, Lblk, Csel0, Csel1)):
                    self.mm(pb[:, 4 * i:4 * i + 4], msk, g4, True, True, ["c_gdn", smk + "g"], [pk])
                self.act(edec, pb[:, 0:16], AF.Exp, [pk], [smk + "edec"])
                self.ts("dve", necum, edec[:, 0:4], -1.0, None, ALU.mult, None, [smk + "edec"], [smk + "necum"])

            def pre_task(s):
                d = Hs[s]
                sm = smalls[p][s]
                smk = "sm%d" % s + pk_
                g4 = sm[:, 20:24]
                beta = sm[:, 28:32]
                edec = sm[:, 32:48]
                tok = slice(s * 128, (s + 1) * 128)
                K_ = lambda nm: "H%d%s" % (s, nm)
                kT_keys = ["kTn" + pk_]
                qT_keys = ["qTn" + pk_]
                vT_keys = ["vT" + pk_]
                kT, qT, vTt = kTn[p], qTn[p], vT[p]
                for h in range(4):
                    self.act(d["fA"][:, h, :], Ublk, AF.Copy, ["c_gdn", smk + "g"], [K_("fA")], scale=g4[:, h:h + 1])
                pbt, pkt = bank()
                ptv = pbt[:].bitcast(BF16)
                for h in range(4):
                    self.tr(ptv[:, h * 128:(h + 1) * 128], kT[:, h, tok], kT_keys, [pkt])
                for h in range(4):
                    self.tr(ptv[:, 512 + h * 128:512 + (h + 1) * 128], vTt[:, h, tok], vT_keys, [pkt])
                self.tt("dve", d["kd"], h4(ptv[:, 0:512]), bc_h(edec[:, 4:8]), ALU.mult, [pkt, smk + "edec"], [K_("kd")])
                self.cp("act", d["vtf"], h4(ptv[:, 512:1024]), [pkt], [K_("vtf")])
                pbE, pkE = bank()
                for h in range(4):
                    self.mm(pbE[:, h * 128:(h + 1) * 128], Lblk, d["fA"][:, h, :], True, False, ["c_gdn", K_("fA")], [pkE])
                    self.mm(pbE[:, h * 128:(h + 1) * 128], identF, NEGc, False, True, ["c_gdn", "c_idf"], [pkE])
                self.act(d["Ec"], h4(pbE[:]), AF.Exp, [pkE], [K_("Ec")])
                self.tt("dve", d["fA"], d["Ec"], bc_m(SM), ALU.mult, [K_("Ec"), "c_gdn"], [K_("fA")])
                for h in range(4):
                    self.act(d["fA"][:, h, :], d["fA"][:, h, :], AF.Copy, [K_("fA"), smk + "beta"], [K_("fA")], scale=beta[:, h:h + 1])
                yield
                pbG, pkG = bank()
                for h in range(4):
                    self.mm(pbG[:, h * 128:(h + 1) * 128], kT[:, h, tok], kT[:, h, tok], True, True, kT_keys, [pkG])
                self.tt("dve", d["Nf"], h4(pbG[:]), d["fA"], ALU.mult, [pkG, K_("fA")], [K_("Nf")])
                self.cp("act", d["Nb"][0], d["Nf"], [K_("Nf")], [K_("Nb0")])
                self.stt(d["Pf"], d["Nf"], -1.0, bc_m(identF), ALU.mult, ALU.add, [K_("Nf"), "c_idf"], [K_("Pf")])
                self.cp("act", d["Pb"], d["Pf"], [K_("Pf")], [K_("Pb")])
                pbI, pkI = bank()
                for h in range(4):
                    self.mm(pbI[:, h * 128:(h + 1) * 128], kT[:, h, tok], qT[:, h, tok], True, True, kT_keys + qT_keys, [pkI])
                self.tt("dve", d["intraT"], h4(pbI[:]), d["Ec"], ALU.mult, [pkI, K_("Ec")], [K_("intraT")])
                yield
                pbt, pkt = bank()
                ptv = pbt[:].bitcast(BF16)
                for h in range(4):
                    self.tr(ptv[:, h * 128:(h + 1) * 128], d["Nb"][0][:, h, :], [K_("Nb0")], [pkt])
                self.cp("act", d["NTb"][0], h4(ptv[:, 0:512]), [pkt], [K_("NTb0")])
                yield

                def square(lev):
                    cur = (lev - 1) % 2
                    nxt = lev % 2
                    kN = [K_("NTb%d" % cur), K_("Nb%d" % cur)]
                    if lev < 5:
                        pbn, pkn = bank()
                        for h in range(4):
                            self.mm(pbn[:, h * 128:(h + 1) * 128], d["NTb"][cur][:, h, :], d["Nb"][cur][:, h, :], True, True, kN, [pkn])
                        self.cp("act", d["Nb"][nxt], h4(pbn[:]), [pkn], [K_("Nb%d" % nxt)])
                    pbn2, pkn2 = bank()
                    for h in range(4):
                        self.mm(pbn2[:, h * 128:(h + 1) * 128], d["Nb"][cur][:, h, :], d["NTb"][cur][:, h, :], True, True, kN, [pkn2])
                    self.cp("act", d["NTb"][nxt], h4(pbn2[:]), [pkn2], [K_("NTb%d" % nxt)])

                def pupd(lev):
                    nxt = lev % 2
                    pbp, pkp = bank()
                    for h in range(4):
                        self.mm(pbp[:, h * 128:(h + 1) * 128], d["NTb"][nxt][:, h, :], d["Pb"][:, h, :], True, True, [K_("NTb%d" % nxt), K_("Pb")], [pkp])
                    self.tt("dve", d["Pf"], d["Pf"], h4(pbp[:]), ALU.add, [pkp, K_("Pf")], [K_("Pf")])
                    self.cp("act", d["Pb"], d["Pf"], [K_("Pf")], [K_("Pb")])

                square(1)
                yield
                for lev in range(2, 6):
                    pupd(lev - 1)
                    square(lev)
                    yield
                pupd(5)
                yield

            def scan_task(s):
                d = Hs[s]
                sm = smalls[p][s]
                smk = "sm%d" % s + pk_
                beta = sm[:, 28:32]
                edec = sm[:, 32:48]
                necum = sm[:, 48:52]
                K_ = lambda nm: "H%d%s" % (s, nm)
                oa = oall[s]
                kT, qT = kTn[p], qTn[p]
                kT_keys = ["kTn" + pk_]
                qT_keys = ["qTn" + pk_]
                okey = "oall%d" % s
                for c in range(2):
                    Rr = slice(c * 64, (c + 1) * 64)
                    ctok = slice(s * 128 + c * 64, s * 128 + (c + 1) * 64)
                    v4 = lambda ap: h4(ap[Rr, :])
                    pb1, pk1 = bank()
                    for h in range(4):
                        self.mm(pb1[Rr, h * 128:(h + 1) * 128], kT[:, h, ctok], Sb[:, h, :], True, True, kT_keys + ["Sb"], [pk1])
                    for h in range(4):
                        self.stt(d["r"][Rr, h, :], pb1[Rr, h * 128:(h + 1) * 128], necum[Rr, h:h + 1], d["vtf"][Rr, h, :], ALU.mult, ALU.add,
                                 [pk1, smk + "necum", K_("vtf")], [K_("r")])
                    pb3, pk3 = bank()
                    for h in range(4):
                        self.mm(pb3[Rr, h * 128:(h + 1) * 128], qT[:, h, ctok], Sb[:, h, :], True, True, qT_keys + ["Sb"], [pk3])
                    self.tt("dve", d["Nf"][Rr], v4(pb3), bc_h(edec[:, 0:4], rows=Rr), ALU.mult, [pk3, smk + "edec"], [K_("Nf")])
                    yield
                    pb2, pk2 = bank()
                    for h in range(4):
                        self.mm(pb2[Rr, h * 128:(h + 1) * 128], d["Pb"][Rr, h, c * 64:(c + 1) * 64], d["r"][Rr, h, :], True, True, [K_("Pb"), K_("r")], [pk2])
                    self.tt("dve", d["vnew"][Rr], v4(pb2), bc_h(beta, rows=Rr), ALU.mult, [pk2, smk + "beta"], [K_("vnew")])
                    yield
                    pb5, pk5 = bank()
                    for h in range(4):
                        self.mm(pb5[:, h * 128:(h + 1) * 128], d["kd"][Rr, h, :], d["vnew"][Rr, h, :], True, True, [K_("kd"), K_("vnew")], [pk5])
                    for h in range(4):
                        self.stt(Sst[:, h, :], Sst[:, h, :], edec[:, 8 + 4 * c + h:9 + 4 * c + h], pb5[:, h * 128:(h + 1) * 128], ALU.mult, ALU.add,
                                 [pk5, "S", smk + "edec"], ["S"])
                    self.cp("act", Sb, Sst, ["S"], ["Sb"])
                    pb4, pk4 = bank()
                    for h in range(4):
                        self.mm(pb4[Rr, h * 128:(h + 1) * 128], d["intraT"][Rr, h, c * 64:(c + 1) * 64], d["vnew"][Rr, h, :], True, True, [K_("intraT"), K_("vnew")], [pk4])
                    self.tt("dve", oa[Rr], d["Nf"][Rr], v4(pb4), ALU.add, [pk4, K_("Nf")], [okey])
                    yield

            def post_task(s):
                blk = m * NS + s
                d = Hs[s]
                sm = smalls[p][s]
                smk = "sm%d" % s + pk_
                ss4 = sm[:, 52:56]
                rstd4 = sm[:, 56:60]
                oa = oall[s]
                okey = "oall%d" % s
                K_ = lambda nm: "H%d%s" % (s, nm)
                mx = mixT[blk % 2]
                mka = "mixTa%d" % (blk % 2)
                mkg = "mixTg%d" % (blk % 2)
                rows = slice(m * TM + s * 128, m * TM + (s + 1) * 128)
                r = blk % 2
                z = zb[r]
                zk = "zb%d" % r
                P.dma("sp", z, src[rows, :], writes=[zk])
                self.act(d["Nf"], oa, AF.Square, [okey], [K_("Nf")])
                P.op("dve", lambda e, ss4=ss4, src_=d["Nf"]: e.tensor_reduce(out=ss4, in_=src_, axis=mybir.AxisListType.X, op=ALU.add),
                     reads=[K_("Nf")], writes=[smk + "ss"])
                yield
                self.ts("dve", rstd4, ss4, 1.0 / 128.0, RMS_EPS, ALU.mult, ALU.add, [smk + "ss"], [smk + "rstd"])
                self.act(rstd4, rstd4, AF.Ln, [smk + "rstd"], [smk + "rstd"])
                self.act(rstd4, rstd4, AF.Exp, [smk + "rstd"], [smk + "rstd"], scale=-0.5)
                yield
                for h in range(4):
                    self.stt(og[:, h, :], oa[:, h, :], rstd4[:, h:h + 1], zg[p][s][:, h * 128:(h + 1) * 128], ALU.mult, ALU.mult,
                             [okey, smk + "rstd", "zg%d" % s + pk_], ["og"])
                yield
                pbt, pkt = bank()
                ptv = pbt[:].bitcast(BF16)
                for h in range(4):
                    self.tr(ptv[:, h * 128:(h + 1) * 128], og[:, h, :], ["og"], [pkt])
                self.act(mx[:, 4:8, :], ptv[:, 0:512].rearrange("p (c t) -> p c t", c=4), AF.Copy, [pkt, "ngcol"], [mkg], scale=ngcol)
                yield
                for n in range(2):
                    pbn, pkn = bank()
                    for c in range(8):
                        self.mm(pbn[:], mx[:, c, :], wo[:, c, n * 512:(n + 1) * 512], c == 0, c == 7, [mka + "_0", mka + "_1", mkg] + ko, [pkn])
                    self.stt(z[:, n * 512:(n + 1) * 512], z[:, n * 512:(n + 1) * 512], ALPHA, pbn[:], ALU.mult, ALU.add, [zk, pkn], [zk])
                    yield

                def tail(z=z, zk=zk, rows=rows):
                    dd = self.ln_tail(z, zk, dst[rows, :])
                    if last:
                        self.finals.append(dd)
                deferredA.append(tail)
                yield

            return {"attn0": (attn_task(0), []), "attn1": (attn_task(1), []),
                    "pre0": (pre_task(0), []), "pre1": (pre_task(1), ["pre0@2"]),
                    "scan0": (scan_task(0), ["pre0"]), "scan1": (scan_task(1), ["pre1", "scan0"]),
                    "post0": (post_task(0), ["scan0", "attn0"]), "post1": (post_task(1), ["scan1", "attn1", "post0"])}

        def run(tasks):
            done = set()
            steps = {n: 0 for n in tasks}
            active = []
            pending = dict(tasks)

            def ok(dep):
                if "@" in dep:
                    n, k = dep.split("@")
                    return n in done or steps.get(n, 0) >= int(k)
                return dep in done or dep not in tasks

            while pending or active:
                for n in list(pending):
                    if all(ok(dp) for dp in pending[n][1]):
                        active.append(n)
                        del pending[n]
                assert active, ("deadlock", list(pending))
                for n in list(active):
                    try:
                        next(tasks[n][0])
                        steps[n] += 1
                    except StopIteration:
                        active.remove(n)
                        done.add(n)
                if deferredA and "XP" in tasks and steps.get("XP", 0) >= 4:
                    for fn in deferredA:
                        fn()
                    del deferredA[:]

        run({"XT": (XT_task(0), []), "XP": (XP_task(0), ["XT"]), "XC": (XC_task(0), ["XT"])})
        for m in range(nmt):
            tasks = make_Y(m)
            if m + 1 < nmt:
                tasks["XT"] = (XT_task(m + 1), [])
                tasks["XP"] = (XP_task(m + 1), ["XT"])
                tasks["XC"] = (XC_task(m + 1), ["XT"])
            run(tasks)
        for fn in deferredA:
            fn()
        del deferredA[:]


def make_consts():
    c = {"c_ident": np.eye(128, dtype=np.float32)}
    j = np.arange(128)[:, None].astype(np.float64)
    i = np.arange(128)[None, :].astype(np.float64)
    em = np.zeros((128, 2, 2, 4, 128), np.float64)
    for g in range(2):
        for cc in range(4):
            h = 4 * g + cc
            slope = 2.0 ** (-8.0 * (h + 1) / 8.0)
            d_cur = i - j
            em[:, 1, g, cc, :] = np.where(d_cur >= 0, np.exp(-slope * d_cur), 0.0)
            d_prev = i + 128 - j
            em[:, 0, g, cc, :] = np.where(d_prev < 128, np.exp(-slope * d_prev), 0.0)
    c["c_emask"] = np.ascontiguousarray(em.reshape(128, 2, 2, 512), dtype=np.float32)
    a = np.arange(128)[:, None]
    b = np.arange(128)[None, :]
    same = (a // 64) == (b // 64)
    gd = np.zeros((128, 6, 128), np.float32)
    gd[:, 0, :] = same & (a <= b)
    gd[:, 1, :] = same & (a > b)
    gd[:, 2, :] = (a // 64 == 0) & (b >= 0)
    gd[:, 3, :] = (a // 64 == 1) & (b >= 0)
    gd[:, 4, :] = np.where(same & (b >= a), 0.0, -1.0e4)
    gd[:, 5, :] = same & (b > a)
    c["c_gdn"] = gd
    return c


def prep_weights(inputs):
    out = {}
    w_in = np.asarray(inputs["w_in"], dtype=np.float32)
    perm = np.empty(512, np.int64)
    for cc in range(4):
        for g in range(2):
            perm[cc * 128 + g * 64: cc * 128 + (g + 1) * 64] = (4 * g + cc) * 64 + np.arange(64)
    w_in_p = w_in.copy()
    w_in_p[:, :, 0:512] = w_in[:, :, perm]
    out["w_in_p"] = np.ascontiguousarray(w_in_p)
    w_out = np.asarray(inputs["w_mix_out"], dtype=np.float32)
    w_out_p = w_out.copy()
    w_out_p[:, 0:512, :] = w_out[:, perm, :]
    out["w_out_p"] = np.ascontiguousarray(w_out_p)
    cw = np.asarray(inputs["conv_w"], dtype=np.float32)
    out["conv_wT"] = np.ascontiguousarray(cw.reshape(cw.shape[0], 4, 12, 128).transpose(0, 3, 2, 1))
    for k, v in inputs.items():
        if k not in ("x", "mem", "w_in", "w_mix_out", "conv_w"):
            out[k] = np.ascontiguousarray(v, dtype=np.float32)
    return out


_CACHE = {}


def kernel(**inputs):
    key = "full"
    if key not in _CACHE:
        _CACHE[key] = Builder().build()
    nc = _CACHE[key]
    shared = prep_weights(inputs)
    shared.update(make_consts())
    in_maps = []
    for c in range(NCORES):
        m = dict(shared)
        m["x"] = np.ascontiguousarray(inputs["x"][c], dtype=np.float32)
        m["mem"] = np.ascontiguousarray(inputs["mem"][c], dtype=np.float32)
        in_maps.append(m)
    res = run_bass_kernel_spmd(nc, in_maps, core_ids=list(range(NCORES)))
    return np.stack([np.asarray(r["out"], dtype=np.float32) for r in res.results], axis=0)
```

```python
import contextlib
import numpy as np
import concourse.bass as bass
import concourse.mybir as mybir
from concourse.bass_utils import run_bass_kernel_spmd

F32 = mybir.dt.float32
BF16 = mybir.dt.bfloat16
AF = mybir.ActivationFunctionType
ALU = mybir.AluOpType

D = 1024
SEQ = 4096
DEPTH = 2
MEM = 256
DIN = 2824
DFF = 4096
ALPHA = float((2 * DEPTH) ** 0.25)
LN_EPS = 1e-5
RMS_EPS = 1e-6
NCORES = 8

ENGS = ("pe", "act", "dve", "pool", "sp")
SEM_ROLL = 30000


class _Op:
    __slots__ = ("eng", "fn", "deps", "is_dma", "sem", "val", "signal", "key", "prefetch")


class Prog:
    def __init__(self, nc):
        self.nc = nc
        self.ops = {e: [] for e in ENGS}
        self.last_w = {}
        self.readers = {}
        self.dma_cnt = {}
        self.all_ops = []

    def _deps(self, op, reads, writes, extra=()):
        pr = [r for r in reads if isinstance(r, str) and r.startswith("psum")]
        if pr:
            reads = [r for r in reads if r not in pr]
            writes = list(writes) + [r for r in pr if r not in writes]
        deps = []
        for r in reads:
            w = self.last_w.get(r)
            if w is not None:
                deps.append((w, "raw"))
        for w_ in writes:
            w = self.last_w.get(w_)
            if w is not None:
                deps.append((w, "waw"))
            for rd in self.readers.get(w_, ()):
                deps.append((rd, "war"))
        for d in extra:
            deps.append((d, "raw"))
        for r in reads:
            self.readers.setdefault(r, []).append(op)
        for w_ in writes:
            self.last_w[w_] = op
            self.readers[w_] = []
        out = []
        seen = set()
        for d, kind in deps:
            if d is op or id(d) in seen:
                continue
            if (not d.is_dma) and (not op.is_dma) and d.eng == op.eng:
                if op.eng == "pe" or (kind != "raw" and op.eng != "pool"):
                    continue
            seen.add(id(d))
            out.append(d)
        op.deps = out

    def op(self, eng, fn, reads=(), writes=(), extra=()):
        o = _Op()
        o.eng = eng
        o.fn = fn
        o.is_dma = False
        o.signal = False
        o.sem = None
        o.val = 0
        o.key = None
        o.prefetch = False
        self._deps(o, reads, writes, extra)
        self.ops[eng].append(o)
        self.all_ops.append(o)
        return o

    def dma(self, eng, out, in_, reads=(), writes=(), key=None, prefetch=False, extra=()):
        o = _Op()
        o.eng = eng
        o.is_dma = True
        o.fn = (out, in_)
        o.signal = True
        o.prefetch = prefetch
        o.key = key if key is not None else tuple(writes)[0]
        self.dma_cnt[o.key] = self.dma_cnt.get(o.key, 0) + 1
        o.val = 16 * self.dma_cnt[o.key]
        o.sem = None
        self._deps(o, reads, writes, extra)
        self.ops[eng].append(o)
        self.all_ops.append(o)
        return o

    def fence(self):
        lasts = []
        for e in ENGS:
            last = None
            for o in reversed(self.ops[e]):
                if not o.is_dma and o.fn is not None:
                    last = o
                    break
            if last is not None:
                lasts.append(last)
        dmas = {}
        for o in self.all_ops:
            if o.is_dma and not o.prefetch:
                dmas[o.key] = o
        deps = lasts + list(dmas.values())
        for e in ENGS:
            o = _Op()
            o.eng = e
            o.fn = None
            o.is_dma = False
            o.signal = False
            o.sem = None
            o.val = 0
            o.key = None
            o.prefetch = False
            o.deps = list(deps)
            self.ops[e].append(o)
            self.all_ops.append(o)

    def emit(self, final_wait_ops=()):
        nc = self.nc
        for o in self.all_ops:
            for d in o.deps:
                d.signal = True
        for o in final_wait_ops:
            o.signal = True
        eng_sems = {}
        for e in ENGS:
            cnt = 0
            k = 0
            for o in self.ops[e]:
                if o.is_dma or not o.signal:
                    continue
                if cnt >= SEM_ROLL:
                    k += 1
                    cnt = 0
                cnt += 1
                o.sem = ("eng", e, k)
                o.val = cnt
                eng_sems[(e, k)] = True
        for o in self.all_ops:
            if o.is_dma:
                o.sem = ("dma", o.key)
        all_sem_keys = [("eng", e, k) for (e, k) in eng_sems] + [("dma", k) for k in self.dma_cnt]
        self.n_sems = len(all_sem_keys)
        stats = {"waits": 0, "ops": 0}
        with contextlib.ExitStack() as st:
            semh = {}
            for i, sk in enumerate(all_sem_keys):
                semh[sk] = st.enter_context(nc.semaphore("s%d" % i))
            block = st.enter_context(nc.Block())
            engmap = {"pe": block.tensor, "act": block.scalar, "dve": block.vector,
                      "pool": block.gpsimd, "sp": block.sync}
            for e in ENGS:
                ops = self.ops[e]
                finals = list(final_wait_ops) if e == "sp" else []
                if not ops and not finals:
                    continue

                def body(engine, ops=ops, finals=finals):
                    waited = {}

                    def wait_for(d):
                        if waited.get(d.sem, 0) >= d.val:
                            return
                        engine.wait_ge(semh[d.sem], d.val)
                        waited[d.sem] = d.val
                        stats["waits"] += 1

                    for o in ops:
                        for d in o.deps:
                            wait_for(d)
                        if o.is_dma:
                            out, in_ = o.fn
                            ins = engine.dma_start(out=out, in_=in_)
                        elif o.fn is None:
                            continue
                        else:
                            ins = o.fn(engine)
                        if o.signal:
                            ins.then_inc(semh[o.sem], 16 if o.is_dma else 1)
                        stats["ops"] += 1
                    for d in finals:
                        wait_for(d)

                engmap[e](body)
        self.stats = stats


class Region:
    def __init__(self, big, base, size):
        self.big, self.base, self.size, self.off = big, base, size, 0

    def reset(self):
        self.off = 0

    def f32(self, n):
        assert self.off + n <= self.size, ("arena overflow", self.off, n, self.size)
        v = self.big[:, self.base + self.off: self.base + self.off + n]
        self.off += n
        return v

    def bf16(self, n):
        assert n % 2 == 0
        return self.f32(n // 2).bitcast(BF16)


KB = 256


class Builder:
    def __init__(self, seq=SEQ, phases=("A", "B", "C"), depth=DEPTH, debug=()):
        self.seq = seq
        self.depth = depth
        self.phases = phases
        self.debug = debug

    def build(self):
        nc = bass.Bass("TRN2", target_bir_lowering=False)
        self.nc = nc
        S = self.seq
        dr = {}

        def din(name, shape):
            dr[name] = nc.dram_tensor(name, list(shape), F32, kind="ExternalInput").ap()

        din("x", [S, D])
        din("mem", [MEM, D])
        din("attn_sinks", [DEPTH, 8])
        din("a_log", [DEPTH, 4])
        din("dt_bias", [DEPTH, 4])
        din("gdn_norm_g", [DEPTH, 128])
        din("wq_mem", [DEPTH, D, D])
        din("wk_mem", [DEPTH, D, D])
        din("wv_mem", [DEPTH, D, D])
        din("wo_mem", [DEPTH, D, D])
        din("w_ff1", [DEPTH, D, DFF])
        din("w_ff2", [DEPTH, DFF, D])
        din("ln_g", [DEPTH, 3, D])
        din("ln_b", [DEPTH, 3, D])
        din("c_ident", [128, 128])
        din("c_emask", [128, 2, 2, 512])
        din("c_gdn", [128, 6, 128])
        din("w_in_p", [DEPTH, D, DIN])
        din("w_out_p", [DEPTH, D, D])
        din("conv_wT", [DEPTH, 128, 12, 4])
        dr["out"] = nc.dram_tensor("out", [S, D], F32, kind="ExternalOutput").ap()
        for nm in ("xs0", "xs1", "ypart"):
            dr[nm] = nc.dram_tensor(nm, [S, D], F32).ap()
        for nm, shp in self.debug:
            dr[nm] = nc.dram_tensor(nm, list(shp), F32, kind="ExternalOutput").ap()
        self.dr = dr

        with contextlib.ExitStack() as st:
            self.st = st
            NF = 207 * KB
            big = st.enter_context(nc.sbuf_tensor("big", [128, NF], F32))
            self.CONST = Region(big, 0, 21 * KB)
            self.WA = Region(big, 21 * KB, 64 * KB)
            self.WB = Region(big, 85 * KB, 64 * KB)
            self.WORK = Region(big, 149 * KB, NF - 149 * KB)
            self.psum = [st.enter_context(nc.psum_tensor("pb%d" % i, [128, 512], F32)) for i in range(8)]
            P = Prog(nc)
            self.P = P
            self.finals = []
            self.setup_consts()
            self.program()
            P.emit(final_wait_ops=self.finals)
            self.stats = dict(P.stats, sems=P.n_sems)
        return nc

    def setup_consts(self):
        P, dr = self.P, self.dr
        C = self.CONST
        idf = C.f32(128)
        self.ident = C.bf16(128)
        P.dma("sp", idf, dr["c_ident"], writes=["c_idf"])
        P.op("dve", lambda e: e.tensor_copy(out=self.ident, in_=idf), reads=["c_idf"], writes=["c_ident"])
        self.gb = [C.f32(1024), C.f32(1024)]
        self.ones = C.bf16(128)
        P.op("pool", lambda e: e.memset(self.ones, 1.0), writes=["c_ones"])
        self.identF = idf
        self.emask = C.f32(2048).rearrange("p (a b q) -> p a b q", a=2, b=2)
        P.dma("sp", self.emask, dr["c_emask"], writes=["c_emask"])
        self.gdnc = C.f32(768).rearrange("p (a q) -> p a q", a=6)
        P.dma("sp", self.gdnc, dr["c_gdn"], writes=["c_gdn"])
        self.rot_ex = 0
        self.tap_mix = None

    def tap(self, name, ap, key):
        if name not in [d[0] for d in self.debug]:
            return
        d = self.P.dma("pool", self.dr[name], ap, reads=list(key) if isinstance(key, list) else [key], writes=[("tap", name)])
        self.finals.append(d)

    def mm(self, out, lhsT, rhs, start, stop, reads, writes):
        return self.P.op("pe", lambda e: e.matmul(out, lhsT=lhsT, rhs=rhs, start=start, stop=stop), reads=reads, writes=writes)

    def tr(self, out, in_, reads, writes):
        return self.P.op("pe", lambda e: e.transpose(out=out, in_=in_, identity=self.ident), reads=list(reads) + ["c_ident"], writes=writes)

    def act(self, out, in_, func, reads, writes, bias=None, scale=None, accum_out=None):
        kw = {}
        if bias is not None:
            kw["bias"] = bias
        if scale is not None:
            kw["scale"] = scale
        if accum_out is not None:
            kw["accum_out"] = accum_out
        return self.P.op("act", lambda e: e.activation(out=out, in_=in_, func=func, **kw), reads=reads, writes=writes)

    def cp(self, eng, out, in_, reads, writes):
        if eng == "act":
            return self.P.op("act", lambda e: e.copy(out=out, in_=in_), reads=reads, writes=writes)
        return self.P.op(eng, lambda e: e.tensor_copy(out=out, in_=in_), reads=reads, writes=writes)

    def tt(self, eng, out, in0, in1, op, reads, writes):
        return self.P.op(eng, lambda e: e.tensor_tensor(out=out, in0=in0, in1=in1, op=op), reads=reads, writes=writes)

    def ts(self, eng, out, in0, s1, s2, op0, op1, reads, writes, accum_out=None):
        if op1 is None:
            return self.P.op(eng, lambda e: e.tensor_scalar(out=out, in0=in0, scalar1=s1, scalar2=None, op0=op0), reads=reads, writes=writes)
        if accum_out is not None:
            return self.P.op(eng, lambda e: e.tensor_scalar(out=out, in0=in0, scalar1=s1, scalar2=s2, op0=op0, op1=op1, accum_out=accum_out), reads=reads, writes=writes)
        return self.P.op(eng, lambda e: e.tensor_scalar(out=out, in0=in0, scalar1=s1, scalar2=s2, op0=op0, op1=op1), reads=reads, writes=writes)

    def stt(self, out, in0, scalar, in1, op0, op1, reads, writes):
        return self.P.op("dve", lambda e: e.scalar_tensor_tensor(out=out, in0=in0, scalar=scalar, in1=in1, op0=op0, op1=op1), reads=reads, writes=writes)

    def load_ln(self, l, i):
        P, dr = self.P, self.dr
        P.dma("sp", self.gb[0], dr["ln_g"][l, i:i + 1, :].broadcast_to([128, D]), writes=["ln_g"])
        P.dma("sp", self.gb[1], dr["ln_b"][l, i:i + 1, :].broadcast_to([128, D]), writes=["ln_b"])

    def load_weight(self, dst, src, key, rows_per_part_chunk=128, prefetch=True):
        P = self.P
        kc = dst.shape[1]
        h = max(1, kc // 2)
        srcv = src.rearrange("(k p) n -> p k n", p=128)
        for j, (a, b) in enumerate(((0, h), (h, kc))):
            if a == b:
                continue
            P.dma("pool", dst[:, a:b, :], srcv[:, a:b, :], writes=[(key, j)], prefetch=prefetch)
        return [(key, 0), (key, 1)] if kc > 1 else [(key, 0)]

    def x_to_xT(self, xin, xin_key, xb, xT, nsub, tag):
        self.cp("act", xb, xin, [xin_key], [tag + "xb"])
        for s in range(nsub):
            pb = self.psum[self.rot_pt % 2]
            pkey = "psum%d" % (self.rot_pt % 2)
            self.rot_pt += 1
            pT = pb[:].bitcast(BF16).rearrange("p (k t) -> p k t", k=8)
            for k in range(8):
                self.tr(pT[:, k, :], xb[:, s, k * 128:(k + 1) * 128], [tag + "xb"], [pkey])
            self.cp("dve", xT[:, :, s * 128:(s + 1) * 128], pT, [pkey], [tag + "xT"])

    def ln_tail(self, z, zkey, dst_rows):
        P = self.P
        r = self.ln_rot % 2
        self.ln_rot += 1
        sm = self.ln_small[r]
        st6 = sm[:, 0:12]
        mv = sm[:, 12:14]
        rstd = sm[:, 14:15]
        nmr = sm[:, 15:16]
        lnv = sm[:, 16:17]
        sk = "ln_small%d" % r
        P.op("dve", lambda e: e.bn_stats(out=st6[:, 0:6], in_=z[:, 0:512]), reads=[zkey], writes=[sk + "a"])
        P.op("dve", lambda e: e.bn_stats(out=st6[:, 6:12], in_=z[:, 512:1024]), reads=[zkey], writes=[sk + "b"])
        P.op("dve", lambda e: e.bn_aggr(out=mv, in_=st6.rearrange("p (a b) -> p a b", a=2)), reads=[sk + "a", sk + "b"], writes=[sk + "mv"])
        self.ts("dve", lnv, mv[:, 1:2], LN_EPS, None, ALU.add, None, [sk + "mv"], [sk + "lnv"])
        self.act(lnv, lnv, AF.Ln, [sk + "lnv"], [sk + "lnv"])
        self.act(rstd, lnv, AF.Exp, [sk + "lnv"], [sk + "rstd"], scale=-0.5)
        self.stt(nmr, mv[:, 0:1], -1.0, rstd, ALU.mult, ALU.mult, [sk + "mv", sk + "rstd"], [sk + "nmr"])
        self.act(z, z, AF.Identity, [zkey, sk + "rstd", sk + "nmr"], [zkey], bias=nmr, scale=rstd)
        self.tt("dve", z, z, self.gb[0], ALU.mult, [zkey, "ln_g"], [zkey])
        self.tt("dve", z, z, self.gb[1], ALU.add, [zkey, "ln_b"], [zkey])
        d = P.dma("sp", dst_rows, z, reads=[zkey], writes=[("st", zkey)])
        return d

    def program(self):
        P, dr = self.P, self.dr
        self.rot_pt = 0
        self.ln_rot = 0
        self.rot_ps = 0
        cur = dr["x"]
        bufs = [dr["xs0"], dr["xs1"]]
        bi = 0
        nl = self.depth
        order = [(l, ph) for l in range(nl) for ph in ("A", "B", "C1", "C2") if ph[0] in self.phases]
        wts = {}

        def load(i):
            if i < len(order) and i not in wts:
                l, ph = order[i]
                wts[i] = getattr(self, "weights_" + ph)(l)

        load(0)
        for i, (l, ph) in enumerate(order):
            last = i == len(order) - 1
            if ph == "A":
                dst = dr["out"] if last else bufs[bi]
                self.compute_A(l, wts[i], cur, dst, last)
                P.fence()
                load(i + 1)
                load(i + 2)
            else:
                if ph == "C1":
                    dst = None
                else:
                    dst = dr["out"] if last else bufs[bi]
                getattr(self, "compute_" + ph)(l, wts[i], cur, dst, last)
                P.fence()
                load(i + 1)
                if i + 1 < len(order) and order[i + 1][1] != "A":
                    load(i + 2)
            if dst is not None:
                cur = dst
                bi ^= 1

    def weights_C1(self, l):
        return self._weights_C(l, 0, self.WA, "WA")

    def weights_C2(self, l):
        return self._weights_C(l, 1, self.WB, "WB")

    def _weights_C(self, l, half, slot, slotname):
        dr = self.dr
        slot.reset()
        w1 = slot.bf16(8 * 2048).rearrange("p (k n) -> p k n", k=8)
        w2 = slot.bf16(16 * 1024).rearrange("p (k n) -> p k n", k=16)
        k1 = self.load_weight(w1, dr["w_ff1"][l][:, half * 2048:(half + 1) * 2048], slotname + "a")
        k2 = self.load_weight(w2, dr["w_ff2"][l][half * 2048:(half + 1) * 2048, :], slotname + "b")
        return (w1, w2, k1, k2)

    def compute_C1(self, l, wts, src, dst, last):
        self._compute_C(l, wts, src, dst, last, 0)

    def compute_C2(self, l, wts, src, dst, last):
        self._compute_C(l, wts, src, dst, last, 1)

    def _compute_C(self, l, wts, src, dst, last, half):
        P, dr = self.P, self.dr
        w1, w2, k1, k2 = wts
        S = self.seq
        TM = 256
        NS = TM // 128
        nmt = S // TM
        W = self.WORK
        W.reset()
        if half == 1:
            self.load_ln(l, 2)
        xin = [W.f32(NS * 1024).rearrange("p (s d) -> p s d", s=NS) for _ in range(2)]
        xb = W.bf16(NS * 1024).rearrange("p (s d) -> p s d", s=NS)
        xTs = [W.bf16(8 * TM).rearrange("p (k t) -> p k t", k=8) for _ in range(2)]
        hTs = [W.bf16(16 * TM).rearrange("p (f t) -> p f t", f=16) for _ in range(2)]
        rtmp = [W.f32(TM) for _ in range(2)]
        zb = [W.f32(1024) for _ in range(2)]
        self.ln_small = [W.f32(32) for _ in range(2)]
        srcv = src.rearrange("(m s p) d -> m p s d", s=NS, p=128)
        deferred = []

        def prep(m):
            self.x_to_xT(xin[m % 2], "xin%d" % (m % 2), xb, xTs[m % 2], NS, "C%d" % (m % 2))

        P.dma("sp", xin[0], srcv[0], writes=["xin0"])
        prep(0)
        for m in range(nmt):
            xi = xin[m % 2]
            xk = "xin%d" % (m % 2)
            xT = xTs[m % 2]
            xtag = "C%d" % (m % 2)
            hT = hTs[m % 2]
            hk = "hT%d" % (m % 2)
            if m + 1 < nmt:
                P.dma("sp", xin[(m + 1) % 2], srcv[m + 1], writes=["xin%d" % ((m + 1) % 2)])
            for f in range(16):
                pb = self.psum[2 + f % 2]
                pk = "psum%d" % (2 + f % 2)
                for k in range(8):
                    self.mm(pb[:, 0:TM], w1[:, k, f * 128:(f + 1) * 128], xT[:, k, :], k == 0, k == 7, [xtag + "xT"] + k1, [pk])
                rt = rtmp[f % 2]
                rk = "rtmp%d" % (f % 2)
                self.act(rt, pb[:, 0:TM], AF.Relu, [pk], [rk])
                self.tt("dve", hT[:, f, :], rt, rt, ALU.mult, [rk], [hk])
            if m + 1 < nmt:
                prep(m + 1)
            for fn in deferred:
                fn()
            deferred = []
            for s in range(NS):
                r = (m * NS + s) % 2
                pys = [self.psum[4 + 2 * r], self.psum[5 + 2 * r]]
                pyk = ["psum%d" % (4 + 2 * r), "psum%d" % (5 + 2 * r)]
                rows = slice(m * TM + s * 128, m * TM + (s + 1) * 128)
                z = zb[r]
                zk = "zb%d" % r
                if half == 1:
                    P.dma("sp", z, dr["ypart"][rows, :], writes=[zk])
                for n in range(2):
                    for f in range(16):
                        self.mm(pys[n][:], hT[:, f, s * 128:(s + 1) * 128], w2[:, f, n * 512:(n + 1) * 512],
                                f == 0, f == 15, [hk] + k2, [pyk[n]])
                if half == 0:
                    for n in range(2):
                        self.stt(z[:, n * 512:(n + 1) * 512], xi[:, s, n * 512:(n + 1) * 512], ALPHA, pys[n][:], ALU.mult, ALU.add, [xk, pyk[n]], [zk])
                    P.dma("sp", dr["ypart"][rows, :], z, reads=[zk], writes=[("st", zk)])
                else:
                    for n in range(2):
                        self.tt("dve", z[:, n * 512:(n + 1) * 512], z[:, n * 512:(n + 1) * 512], pys[n][:], ALU.add, [pyk[n], zk], [zk])

                    def tail(z=z, zk=zk, rows=rows):
                        d = self.ln_tail(z, zk, dst[rows, :])
                        if last:
                            self.finals.append(d)
                    deferred.append(tail)
        for fn in deferred:
            fn()

    def weights_B(self, l):
        dr = self.dr
        slot = self.WB
        slot.reset()
        ws = []
        ks = []
        views = {nm: slot.bf16(8 * 1024).rearrange("p (k n) -> p k n", k=8) for nm in ("wq_mem", "wk_mem", "wv_mem", "wo_mem")}
        keys = {}
        for nm in ("wk_mem", "wv_mem", "wq_mem", "wo_mem"):
            keys[nm] = self.load_weight(views[nm], dr[nm][l], "WB" + nm[1])
        for nm in ("wq_mem", "wk_mem", "wv_mem", "wo_mem"):
            ws.append(views[nm])
            ks.append(keys[nm])
        return ws, ks

    def bank(self):
        i = 2 + self.rot_ps % 4
        self.rot_ps += 1
        return self.psum[i], "psum%d" % i

    def compute_B(self, l, wts, src, dst, last):
        P, dr = self.P, self.dr
        (wq, wk, wv, wo), (kq, kk, kv, ko) = wts
        S = self.seq
        TM = 256
        NS = 2
        nmt = S // TM
        W = self.WORK
        W.reset()
        self.load_ln(l, 1)
        xb = W.bf16(NS * 1024).rearrange("p (s d) -> p s d", s=NS)
        xT = W.bf16(8 * TM).rearrange("p (k t) -> p k t", k=8)
        kTm = W.bf16(8 * MEM).rearrange("p (c m) -> p c m", c=8)
        vm = W.bf16(2 * 1024).rearrange("p (c n) -> p c n", c=2)
        qTs = [W.bf16(8 * TM).rearrange("p (c t) -> p c t", c=8) for _ in range(2)]
        pT = [W.bf16(2 * TM).rearrange("p (c t) -> p c t", c=2) for _ in range(4)]
        rden = [W.f32(TM) for _ in range(4)]
        oTn = W.bf16(8 * TM).rearrange("p (c t) -> p c t", c=8)
        zb = [W.f32(1024) for _ in range(4)]
        self.ln_small = [W.f32(32) for _ in range(2)]
        ones = self.ones
        deferred = []

        def transposes(tag):
            for s in range(NS):
                pb = self.psum[self.rot_pt % 2]
                pkey = "psum%d" % (self.rot_pt % 2)
                self.rot_pt += 1
                pTt = pb[:].bitcast(BF16).rearrange("p (k t) -> p k t", k=8)
                for k in range(8):
                    self.tr(pTt[:, k, :], xb[:, s, k * 128:(k + 1) * 128], ["Bxb"], [pkey])
                self.cp("dve", xT[:, :, s * 128:(s + 1) * 128], pTt, [pkey], ["BxT"])

        P.dma("pool", xb, dr["mem"].rearrange("(s p) d -> p s d", p=128), writes=["Bxb"])
        transposes("B")
        for c in range(8):
            pb, pk = self.bank()
            for k in range(8):
                self.mm(pb[:, 0:MEM], wk[:, k, c * 128:(c + 1) * 128], xT[:, k, :], k == 0, k == 7, ["BxT"] + kk, [pk])
            self.cp("act" if c % 2 else "dve", kTm[:, c, :], pb[:, 0:MEM], [pk], ["kTm"])
        for mc in range(2):
            for n in range(2):
                pb, pk = self.bank()
                for k in range(8):
                    self.mm(pb[:], xT[:, k, mc * 128:(mc + 1) * 128], wv[:, k, n * 512:(n + 1) * 512], k == 0, k == 7, ["BxT"] + kv, [pk])
                self.cp("act" if n % 2 else "dve", vm[:, mc, n * 512:(n + 1) * 512], pb[:], [pk], ["vm"])
        srcv = src.rearrange("(m s p) d -> m p s d", s=NS, p=128)

        def X_task(m):
            qT = qTs[m % 2]
            qk = "qT%d" % (m % 2)
            if m == 0:
                P.dma("pool", xb, srcv[0], writes=["Bxb"])
            transposes("B")
            if m + 1 < nmt:
                P.dma("pool", xb, srcv[m + 1], writes=["Bxb"])
            yield
            for c in range(8):
                pb, pk = self.bank()
                for k in range(8):
                    self.mm(pb[:, 0:TM], wq[:, k, c * 128:(c + 1) * 128], xT[:, k, :], k == 0, k == 7, ["BxT"] + kq, [pk])
                self.act(qT[:, c, :], pb[:, 0:TM], AF.Copy, [pk], [qk], scale=1.0 / 16.0)
                yield

        def head_task(m, h):
            qT = qTs[m % 2]
            qk = "qT%d" % (m % 2)
            pb, pk = self.bank()
            sT = pb[:].rearrange("p (c t) -> p c t", c=2)
            for mc in range(2):
                for dc in range(2):
                    self.mm(sT[:, mc, :], kTm[:, 2 * h + dc, mc * 128:(mc + 1) * 128], qT[:, 2 * h + dc, :], dc == 0, dc == 1, ["kTm", qk], [pk])
            pt = pT[h]
            ptk = "pT%d" % h
            self.act(pt, sT, AF.Exp, [pk], [ptk])
            yield
            pbo, pko = self.bank()
            oT = pbo[:].rearrange("p (c t) -> p c t", c=2)
            for dc in range(2):
                for mc in range(2):
                    self.mm(oT[:, dc, :], vm[:, mc, h * 256 + dc * 128: h * 256 + (dc + 1) * 128], pt[:, mc, :], mc == 0, mc == 1, ["vm", ptk], [pko])
            pbd, pkd = self.bank()
            for mc in range(2):
                self.mm(pbd[:, 0:TM], ones, pt[:, mc, :], mc == 0, mc == 1, ["c_ones", ptk], [pkd])
            rd = rden[h]
            rdk = "rden%d" % h
            self.act(rd, pbd[:, 0:TM], AF.Ln, [pkd], [rdk])
            self.act(rd, rd, AF.Exp, [rdk], [rdk], scale=-1.0)
            for dc in range(2):
                self.tt("dve", oTn[:, 2 * h + dc, :], oT[:, dc, :], rd, ALU.mult, [pko, rdk], ["oTn%d" % h])
            yield

        def load_res(m):
            for s in range(NS):
                rows = slice(m * TM + s * 128, m * TM + (s + 1) * 128)
                r = (m * NS + s) % 4
                P.dma("sp", zb[r], src[rows, :], writes=["zb%d" % r])

        def post_task(m):
            for s in range(NS):
                rows = slice(m * TM + s * 128, m * TM + (s + 1) * 128)
                r = (m * NS + s) % 4
                z = zb[r]
                zk = "zb%d" % r
                for n in range(2):
                    pbn = self.psum[6 + n]
                    pkn = "psum%d" % (6 + n)
                    for c in range(8):
                        self.mm(pbn[:], oTn[:, c, s * 128:(s + 1) * 128], wo[:, c, n * 512:(n + 1) * 512], c == 0, c == 7,
                                ["oTn%d" % (c // 2)] + ko, [pkn])
                    self.stt(z[:, n * 512:(n + 1) * 512], z[:, n * 512:(n + 1) * 512], ALPHA, pbn[:], ALU.mult, ALU.add, [zk, pkn], [zk])
                    yield

                def tail(z=z, zk=zk, rows=rows):
                    d = self.ln_tail(z, zk, dst[rows, :])
                    if last:
                        self.finals.append(d)
                deferred.append(tail)

        def run(groups):
            bg = groups.pop(0)
            for grp in groups:
                alive = list(grp)
                while alive:
                    for t in list(alive):
                        try:
                            next(t)
                        except StopIteration:
                            alive.remove(t)
                    if bg is not None:
                        try:
                            next(bg)
                        except StopIteration:
                            bg = None
            while bg is not None:
                try:
                    next(bg)
                except StopIteration:
                    bg = None

        run([X_task(0), []])
        for m in range(nmt):
            bg = X_task(m + 1) if m + 1 < nmt else None
            load_res(m)
            fl = list(deferred)
            del deferred[:]

            def flush(fl=fl):
                for fn in fl:
                    fn()
                    yield

            run([bg, [head_task(m, h) for h in range(4)] + [flush()], [post_task(m)]])
        for fn in deferred:
            fn()

    def weights_A(self, l):
        dr = self.dr
        slot = self.WA
        slot.reset()
        win = dr["w_in_p"][l]
        spec = (("q", 0, 512), ("k", 512, 128), ("v", 640, 128), ("g", 768, 1536), ("ab", 2304, 8), ("z", 2312, 512))
        w = {}
        ks = {}
        for nm, c0, n in spec:
            w[nm] = slot.bf16(8 * n).rearrange("p (k n) -> p k n", k=8)
            ks[nm] = self.load_weight(w[nm], win[:, c0:c0 + n], "WA" + nm)
        w["o"] = slot.bf16(8 * 1024).rearrange("p (k n) -> p k n", k=8)
        ks["o"] = self.load_weight(w["o"], dr["w_out_p"][l], "WAo")
        return w, ks

    def compute_A(self, l, wts, src, dst, last):
        P, dr = self.P, self.dr
        w, ks = wts
        S = self.seq
        TM = 256
        NS = 2
        nmt = S // TM
        R = Region(self.WB.big, self.WB.base, self.WB.size + self.WORK.size)
        ones, identF = self.ones, self.identF
        Ublk, Lblk, Csel0, Csel1, NEGc, SM = [self.gdnc[:, i, :] for i in range(6)]
        self.rotA = 0

        def bank():
            i = self.rotA % 8
            self.rotA += 1
            return self.psum[i], "psum%d" % i

        def h4(ap):
            return ap.rearrange("p (h d) -> p h d", h=4)

        xb = R.bf16(NS * 1024).rearrange("p (s d) -> p s d", s=NS)
        xT = R.bf16(8 * TM).rearrange("p (k t) -> p k t", k=8)
        aqT = [R.bf16(4 * TM).rearrange("p (c t) -> p c t", c=4) for _ in range(2)]
        akT = [R.bf16(TM) for _ in range(2)]
        vtok = [R.bf16(TM).rearrange("p (b d) -> p b d", b=2) for _ in range(2)]
        qTn = [R.bf16(4 * TM).rearrange("p (h t) -> p h t", h=4) for _ in range(2)]
        kTn = [R.bf16(4 * TM).rearrange("p (h t) -> p h t", h=4) for _ in range(2)]
        vT = [R.bf16(4 * TM).rearrange("p (h t) -> p h t", h=4) for _ in range(2)]
        ab = [R.f32(16).rearrange("p (s c) -> p s c", s=2) for _ in range(2)]
        zg = [[R.bf16(512) for _ in range(2)] for _ in range(2)]
        NG = 4
        gbuf = [R.f32(260) for _ in range(NG)]
        cacc = [R.f32(TM) for _ in range(NG)]
        chalo = R.f32(48).rearrange("p (c j) -> p c j", c=12)
        sq = [R.bf16(TM) for _ in range(NG)]
        zgf = R.f32(512)
        exb = [R.f32(512) for _ in range(2)]
        ptb = [R.bf16(512) for _ in range(4)]
        dtot = [R.f32(512) for _ in range(2)]
        mixT = [R.bf16(8 * 128).rearrange("p (c t) -> p c t", c=8) for _ in range(2)]
        oall = [h4(R.f32(512)) for _ in range(2)]
        og = h4(R.bf16(512))
        Sst = h4(R.f32(512))
        Sb = h4(R.bf16(512))
        zb = [R.f32(1024) for _ in range(2)]
        self.ln_small = [R.f32(32) for _ in range(2)]
        convw = R.f32(48).rearrange("p (c j) -> p c j", c=12)
        esk = R.f32(4)
        esink_b = R.f32(512)
        nA = R.f32(4)
        dtb = R.f32(4)
        ng4 = R.f32(512)
        smalls = [[R.f32(64) for _ in range(2)] for _ in range(2)]
        Hs = []
        for s_ in range(2):
            d = {}
            for nm in ("fA", "Ec", "Nf", "Pf"):
                d[nm] = h4(R.f32(512))
            d["Nb"] = [h4(R.bf16(512)) for _ in range(2)]
            d["NTb"] = [h4(R.bf16(512)) for _ in range(2)]
            for nm in ("Pb", "intraT", "kd", "r", "vnew", "vtf"):
                d[nm] = h4(R.bf16(512))
            Hs.append(d)
        self.arenaA = R.off

        P.dma("sp", convw, dr["conv_wT"][l], writes=["convw"])
        P.dma("sp", esk[0:64, :], dr["attn_sinks"][l:l + 1, 0:4].broadcast_to([64, 4]), writes=["esk0"])
        P.dma("sp", esk[64:128, :], dr["attn_sinks"][l:l + 1, 4:8].broadcast_to([64, 4]), writes=["esk1"])
        P.dma("sp", nA, dr["a_log"][l:l + 1, :].broadcast_to([128, 4]), writes=["nA"])
        P.dma("sp", dtb, dr["dt_bias"][l:l + 1, :].broadcast_to([128, 4]), writes=["dtb"])
        ngcol = ng4[:, 0:1]
        P.dma("sp", ngcol, dr["gdn_norm_g"][l].rearrange("(p o) -> p o", o=1), writes=["ngcol"])
        self.load_ln(l, 0)
        for c in range(4):
            self.act(esink_b[:, c * 128:(c + 1) * 128], identF, AF.Exp, ["esk0", "esk1", "c_idf"], ["esink_b"], bias=esk[:, c:c + 1], scale=0.0)
        self.act(nA, nA, AF.Exp, ["nA"], ["nA"])
        self.ts("dve", nA, nA, -1.0, None, ALU.mult, None, ["nA"], ["nA"])
        P.op("pool", lambda e: e.memset(Sst, 0.0), writes=["S"])
        P.op("pool", lambda e: e.memset(Sb, 0.0), writes=["Sb"])
        P.op("pool", lambda e: e.memset(chalo, 0.0), writes=["chalo%d" % c for c in range(12)])

        emask = self.emask
        srcv = src.rearrange("(m s p) d -> m p s d", s=NS, p=128)
        kq, kk, kv, kg, kab, kz, ko = ks["q"], ks["k"], ks["v"], ks["g"], ks["ab"], ks["z"], ks["o"]
        wq, wk, wv, wg, wab, wz, wo = w["q"], w["k"], w["v"], w["g"], w["ab"], w["z"], w["o"]
        deferredA = []

        def bc_h(ap4, n=128, rows=slice(0, 128)):
            a = ap4[rows, :]
            return a.unsqueeze(2).to_broadcast([a.shape[0], 4, n])

        def bc_m(ap, rows=slice(0, 128)):
            a = ap[rows, :]
            return a.unsqueeze(1).to_broadcast([a.shape[0], 4, a.shape[1]])

        def small_task(p, s):
            pk_ = "p%d" % p
            sm = smalls[p][s]
            smk = "sm%d" % s + pk_
            abk = "ab%d" % s + pk_
            x4, ax, e4, l4, sp4, g4, eb4, beta = [sm[:, 4 * i:4 * i + 4] for i in range(8)]
            edec = sm[:, 32:48]
            necum = sm[:, 48:52]
            self.tt("dve", x4, ab[p][:, s, 0:4], dtb, ALU.add, [abk, "dtb"], [smk + "x"])
            self.stt(ax, x4, -1.0, x4, ALU.mult, ALU.max, [smk + "x"], [smk + "ax"])
            self.act(e4, ax, AF.Exp, [smk + "ax"], [smk + "e"], scale=-1.0)
            self.act(l4, e4, AF.Ln, [smk + "e"], [smk + "l"], bias=1.0)
            self.stt(sp4, x4, 0.0, l4, ALU.max, ALU.add, [smk + "x", smk + "l"], [smk + "sp"])
            self.tt("dve", g4, sp4, nA, ALU.mult, [smk + "sp", "nA"], [smk + "g"])
            self.act(eb4, ab[p][:, s, 4:8], AF.Exp, [abk], [smk + "eb"], scale=-1.0)
            self.ts("dve", eb4, eb4, 1.0, None, ALU.add, None, [smk + "eb"], [smk + "eb"])
            P.op("dve", lambda e, beta=beta, eb4=eb4: e.reciprocal(out=beta, in_=eb4), reads=[smk + "eb"], writes=[smk + "beta"])
            pb, pk = bank()
            for i, msk in enumerate((Ublk, Lblk, Csel0, Csel1)):
                self.mm(pb[:, 4 * i:4 * i + 4], msk, g4, True, True, ["c_gdn", smk + "g"], [pk])
            self.act(edec, pb[:, 0:16], AF.Exp, [pk], [smk + "edec"])
            self.ts("dve", necum, edec[:, 0:4], -1.0, None, ALU.mult, None, [smk + "edec"], [smk + "necum"])

        def X_task(m):
            p = m % 2
            pk_ = "p%d" % p
            P.dma("pool", xb, srcv[m], writes=["Axb"])
            for s in range(NS):
                pb = self.psum[self.rot_pt % 2 * 0 + 0] if False else None
                pbk, pkk = bank()
                pT = pbk[:].bitcast(BF16).rearrange("p (k t) -> p k t", k=8)
                for k in range(8):
                    self.tr(pT[:, k, :], xb[:, s, k * 128:(k + 1) * 128], ["Axb"], [pkk])
                self.cp("act", xT[:, :, s * 128:(s + 1) * 128], pT, [pkk], ["AxT"])
                yield
            for c in range(4):
                pb, pk = bank()
                for k in range(8):
                    self.mm(pb[:, 0:TM], wq[:, k, c * 128:(c + 1) * 128], xT[:, k, :], k == 0, k == 7, ["AxT"] + kq, [pk])
                self.act(aqT[p][:, c, :], pb[:, 0:TM], AF.Copy, [pk], ["aqT" + pk_], scale=0.125)
                yield
            pb, pk = bank()
            for k in range(8):
                self.mm(pb[:, 0:TM], wk[:, k, :], xT[:, k, :], k == 0, k == 7, ["AxT"] + kk, [pk])
            self.cp("act", akT[p], pb[:, 0:TM], [pk], ["akT" + pk_])
            yield
            for s in range(NS):
                pb, pk = bank()
                for k in range(8):
                    self.mm(pb[:, 0:128], xT[:, k, s * 128:(s + 1) * 128], wv[:, k, :], k == 0, k == 7, ["AxT"] + kv, [pk])
                self.cp("act", vtok[p][:, s, :], pb[:, 0:128], [pk], ["vt" + pk_])
                yield
            for s in range(NS):
                pb, pk = bank()
                for k in range(8):
                    self.mm(pb[:, 0:8], xT[:, k, s * 128:(s + 1) * 128], wab[:, k, :], k == 0, k == 7, ["AxT"] + kab, [pk])
                self.cp("dve", ab[p][:, s, :], pb[:, 0:8], [pk], ["ab%d" % s + pk_])
                yield
                pb, pk = bank()
                for k in range(8):
                    self.mm(pb[:], xT[:, k, s * 128:(s + 1) * 128], wz[:, k, :], k == 0, k == 7, ["AxT"] + kz, [pk])
                self.act(zg[p][s], pb[:], AF.Silu, [pk], ["zg%d" % s + pk_])
                yield
                small_task(p, s)
                yield

            def conv_task(ch, slot):
                gb_ = gbuf[slot]
                gk = "gbuf%d" % slot
                hkk = "chalo%d" % ch
                acc = cacc[slot]
                ak = "cacc%d" % slot
                pb, pk = bank()
                for k in range(8):
                    self.mm(pb[:, 0:TM], wg[:, k, ch * 128:(ch + 1) * 128], xT[:, k, :], k == 0, k == 7, ["AxT"] + kg, [pk])
                self.cp("pool", gb_[:, 0:3], chalo[:, ch, 0:3], [hkk], [gk + "h"])
                self.cp("act", gb_[:, 3:259], pb[:, 0:TM], [pk], [gk])
                self.cp("pool", chalo[:, ch, 0:3], gb_[:, 256:259], [gk], [hkk])
                yield
                self.ts("dve", acc, gb_[:, 3:259], convw[:, ch, 3:4], None, ALU.mult, None, [gk, "convw"], [ak])
                yield
                for j in (2, 1, 0):
                    self.stt(acc, gb_[:, j:j + TM], convw[:, ch, j:j + 1], acc, ALU.mult, ALU.add, [gk, gk + "h", "convw", ak], [ak])
                    yield
                if ch < 8:
                    self.act(acc, acc, AF.Silu, [ak], [ak])
                    yield
                    sqb = sq[slot]
                    sqk = "sq%d" % slot
                    self.act(sqb, acc, AF.Square, [ak], [sqk])
                    yield
                    pb2, pk2 = bank()
                    self.mm(pb2[:, 0:TM], ones, sqb, True, True, ["c_ones", sqk], [pk2])
                    lr = gb_[:, 0:TM]
                    gkk = [gk, gk + "h"]
                    self.ts("dve", lr, pb2[:, 0:TM], RMS_EPS, None, ALU.add, None, [pk2], gkk)
                    yield
                    self.act(lr, lr, AF.Ln, gkk, gkk)
                    yield
                    self.act(lr, lr, AF.Exp, gkk, gkk, scale=-0.5)
                    yield
                    if ch < 4:
                        self.stt(qTn[p][:, ch, :], acc, float(128 ** -0.5), lr, ALU.mult, ALU.mult, [ak] + gkk, ["qTn" + pk_])
                    else:
                        self.tt("dve", kTn[p][:, ch - 4, :], acc, lr, ALU.mult, [ak] + gkk, ["kTn" + pk_])
                else:
                    self.act(vT[p][:, ch - 8, :], acc, AF.Silu, [ak], ["vT" + pk_])
                yield

            for grp in range(3):
                alive = [conv_task(grp * NG + i, i) for i in range(NG)]
                while alive:
                    for t in list(alive):
                        try:
                            next(t)
                        except StopIteration:
                            alive.remove(t)
                    yield

        def make_Y(m):
            p = m % 2
            pk_ = "p%d" % p
            q_ = "p%d" % (1 - p)

            def attn_task(s):
                blk = m * NS + s
                tok = slice(s * 128, (s + 1) * 128)
                mx = mixT[blk % 2]
                mka = "mixTa%d" % (blk % 2)
                dt = dtot[s]
                for g in range(2):
                    gp = slice(g * 64, (g + 1) * 64)
                    dk = "dtot%d_%d" % (s, g)
                    kbs = ([0] if blk > 0 else []) + [1]
                    pts = []
                    for kb in kbs:
                        pb, pk = bank()
                        if kb == 1:
                            kap, kkey = akT[p][gp, s * 128:(s + 1) * 128], "akT" + pk_
                        elif s == 1:
                            kap, kkey = akT[p][gp, 0:128], "akT" + pk_
                        else:
                            kap, kkey = akT[1 - p][gp, 128:256], "akT" + q_
                        self.mm(pb[:], kap, aqT[p][gp, :, tok], True, True, [kkey, "aqT" + pk_], [pk])
                        ei = self.rot_ex % 2
                        pi = self.rot_ex % 4
                        self.rot_ex += 1
                        self.act(exb[ei], pb[:], AF.Exp, [pk], ["exb%d" % ei])
                        self.tt("pool", ptb[pi], exb[ei], emask[:, kb, g, :], ALU.mult, ["exb%d" % ei, "c_emask"], ["ptb%d" % pi])
                        pts.append((kb, ptb[pi], "ptb%d" % pi))
                        yield
                    pbo, pko = bank()
                    pbd, pkd = bank()
                    for idx, (kb, pt, ptk) in enumerate(pts):
                        if kb == 1:
                            vap, vkey = vtok[p][:, s, gp], "vt" + pk_
                        elif s == 1:
                            vap, vkey = vtok[p][:, 0, gp], "vt" + pk_
                        else:
                            vap, vkey = vtok[1 - p][:, 1, gp], "vt" + q_
                        self.mm(pbo[gp, :], vap, pt, idx == 0, idx == len(pts) - 1, [vkey, ptk], [pko])
                    for idx, (kb, pt, ptk) in enumerate(pts):
                        self.mm(pbd[gp, :], ones[:, 0:64], pt, idx == 0, idx == len(pts) - 1, ["c_ones", ptk], [pkd])
                    self.tt("dve", dt[gp, :], pbd[gp, :], esink_b[gp, :], ALU.add, [pkd, "esink_b"], [dk])
                    self.act(dt[gp, :], dt[gp, :], AF.Ln, [dk], [dk])
                    self.act(dt[gp, :], dt[gp, :], AF.Exp, [dk], [dk], scale=-1.0)
                    self.tt("dve", mx[gp, 0:4, :], pbo[gp, :].rearrange("p (c q) -> p c q", c=4), dt[gp, :].rearrange("p (c q) -> p c q", c=4),
                            ALU.mult, [pko, dk], [mka + "_%d" % g])
                    yield

            def small_task_unused(s):
                sm = smalls[p][s]
                smk = "sm%d" % s + pk_
                abk = "ab%d" % s + pk_
                x4, ax, e4, l4, sp4, g4, eb4, beta = [sm[:, 4 * i:4 * i + 4] for i in range(8)]
                edec = sm[:, 32:48]
                necum = sm[:, 48:52]
                self.tt("dve", x4, ab[p][:, s, 0:4], dtb, ALU.add, [abk, "dtb"], [smk + "x"])
                self.stt(ax, x4, -1.0, x4, ALU.mult, ALU.max, [smk + "x"], [smk + "ax"])
                self.act(e4, ax, AF.Exp, [smk + "ax"], [smk + "e"], scale=-1.0)
                self.act(l4, e4, AF.Ln, [smk + "e"], [smk + "l"], bias=1.0)
                self.stt(sp4, x4, 0.0, l4, ALU.max, ALU.add, [smk + "x", smk + "l"], [smk + "sp"])
                self.tt("dve", g4, sp4, nA, ALU.mult, [smk + "sp", "nA"], [smk + "g"])
                self.act(eb4, ab[p][:, s, 4:8], AF.Exp, [abk], [smk + "eb"], scale=-1.0)
                self.ts("dve", eb4, eb4, 1.0, None, ALU.add, None, [smk + "eb"], [smk + "eb"])
                P.op("dve", lambda e, beta=beta, eb4=eb4: e.reciprocal(out=beta, in_=eb4), reads=[smk + "eb"], writes=[smk + "beta"])
                pb, pk = bank()
                for i, msk in enumerate((Ublk, Lblk, Csel0, Csel1)):
                    self.mm(pb[:, 4 * i:4 * i + 4], msk, g4, True, True, ["c_gdn", smk + "g"], [pk])
                self.act(edec, pb[:, 0:16], AF.Exp, [pk], [smk + "edec"])
                self.ts("dve", necum, edec[:, 0:4], -1.0, None, ALU.mult, None, [smk + "edec"], [smk + "necum"])

            def pre_task(s):
                d = Hs[s]
                sm = smalls[p][s]
                smk = "sm%d" % s + pk_
                g4 = sm[:, 20:24]
                beta = sm[:, 28:32]
                edec = sm[:, 32:48]
                tok = slice(s * 128, (s + 1) * 128)
                K_ = lambda nm: "H%d%s" % (s, nm)
                kT_keys = ["kTn" + pk_]
                qT_keys = ["qTn" + pk_]
                vT_keys = ["vT" + pk_]
                kT, qT, vTt = kTn[p], qTn[p], vT[p]
                self.tt("pool", d["fA"], bc_m(Ublk), bc_h(g4), ALU.mult, ["c_gdn", smk + "g"], [K_("fA")])
                pbt, pkt = bank()
                ptv = pbt[:].bitcast(BF16)
                for h in range(4):
                    self.tr(ptv[:, h * 128:(h + 1) * 128], kT[:, h, tok], kT_keys, [pkt])
                for h in range(4):
                    self.tr(ptv[:, 512 + h * 128:512 + (h + 1) * 128], vTt[:, h, tok], vT_keys, [pkt])
                self.tt("dve", d["kd"], h4(ptv[:, 0:512]), bc_h(edec[:, 4:8]), ALU.mult, [pkt, smk + "edec"], [K_("kd")])
                self.cp("act", d["vtf"], h4(ptv[:, 512:1024]), [pkt], [K_("vtf")])
                pbE, pkE = bank()
                for h in range(4):
                    self.mm(pbE[:, h * 128:(h + 1) * 128], Lblk, d["fA"][:, h, :], True, False, ["c_gdn", K_("fA")], [pkE])
                    self.mm(pbE[:, h * 128:(h + 1) * 128], identF, NEGc, False, True, ["c_gdn", "c_idf"], [pkE])
                self.act(d["Ec"], h4(pbE[:]), AF.Exp, [pkE], [K_("Ec")])
                self.tt("pool", d["fA"], d["Ec"], bc_m(SM), ALU.mult, [K_("Ec"), "c_gdn"], [K_("fA")])
                self.tt("pool", d["fA"], d["fA"], bc_h(beta), ALU.mult, [K_("fA"), smk + "beta"], [K_("fA")])
                yield
                pbG, pkG = bank()
                for h in range(4):
                    self.mm(pbG[:, h * 128:(h + 1) * 128], kT[:, h, tok], kT[:, h, tok], True, True, kT_keys, [pkG])
                self.tt("dve", d["Nf"], h4(pbG[:]), d["fA"], ALU.mult, [pkG, K_("fA")], [K_("Nf")])
                self.cp("act", d["Nb"][0], d["Nf"], [K_("Nf")], [K_("Nb0")])
                self.stt(d["Pf"], d["Nf"], -1.0, bc_m(identF), ALU.mult, ALU.add, [K_("Nf"), "c_idf"], [K_("Pf")])
                self.cp("act", d["Pb"], d["Pf"], [K_("Pf")], [K_("Pb")])
                pbI, pkI = bank()
                for h in range(4):
                    self.mm(pbI[:, h * 128:(h + 1) * 128], kT[:, h, tok], qT[:, h, tok], True, True, kT_keys + qT_keys, [pkI])
                self.tt("dve", d["intraT"], h4(pbI[:]), d["Ec"], ALU.mult, [pkI, K_("Ec")], [K_("intraT")])
                yield
                pbt, pkt = bank()
                ptv = pbt[:].bitcast(BF16)
                for h in range(4):
                    self.tr(ptv[:, h * 128:(h + 1) * 128], d["Nb"][0][:, h, :], [K_("Nb0")], [pkt])
                self.cp("act", d["NTb"][0], h4(ptv[:, 0:512]), [pkt], [K_("NTb0")])
                yield

                def square(lev):
                    cur = (lev - 1) % 2
                    nxt = lev % 2
                    kN = [K_("NTb%d" % cur), K_("Nb%d" % cur)]
                    if lev < 5:
                        pbn, pkn = bank()
                        for h in range(4):
                            self.mm(pbn[:, h * 128:(h + 1) * 128], d["NTb"][cur][:, h, :], d["Nb"][cur][:, h, :], True, True, kN, [pkn])
                        self.cp("act", d["Nb"][nxt], h4(pbn[:]), [pkn], [K_("Nb%d" % nxt)])
                    pbn2, pkn2 = bank()
                    for h in range(4):
                        self.mm(pbn2[:, h * 128:(h + 1) * 128], d["Nb"][cur][:, h, :], d["NTb"][cur][:, h, :], True, True, kN, [pkn2])
                    self.cp("act", d["NTb"][nxt], h4(pbn2[:]), [pkn2], [K_("NTb%d" % nxt)])

                def pupd(lev):
                    nxt = lev % 2
                    pbp, pkp = bank()
                    for h in range(4):
                        self.mm(pbp[:, h * 128:(h + 1) * 128], d["NTb"][nxt][:, h, :], d["Pb"][:, h, :], True, True, [K_("NTb%d" % nxt), K_("Pb")], [pkp])
                    self.tt("dve", d["Pf"], d["Pf"], h4(pbp[:]), ALU.add, [pkp, K_("Pf")], [K_("Pf")])
                    self.cp("act", d["Pb"], d["Pf"], [K_("Pf")], [K_("Pb")])

                square(1)
                yield
                for lev in range(2, 6):
                    pupd(lev - 1)
                    square(lev)
                    yield
                pupd(5)
                yield

            def scan_task(s):
                d = Hs[s]
                sm = smalls[p][s]
                smk = "sm%d" % s + pk_
                beta = sm[:, 28:32]
                edec = sm[:, 32:48]
                necum = sm[:, 48:52]
                K_ = lambda nm: "H%d%s" % (s, nm)
                oa = oall[s]
                kT, qT = kTn[p], qTn[p]
                kT_keys = ["kTn" + pk_]
                qT_keys = ["qTn" + pk_]
                okey = "oall%d" % s
                for c in range(2):
                    Rr = slice(c * 64, (c + 1) * 64)
                    ctok = slice(s * 128 + c * 64, s * 128 + (c + 1) * 64)
                    v4 = lambda ap: h4(ap[Rr, :])
                    pb1, pk1 = bank()
                    for h in range(4):
                        self.mm(pb1[Rr, h * 128:(h + 1) * 128], kT[:, h, ctok], Sb[:, h, :], True, True, kT_keys + ["Sb"], [pk1])
                    for h in range(4):
                        self.stt(d["r"][Rr, h, :], pb1[Rr, h * 128:(h + 1) * 128], necum[Rr, h:h + 1], d["vtf"][Rr, h, :], ALU.mult, ALU.add,
                                 [pk1, smk + "necum", K_("vtf")], [K_("r")])
                    pb3, pk3 = bank()
                    for h in range(4):
                        self.mm(pb3[Rr, h * 128:(h + 1) * 128], qT[:, h, ctok], Sb[:, h, :], True, True, qT_keys + ["Sb"], [pk3])
                    self.tt("dve", d["Nf"][Rr], v4(pb3), bc_h(edec[:, 0:4], rows=Rr), ALU.mult, [pk3, smk + "edec"], [K_("Nf")])
                    yield
                    pb2, pk2 = bank()
                    for h in range(4):
                        self.mm(pb2[Rr, h * 128:(h + 1) * 128], d["Pb"][Rr, h, c * 64:(c + 1) * 64], d["r"][Rr, h, :], True, True, [K_("Pb"), K_("r")], [pk2])
                    self.tt("dve", d["vnew"][Rr], v4(pb2), bc_h(beta, rows=Rr), ALU.mult, [pk2, smk + "beta"], [K_("vnew")])
                    yield
                    pb5, pk5 = bank()
                    for h in range(4):
                        self.mm(pb5[:, h * 128:(h + 1) * 128], d["kd"][Rr, h, :], d["vnew"][Rr, h, :], True, True, [K_("kd"), K_("vnew")], [pk5])
                    for h in range(4):
                        self.stt(Sst[:, h, :], Sst[:, h, :], edec[:, 8 + 4 * c + h:9 + 4 * c + h], pb5[:, h * 128:(h + 1) * 128], ALU.mult, ALU.add,
                                 [pk5, "S", smk + "edec"], ["S"])
                    self.cp("act", Sb, Sst, ["S"], ["Sb"])
                    pb4, pk4 = bank()
                    for h in range(4):
                        self.mm(pb4[Rr, h * 128:(h + 1) * 128], d["intraT"][Rr, h, c * 64:(c + 1) * 64], d["vnew"][Rr, h, :], True, True, [K_("intraT"), K_("vnew")], [pk4])
                    self.tt("dve", oa[Rr], d["Nf"][Rr], v4(pb4), ALU.add, [pk4, K_("Nf")], [okey])
                    yield

            def post_task(s):
                blk = m * NS + s
                d = Hs[s]
                sm = smalls[p][s]
                smk = "sm%d" % s + pk_
                ss4 = sm[:, 52:56]
                rstd4 = sm[:, 56:60]
                oa = oall[s]
                okey = "oall%d" % s
                K_ = lambda nm: "H%d%s" % (s, nm)
                mx = mixT[blk % 2]
                mka = "mixTa%d" % (blk % 2)
                mkg = "mixTg%d" % (blk % 2)
                rows = slice(m * TM + s * 128, m * TM + (s + 1) * 128)
                r = blk % 2
                z = zb[r]
                zk = "zb%d" % r
                P.dma("sp", z, src[rows, :], writes=[zk])
                self.act(d["Nf"], oa, AF.Square, [okey], [K_("Nf")])
                P.op("dve", lambda e, ss4=ss4, src_=d["Nf"]: e.tensor_reduce(out=ss4, in_=src_, axis=mybir.AxisListType.X, op=ALU.add),
                     reads=[K_("Nf")], writes=[smk + "ss"])
                yield
                self.ts("dve", rstd4, ss4, 1.0 / 128.0, RMS_EPS, ALU.mult, ALU.add, [smk + "ss"], [smk + "rstd"])
                self.act(rstd4, rstd4, AF.Ln, [smk + "rstd"], [smk + "rstd"])
                self.act(rstd4, rstd4, AF.Exp, [smk + "rstd"], [smk + "rstd"], scale=-0.5)
                yield
                for h in range(4):
                    self.stt(og[:, h, :], oa[:, h, :], rstd4[:, h:h + 1], zg[p][s][:, h * 128:(h + 1) * 128], ALU.mult, ALU.mult,
                             [okey, smk + "rstd", "zg%d" % s + pk_], ["og"])
                yield
                pbt, pkt = bank()
                ptv = pbt[:].bitcast(BF16)
                for h in range(4):
                    self.tr(ptv[:, h * 128:(h + 1) * 128], og[:, h, :], ["og"], [pkt])
                self.act(mx[:, 4:8, :], ptv[:, 0:512].rearrange("p (c t) -> p c t", c=4), AF.Copy, [pkt, "ngcol"], [mkg], scale=ngcol)
                yield
                for n in range(2):
                    pbn, pkn = bank()
                    for c in range(8):
                        self.mm(pbn[:], mx[:, c, :], wo[:, c, n * 512:(n + 1) * 512], c == 0, c == 7, [mka + "_0", mka + "_1", mkg] + ko, [pkn])
                    self.stt(z[:, n * 512:(n + 1) * 512], z[:, n * 512:(n + 1) * 512], ALPHA, pbn[:], ALU.mult, ALU.add, [zk, pkn], [zk])
                    yield

                def tail(z=z, zk=zk, rows=rows):
                    dd = self.ln_tail(z, zk, dst[rows, :])
                    if last:
                        self.finals.append(dd)
                deferredA.append(tail)
                yield

            return {"attn0": (attn_task(0), []), "attn1": (attn_task(1), []),
                    "pre0": (pre_task(0), []), "pre1": (pre_task(1), ["pre0@2"]),
                    "scan0": (scan_task(0), ["pre0"]), "scan1": (scan_task(1), ["pre1", "scan0"]),
                    "post0": (post_task(0), ["scan0", "attn0"]), "post1": (post_task(1), ["scan1", "attn1", "post0"])}

        def run(tasks):
            done = set()
            steps = {n: 0 for n in tasks}
            active = []
            pending = dict(tasks)

            def ok(dep):
                if "@" in dep:
                    n, k = dep.split("@")
                    return n in done or steps.get(n, 0) >= int(k)
                return dep in done or dep not in tasks

            while pending or active:
                for n in list(pending):
                    if all(ok(dp) for dp in pending[n][1]):
                        active.append(n)
                        del pending[n]
                assert active, ("deadlock", list(pending))
                for n in list(active):
                    try:
                        next(tasks[n][0])
                        steps[n] += 1
                    except StopIteration:
                        active.remove(n)
                        done.add(n)
                if deferredA and "X" in tasks and steps.get("X", 0) >= 6:
                    for fn in deferredA:
                        fn()
                    del deferredA[:]

        run({"X": (X_task(0), [])})
        for m in range(nmt):
            tasks = make_Y(m)
            if m + 1 < nmt:
                tasks["X"] = (X_task(m + 1), [])
            run(tasks)
        for fn in deferredA:
            fn()
        del deferredA[:]


def make_consts():
    c = {"c_ident": np.eye(128, dtype=np.float32)}
    j = np.arange(128)[:, None].astype(np.float64)
    i = np.arange(128)[None, :].astype(np.float64)
    em = np.zeros((128, 2, 2, 4, 128), np.float64)
    for g in range(2):
        for cc in range(4):
            h = 4 * g + cc
            slope = 2.0 ** (-8.0 * (h + 1) / 8.0)
            d_cur = i - j
            em[:, 1, g, cc, :] = np.where(d_cur >= 0, np.exp(-slope * d_cur), 0.0)
            d_prev = i + 128 - j
            em[:, 0, g, cc, :] = np.where(d_prev < 128, np.exp(-slope * d_prev), 0.0)
    c["c_emask"] = np.ascontiguousarray(em.reshape(128, 2, 2, 512), dtype=np.float32)
    a = np.arange(128)[:, None]
    b = np.arange(128)[None, :]
    same = (a // 64) == (b // 64)
    gd = np.zeros((128, 6, 128), np.float32)
    gd[:, 0, :] = same & (a <= b)
    gd[:, 1, :] = same & (a > b)
    gd[:, 2, :] = (a // 64 == 0) & (b >= 0)
    gd[:, 3, :] = (a // 64 == 1) & (b >= 0)
    gd[:, 4, :] = np.where(same & (b >= a), 0.0, -1.0e4)
    gd[:, 5, :] = same & (b > a)
    c["c_gdn"] = gd
    return c


def prep_weights(inputs):
    out = {}
    w_in = np.asarray(inputs["w_in"], dtype=np.float32)
    perm = np.empty(512, np.int64)
    for cc in range(4):
        for g in range(2):
            perm[cc * 128 + g * 64: cc * 128 + (g + 1) * 64] = (4 * g + cc) * 64 + np.arange(64)
    w_in_p = w_in.copy()
    w_in_p[:, :, 0:512] = w_in[:, :, perm]
    out["w_in_p"] = np.ascontiguousarray(w_in_p)
    w_out = np.asarray(inputs["w_mix_out"], dtype=np.float32)
    w_out_p = w_out.copy()
    w_out_p[:, 0:512, :] = w_out[:, perm, :]
    out["w_out_p"] = np.ascontiguousarray(w_out_p)
    cw = np.asarray(inputs["conv_w"], dtype=np.float32)
    out["conv_wT"] = np.ascontiguousarray(cw.reshape(cw.shape[0], 4, 12, 128).transpose(0, 3, 2, 1))
    for k, v in inputs.items():
        if k not in ("x", "mem", "w_in", "w_mix_out", "conv_w"):
            out[k] = np.ascontiguousarray(v, dtype=np.float32)
    return out


_CACHE = {}


def kernel(**inputs):
    key = "full"
    if key not in _CACHE:
        _CACHE[key] = Builder().build()
    nc = _CACHE[key]
    shared = prep_weights(inputs)
    shared.update(make_consts())
    in_maps = []
    for c in range(NCORES):
        m = dict(shared)
        m["x"] = np.ascontiguousarray(inputs["x"][c], dtype=np.float32)
        m["mem"] = np.ascontiguousarray(inputs["mem"][c], dtype=np.float32)
        in_maps.append(m)
    res = run_bass_kernel_spmd(nc, in_maps, core_ids=list(range(NCORES)))
    return np.stack([np.asarray(r["out"], dtype=np.float32) for r in res.results], axis=0)
```

```python
import contextlib
import numpy as np
import concourse.bass as bass
import concourse.mybir as mybir
from concourse.bass_utils import run_bass_kernel_spmd

F32 = mybir.dt.float32
BF16 = mybir.dt.bfloat16
AF = mybir.ActivationFunctionType
ALU = mybir.AluOpType

D = 1024
SEQ = 4096
DEPTH = 2
MEM = 256
DIN = 2824
DFF = 4096
ALPHA = float((2 * DEPTH) ** 0.25)
LN_EPS = 1e-5
RMS_EPS = 1e-6
NCORES = 8

ENGS = ("pe", "act", "dve", "pool", "sp")
SEM_ROLL = 30000


class _Op:
    __slots__ = ("eng", "fn", "deps", "is_dma", "sem", "val", "signal", "key", "prefetch")


class Prog:
    def __init__(self, nc):
        self.nc = nc
        self.ops = {e: [] for e in ENGS}
        self.last_w = {}
        self.readers = {}
        self.dma_cnt = {}
        self.all_ops = []

    def _deps(self, op, reads, writes, extra=()):
        pr = [r for r in reads if isinstance(r, str) and r.startswith("psum")]
        if pr:
            reads = [r for r in reads if r not in pr]
            writes = list(writes) + [r for r in pr if r not in writes]
        deps = []
        for r in reads:
            w = self.last_w.get(r)
            if w is not None:
                deps.append((w, "raw"))
        for w_ in writes:
            w = self.last_w.get(w_)
            if w is not None:
                deps.append((w, "waw"))
            for rd in self.readers.get(w_, ()):
                deps.append((rd, "war"))
        for d in extra:
            deps.append((d, "raw"))
        for r in reads:
            self.readers.setdefault(r, []).append(op)
        for w_ in writes:
            self.last_w[w_] = op
            self.readers[w_] = []
        out = []
        seen = set()
        for d, kind in deps:
            if d is op or id(d) in seen:
                continue
            if (not d.is_dma) and (not op.is_dma) and d.eng == op.eng:
                if op.eng == "pe" or (kind != "raw" and op.eng != "pool"):
                    continue
            seen.add(id(d))
            out.append(d)
        op.deps = out

    def op(self, eng, fn, reads=(), writes=(), extra=()):
        o = _Op()
        o.eng = eng
        o.fn = fn
        o.is_dma = False
        o.signal = False
        o.sem = None
        o.val = 0
        o.key = None
        o.prefetch = False
        self._deps(o, reads, writes, extra)
        self.ops[eng].append(o)
        self.all_ops.append(o)
        return o

    def dma(self, eng, out, in_, reads=(), writes=(), key=None, prefetch=False, extra=()):
        o = _Op()
        o.eng = eng
        o.is_dma = True
        o.fn = (out, in_)
        o.signal = True
        o.prefetch = prefetch
        o.key = key if key is not None else tuple(writes)[0]
        self.dma_cnt[o.key] = self.dma_cnt.get(o.key, 0) + 1
        o.val = 16 * self.dma_cnt[o.key]
        o.sem = None
        self._deps(o, reads, writes, extra)
        self.ops[eng].append(o)
        self.all_ops.append(o)
        return o

    def fence(self):
        lasts = []
        for e in ENGS:
            last = None
            for o in reversed(self.ops[e]):
                if not o.is_dma and o.fn is not None:
                    last = o
                    break
            if last is not None:
                lasts.append(last)
        dmas = {}
        for o in self.all_ops:
            if o.is_dma and not o.prefetch:
                dmas[o.key] = o
        deps = lasts + list(dmas.values())
        for e in ENGS:
            o = _Op()
            o.eng = e
            o.fn = None
            o.is_dma = False
            o.signal = False
            o.sem = None
            o.val = 0
            o.key = None
            o.prefetch = False
            o.deps = list(deps)
            self.ops[e].append(o)
            self.all_ops.append(o)

    def emit(self, final_wait_ops=()):
        nc = self.nc
        for o in self.all_ops:
            for d in o.deps:
                d.signal = True
        for o in final_wait_ops:
            o.signal = True
        eng_sems = {}
        for e in ENGS:
            cnt = 0
            k = 0
            for o in self.ops[e]:
                if o.is_dma or not o.signal:
                    continue
                if cnt >= SEM_ROLL:
                    k += 1
                    cnt = 0
                cnt += 1
                o.sem = ("eng", e, k)
                o.val = cnt
                eng_sems[(e, k)] = True
        for o in self.all_ops:
            if o.is_dma:
                o.sem = ("dma", o.key)
        all_sem_keys = [("eng", e, k) for (e, k) in eng_sems] + [("dma", k) for k in self.dma_cnt]
        self.n_sems = len(all_sem_keys)
        stats = {"waits": 0, "ops": 0}
        with contextlib.ExitStack() as st:
            semh = {}
            for i, sk in enumerate(all_sem_keys):
                semh[sk] = st.enter_context(nc.semaphore("s%d" % i))
            block = st.enter_context(nc.Block())
            engmap = {"pe": block.tensor, "act": block.scalar, "dve": block.vector,
                      "pool": block.gpsimd, "sp": block.sync}
            for e in ENGS:
                ops = self.ops[e]
                finals = list(final_wait_ops) if e == "sp" else []
                if not ops and not finals:
                    continue

                def body(engine, ops=ops, finals=finals):
                    waited = {}

                    def wait_for(d):
                        if waited.get(d.sem, 0) >= d.val:
                            return
                        engine.wait_ge(semh[d.sem], d.val)
                        waited[d.sem] = d.val
                        stats["waits"] += 1

                    for o in ops:
                        for d in o.deps:
                            wait_for(d)
                        if o.is_dma:
                            out, in_ = o.fn
                            ins = engine.dma_start(out=out, in_=in_)
                        elif o.fn is None:
                            continue
                        else:
                            ins = o.fn(engine)
                        if o.signal:
                            ins.then_inc(semh[o.sem], 16 if o.is_dma else 1)
                        stats["ops"] += 1
                    for d in finals:
                        wait_for(d)

                engmap[e](body)
        self.stats = stats


class Region:
    def __init__(self, big, base, size):
        self.big, self.base, self.size, self.off = big, base, size, 0

    def reset(self):
        self.off = 0

    def f32(self, n):
        assert self.off + n <= self.size, ("arena overflow", self.off, n, self.size)
        v = self.big[:, self.base + self.off: self.base + self.off + n]
        self.off += n
        return v

    def bf16(self, n):
        assert n % 2 == 0
        return self.f32(n // 2).bitcast(BF16)


KB = 256


class Builder:
    def __init__(self, seq=SEQ, phases=("A", "B", "C"), depth=DEPTH, debug=()):
        self.seq = seq
        self.depth = depth
        self.phases = phases
        self.debug = debug

    def build(self):
        nc = bass.Bass("TRN2", target_bir_lowering=False)
        self.nc = nc
        S = self.seq
        dr = {}

        def din(name, shape):
            dr[name] = nc.dram_tensor(name, list(shape), F32, kind="ExternalInput").ap()

        din("x", [S, D])
        din("mem", [MEM, D])
        din("attn_sinks", [DEPTH, 8])
        din("a_log", [DEPTH, 4])
        din("dt_bias", [DEPTH, 4])
        din("gdn_norm_g", [DEPTH, 128])
        din("wq_mem", [DEPTH, D, D])
        din("wk_mem", [DEPTH, D, D])
        din("wv_mem", [DEPTH, D, D])
        din("wo_mem", [DEPTH, D, D])
        din("w_ff1", [DEPTH, D, DFF])
        din("w_ff2", [DEPTH, DFF, D])
        din("ln_g", [DEPTH, 3, D])
        din("ln_b", [DEPTH, 3, D])
        din("c_ident", [128, 128])
        din("c_emask", [128, 2, 2, 512])
        din("c_gdn", [128, 6, 128])
        din("w_in_p", [DEPTH, D, DIN])
        din("w_out_p", [DEPTH, D, D])
        din("conv_wT", [DEPTH, 128, 12, 4])
        dr["out"] = nc.dram_tensor("out", [S, D], F32, kind="ExternalOutput").ap()
        for nm in ("xs0", "xs1", "ypart"):
            dr[nm] = nc.dram_tensor(nm, [S, D], F32).ap()
        for nm, shp in self.debug:
            dr[nm] = nc.dram_tensor(nm, list(shp), F32, kind="ExternalOutput").ap()
        self.dr = dr

        with contextlib.ExitStack() as st:
            self.st = st
            NF = 207 * KB
            big = st.enter_context(nc.sbuf_tensor("big", [128, NF], F32))
            self.CONST = Region(big, 0, 21 * KB)
            self.WA = Region(big, 21 * KB, 64 * KB)
            self.WB = Region(big, 85 * KB, 64 * KB)
            self.WORK = Region(big, 149 * KB, NF - 149 * KB)
            self.psum = [st.enter_context(nc.psum_tensor("pb%d" % i, [128, 512], F32)) for i in range(8)]
            P = Prog(nc)
            self.P = P
            self.finals = []
            self.setup_consts()
            self.program()
            P.emit(final_wait_ops=self.finals)
            self.stats = dict(P.stats, sems=P.n_sems)
        return nc

    def setup_consts(self):
        P, dr = self.P, self.dr
        C = self.CONST
        idf = C.f32(128)
        self.ident = C.bf16(128)
        P.dma("sp", idf, dr["c_ident"], writes=["c_idf"])
        P.op("dve", lambda e: e.tensor_copy(out=self.ident, in_=idf), reads=["c_idf"], writes=["c_ident"])
        self.gb = [C.f32(1024), C.f32(1024)]
        self.ones = C.bf16(128)
        P.op("pool", lambda e: e.memset(self.ones, 1.0), writes=["c_ones"])
        self.identF = idf
        self.emask = C.f32(2048).rearrange("p (a b q) -> p a b q", a=2, b=2)
        P.dma("sp", self.emask, dr["c_emask"], writes=["c_emask"])
        self.gdnc = C.f32(768).rearrange("p (a q) -> p a q", a=6)
        P.dma("sp", self.gdnc, dr["c_gdn"], writes=["c_gdn"])
        self.rot_ex = 0
        self.tap_mix = None

    def tap(self, name, ap, key):
        if name not in [d[0] for d in self.debug]:
            return
        d = self.P.dma("pool", self.dr[name], ap, reads=list(key) if isinstance(key, list) else [key], writes=[("tap", name)])
        self.finals.append(d)

    def mm(self, out, lhsT, rhs, start, stop, reads, writes):
        return self.P.op("pe", lambda e: e.matmul(out, lhsT=lhsT, rhs=rhs, start=start, stop=stop), reads=reads, writes=writes)

    def tr(self, out, in_, reads, writes):
        return self.P.op("pe", lambda e: e.transpose(out=out, in_=in_, identity=self.ident), reads=list(reads) + ["c_ident"], writes=writes)

    def act(self, out, in_, func, reads, writes, bias=None, scale=None, accum_out=None):
        kw = {}
        if bias is not None:
            kw["bias"] = bias
        if scale is not None:
            kw["scale"] = scale
        if accum_out is not None:
            kw["accum_out"] = accum_out
        return self.P.op("act", lambda e: e.activation(out=out, in_=in_, func=func, **kw), reads=reads, writes=writes)

    def cp(self, eng, out, in_, reads, writes):
        if eng == "act":
            return self.P.op("act", lambda e: e.copy(out=out, in_=in_), reads=reads, writes=writes)
        return self.P.op(eng, lambda e: e.tensor_copy(out=out, in_=in_), reads=reads, writes=writes)

    def tt(self, eng, out, in0, in1, op, reads, writes):
        return self.P.op(eng, lambda e: e.tensor_tensor(out=out, in0=in0, in1=in1, op=op), reads=reads, writes=writes)

    def ts(self, eng, out, in0, s1, s2, op0, op1, reads, writes, accum_out=None):
        if op1 is None:
            return self.P.op(eng, lambda e: e.tensor_scalar(out=out, in0=in0, scalar1=s1, scalar2=None, op0=op0), reads=reads, writes=writes)
        if accum_out is not None:
            return self.P.op(eng, lambda e: e.tensor_scalar(out=out, in0=in0, scalar1=s1, scalar2=s2, op0=op0, op1=op1, accum_out=accum_out), reads=reads, writes=writes)
        return self.P.op(eng, lambda e: e.tensor_scalar(out=out, in0=in0, scalar1=s1, scalar2=s2, op0=op0, op1=op1), reads=reads, writes=writes)

    def stt(self, out, in0, scalar, in1, op0, op1, reads, writes):
        return self.P.op("dve", lambda e: e.scalar_tensor_tensor(out=out, in0=in0, scalar=scalar, in1=in1, op0=op0, op1=op1), reads=reads, writes=writes)

    def load_ln(self, l, i):
        P, dr = self.P, self.dr
        P.dma("sp", self.gb[0], dr["ln_g"][l, i:i + 1, :].broadcast_to([128, D]), writes=["ln_g"])
        P.dma("sp", self.gb[1], dr["ln_b"][l, i:i + 1, :].broadcast_to([128, D]), writes=["ln_b"])

    def load_weight(self, dst, src, key, rows_per_part_chunk=128, prefetch=True):
        P = self.P
        kc = dst.shape[1]
        h = max(1, kc // 2)
        srcv = src.rearrange("(k p) n -> p k n", p=128)
        for j, (a, b) in enumerate(((0, h), (h, kc))):
            if a == b:
                continue
            P.dma("pool", dst[:, a:b, :], srcv[:, a:b, :], writes=[(key, j)], prefetch=prefetch)
        return [(key, 0), (key, 1)] if kc > 1 else [(key, 0)]

    def x_to_xT(self, xin, xin_key, xb, xT, nsub, tag):
        self.cp("act", xb, xin, [xin_key], [tag + "xb"])
        for s in range(nsub):
            pb = self.psum[self.rot_pt % 2]
            pkey = "psum%d" % (self.rot_pt % 2)
            self.rot_pt += 1
            pT = pb[:].bitcast(BF16).rearrange("p (k t) -> p k t", k=8)
            for k in range(8):
                self.tr(pT[:, k, :], xb[:, s, k * 128:(k + 1) * 128], [tag + "xb"], [pkey])
            self.cp("dve", xT[:, :, s * 128:(s + 1) * 128], pT, [pkey], [tag + "xT"])

    def ln_tail(self, z, zkey, dst_rows):
        P = self.P
        r = self.ln_rot % 2
        self.ln_rot += 1
        sm = self.ln_small[r]
        st6 = sm[:, 0:12]
        mv = sm[:, 12:14]
        rstd = sm[:, 14:15]
        nmr = sm[:, 15:16]
        lnv = sm[:, 16:17]
        sk = "ln_small%d" % r
        P.op("dve", lambda e: e.bn_stats(out=st6[:, 0:6], in_=z[:, 0:512]), reads=[zkey], writes=[sk + "a"])
        P.op("dve", lambda e: e.bn_stats(out=st6[:, 6:12], in_=z[:, 512:1024]), reads=[zkey], writes=[sk + "b"])
        P.op("dve", lambda e: e.bn_aggr(out=mv, in_=st6.rearrange("p (a b) -> p a b", a=2)), reads=[sk + "a", sk + "b"], writes=[sk + "mv"])
        self.ts("dve", lnv, mv[:, 1:2], LN_EPS, None, ALU.add, None, [sk + "mv"], [sk + "lnv"])
        self.act(lnv, lnv, AF.Ln, [sk + "lnv"], [sk + "lnv"])
        self.act(rstd, lnv, AF.Exp, [sk + "lnv"], [sk + "rstd"], scale=-0.5)
        self.stt(nmr, mv[:, 0:1], -1.0, rstd, ALU.mult, ALU.mult, [sk + "mv", sk + "rstd"], [sk + "nmr"])
        self.act(z, z, AF.Identity, [zkey, sk + "rstd", sk + "nmr"], [zkey], bias=nmr, scale=rstd)
        self.tt("dve", z, z, self.gb[0], ALU.mult, [zkey, "ln_g"], [zkey])
        self.tt("dve", z, z, self.gb[1], ALU.add, [zkey, "ln_b"], [zkey])
        d = P.dma("sp", dst_rows, z, reads=[zkey], writes=[("st", zkey)])
        return d

    def program(self):
        P, dr = self.P, self.dr
        self.rot_pt = 0
        self.ln_rot = 0
        self.rot_ps = 0
        cur = dr["x"]
        bufs = [dr["xs0"], dr["xs1"]]
        bi = 0
        nl = self.depth
        order = [(l, ph) for l in range(nl) for ph in ("A", "B", "C1", "C2") if ph[0] in self.phases]
        wts = {}

        def load(i):
            if i < len(order) and i not in wts:
                l, ph = order[i]
                wts[i] = getattr(self, "weights_" + ph)(l)

        load(0)
        for i, (l, ph) in enumerate(order):
            last = i == len(order) - 1
            if ph == "A":
                dst = dr["out"] if last else bufs[bi]
                self.compute_A(l, wts[i], cur, dst, last)
                P.fence()
                load(i + 1)
                load(i + 2)
            else:
                if ph == "C1":
                    dst = None
                else:
                    dst = dr["out"] if last else bufs[bi]
                getattr(self, "compute_" + ph)(l, wts[i], cur, dst, last)
                P.fence()
                load(i + 1)
                if i + 1 < len(order) and order[i + 1][1] != "A":
                    load(i + 2)
            if dst is not None:
                cur = dst
                bi ^= 1

    def weights_C1(self, l):
        return self._weights_C(l, 0, self.WA, "WA")

    def weights_C2(self, l):
        return self._weights_C(l, 1, self.WB, "WB")

    def _weights_C(self, l, half, slot, slotname):
        dr = self.dr
        slot.reset()
        w1 = slot.bf16(8 * 2048).rearrange("p (k n) -> p k n", k=8)
        w2 = slot.bf16(16 * 1024).rearrange("p (k n) -> p k n", k=16)
        k1 = self.load_weight(w1, dr["w_ff1"][l][:, half * 2048:(half + 1) * 2048], slotname + "a")
        k2 = self.load_weight(w2, dr["w_ff2"][l][half * 2048:(half + 1) * 2048, :], slotname + "b")
        return (w1, w2, k1, k2)

    def compute_C1(self, l, wts, src, dst, last):
        self._compute_C(l, wts, src, dst, last, 0)

    def compute_C2(self, l, wts, src, dst, last):
        self._compute_C(l, wts, src, dst, last, 1)

    def _compute_C(self, l, wts, src, dst, last, half):
        P, dr = self.P, self.dr
        w1, w2, k1, k2 = wts
        S = self.seq
        TM = 256
        NS = TM // 128
        nmt = S // TM
        W = self.WORK
        W.reset()
        if half == 1:
            self.load_ln(l, 2)
        xin = [W.f32(NS * 1024).rearrange("p (s d) -> p s d", s=NS) for _ in range(2)]
        xb = W.bf16(NS * 1024).rearrange("p (s d) -> p s d", s=NS)
        xTs = [W.bf16(8 * TM).rearrange("p (k t) -> p k t", k=8) for _ in range(2)]
        hTs = [W.bf16(16 * TM).rearrange("p (f t) -> p f t", f=16) for _ in range(2)]
        rtmp = [W.f32(TM) for _ in range(2)]
        zb = [W.f32(1024) for _ in range(2)]
        self.ln_small = [W.f32(32) for _ in range(2)]
        srcv = src.rearrange("(m s p) d -> m p s d", s=NS, p=128)
        deferred = []

        def prep(m):
            self.x_to_xT(xin[m % 2], "xin%d" % (m % 2), xb, xTs[m % 2], NS, "C%d" % (m % 2))

        P.dma("sp", xin[0], srcv[0], writes=["xin0"])
        prep(0)
        for m in range(nmt):
            xi = xin[m % 2]
            xk = "xin%d" % (m % 2)
            xT = xTs[m % 2]
            xtag = "C%d" % (m % 2)
            hT = hTs[m % 2]
            hk = "hT%d" % (m % 2)
            if m + 1 < nmt:
                P.dma("sp", xin[(m + 1) % 2], srcv[m + 1], writes=["xin%d" % ((m + 1) % 2)])
            for f in range(16):
                pb = self.psum[2 + f % 2]
                pk = "psum%d" % (2 + f % 2)
                for k in range(8):
                    self.mm(pb[:, 0:TM], w1[:, k, f * 128:(f + 1) * 128], xT[:, k, :], k == 0, k == 7, [xtag + "xT"] + k1, [pk])
                rt = rtmp[f % 2]
                rk = "rtmp%d" % (f % 2)
                self.act(rt, pb[:, 0:TM], AF.Relu, [pk], [rk])
                self.tt("dve", hT[:, f, :], rt, rt, ALU.mult, [rk], [hk])
            if m + 1 < nmt:
                prep(m + 1)
            for fn in deferred:
                fn()
            deferred = []
            for s in range(NS):
                r = (m * NS + s) % 2
                pys = [self.psum[4 + 2 * r], self.psum[5 + 2 * r]]
                pyk = ["psum%d" % (4 + 2 * r), "psum%d" % (5 + 2 * r)]
                rows = slice(m * TM + s * 128, m * TM + (s + 1) * 128)
                z = zb[r]
                zk = "zb%d" % r
                if half == 1:
                    P.dma("sp", z, dr["ypart"][rows, :], writes=[zk])
                for n in range(2):
                    for f in range(16):
                        self.mm(pys[n][:], hT[:, f, s * 128:(s + 1) * 128], w2[:, f, n * 512:(n + 1) * 512],
                                f == 0, f == 15, [hk] + k2, [pyk[n]])
                if half == 0:
                    for n in range(2):
                        self.stt(z[:, n * 512:(n + 1) * 512], xi[:, s, n * 512:(n + 1) * 512], ALPHA, pys[n][:], ALU.mult, ALU.add, [xk, pyk[n]], [zk])
                    P.dma("sp", dr["ypart"][rows, :], z, reads=[zk], writes=[("st", zk)])
                else:
                    for n in range(2):
                        self.tt("dve", z[:, n * 512:(n + 1) * 512], z[:, n * 512:(n + 1) * 512], pys[n][:], ALU.add, [pyk[n], zk], [zk])

                    def tail(z=z, zk=zk, rows=rows):
                        d = self.ln_tail(z, zk, dst[rows, :])
                        if last:
                            self.finals.append(d)
                    deferred.append(tail)
        for fn in deferred:
            fn()

    def weights_B(self, l):
        dr = self.dr
        slot = self.WB
        slot.reset()
        ws = []
        ks = []
        views = {nm: slot.bf16(8 * 1024).rearrange("p (k n) -> p k n", k=8) for nm in ("wq_mem", "wk_mem", "wv_mem", "wo_mem")}
        keys = {}
        for nm in ("wk_mem", "wv_mem", "wq_mem", "wo_mem"):
            keys[nm] = self.load_weight(views[nm], dr[nm][l], "WB" + nm[1])
        for nm in ("wq_mem", "wk_mem", "wv_mem", "wo_mem"):
            ws.append(views[nm])
            ks.append(keys[nm])
        return ws, ks

    def bank(self):
        i = 2 + self.rot_ps % 4
        self.rot_ps += 1
        return self.psum[i], "psum%d" % i

    def compute_B(self, l, wts, src, dst, last):
        P, dr = self.P, self.dr
        (wq, wk, wv, wo), (kq, kk, kv, ko) = wts
        S = self.seq
        TM = 256
        NS = 2
        nmt = S // TM
        W = self.WORK
        W.reset()
        self.load_ln(l, 1)
        xb = W.bf16(NS * 1024).rearrange("p (s d) -> p s d", s=NS)
        xT = W.bf16(8 * TM).rearrange("p (k t) -> p k t", k=8)
        kTm = W.bf16(8 * MEM).rearrange("p (c m) -> p c m", c=8)
        vm = W.bf16(2 * 1024).rearrange("p (c n) -> p c n", c=2)
        qTs = [W.bf16(8 * TM).rearrange("p (c t) -> p c t", c=8) for _ in range(2)]
        pT = [W.bf16(2 * TM).rearrange("p (c t) -> p c t", c=2) for _ in range(4)]
        rden = [W.f32(TM) for _ in range(4)]
        oTn = W.bf16(8 * TM).rearrange("p (c t) -> p c t", c=8)
        zb = [W.f32(1024) for _ in range(4)]
        self.ln_small = [W.f32(32) for _ in range(2)]
        ones = self.ones
        deferred = []

        def transposes(tag):
            for s in range(NS):
                pb = self.psum[self.rot_pt % 2]
                pkey = "psum%d" % (self.rot_pt % 2)
                self.rot_pt += 1
                pTt = pb[:].bitcast(BF16).rearrange("p (k t) -> p k t", k=8)
                for k in range(8):
                    self.tr(pTt[:, k, :], xb[:, s, k * 128:(k + 1) * 128], ["Bxb"], [pkey])
                self.cp("dve", xT[:, :, s * 128:(s + 1) * 128], pTt, [pkey], ["BxT"])

        P.dma("pool", xb, dr["mem"].rearrange("(s p) d -> p s d", p=128), writes=["Bxb"])
        transposes("B")
        for c in range(8):
            pb, pk = self.bank()
            for k in range(8):
                self.mm(pb[:, 0:MEM], wk[:, k, c * 128:(c + 1) * 128], xT[:, k, :], k == 0, k == 7, ["BxT"] + kk, [pk])
            self.cp("act" if c % 2 else "dve", kTm[:, c, :], pb[:, 0:MEM], [pk], ["kTm"])
        for mc in range(2):
            for n in range(2):
                pb, pk = self.bank()
                for k in range(8):
                    self.mm(pb[:], xT[:, k, mc * 128:(mc + 1) * 128], wv[:, k, n * 512:(n + 1) * 512], k == 0, k == 7, ["BxT"] + kv, [pk])
                self.cp("act" if n % 2 else "dve", vm[:, mc, n * 512:(n + 1) * 512], pb[:], [pk], ["vm"])
        srcv = src.rearrange("(m s p) d -> m p s d", s=NS, p=128)

        def X_task(m):
            qT = qTs[m % 2]
            qk = "qT%d" % (m % 2)
            if m == 0:
                P.dma("pool", xb, srcv[0], writes=["Bxb"])
            transposes("B")
            if m + 1 < nmt:
                P.dma("pool", xb, srcv[m + 1], writes=["Bxb"])
            yield
            for c in range(8):
                pb, pk = self.bank()
                for k in range(8):
                    self.mm(pb[:, 0:TM], wq[:, k, c * 128:(c + 1) * 128], xT[:, k, :], k == 0, k == 7, ["BxT"] + kq, [pk])
                self.act(qT[:, c, :], pb[:, 0:TM], AF.Copy, [pk], [qk], scale=1.0 / 16.0)
                yield

        def head_task(m, h):
            qT = qTs[m % 2]
            qk = "qT%d" % (m % 2)
            pb, pk = self.bank()
            sT = pb[:].rearrange("p (c t) -> p c t", c=2)
            for mc in range(2):
                for dc in range(2):
                    self.mm(sT[:, mc, :], kTm[:, 2 * h + dc, mc * 128:(mc + 1) * 128], qT[:, 2 * h + dc, :], dc == 0, dc == 1, ["kTm", qk], [pk])
            pt = pT[h]
            ptk = "pT%d" % h
            self.act(pt, sT, AF.Exp, [pk], [ptk])
            yield
            pbo, pko = self.bank()
            oT = pbo[:].rearrange("p (c t) -> p c t", c=2)
            for dc in range(2):
                for mc in range(2):
                    self.mm(oT[:, dc, :], vm[:, mc, h * 256 + dc * 128: h * 256 + (dc + 1) * 128], pt[:, mc, :], mc == 0, mc == 1, ["vm", ptk], [pko])
            pbd, pkd = self.bank()
            for mc in range(2):
                self.mm(pbd[:, 0:TM], ones, pt[:, mc, :], mc == 0, mc == 1, ["c_ones", ptk], [pkd])
            rd = rden[h]
            rdk = "rden%d" % h
            self.act(rd, pbd[:, 0:TM], AF.Ln, [pkd], [rdk])
            self.act(rd, rd, AF.Exp, [rdk], [rdk], scale=-1.0)
            for dc in range(2):
                self.tt("dve", oTn[:, 2 * h + dc, :], oT[:, dc, :], rd, ALU.mult, [pko, rdk], ["oTn%d" % h])
            yield

        def load_res(m):
            for s in range(NS):
                rows = slice(m * TM + s * 128, m * TM + (s + 1) * 128)
                r = (m * NS + s) % 4
                P.dma("sp", zb[r], src[rows, :], writes=["zb%d" % r])

        def post_task(m):
            for s in range(NS):
                rows = slice(m * TM + s * 128, m * TM + (s + 1) * 128)
                r = (m * NS + s) % 4
                z = zb[r]
                zk = "zb%d" % r
                for n in range(2):
                    pbn = self.psum[6 + n]
                    pkn = "psum%d" % (6 + n)
                    for c in range(8):
                        self.mm(pbn[:], oTn[:, c, s * 128:(s + 1) * 128], wo[:, c, n * 512:(n + 1) * 512], c == 0, c == 7,
                                ["oTn%d" % (c // 2)] + ko, [pkn])
                    self.stt(z[:, n * 512:(n + 1) * 512], z[:, n * 512:(n + 1) * 512], ALPHA, pbn[:], ALU.mult, ALU.add, [zk, pkn], [zk])
                    yield

                def tail(z=z, zk=zk, rows=rows):
                    d = self.ln_tail(z, zk, dst[rows, :])
                    if last:
                        self.finals.append(d)
                deferred.append(tail)

        def run(groups):
            bg = groups.pop(0)
            for grp in groups:
                alive = list(grp)
                while alive:
                    for t in list(alive):
                        try:
                            next(t)
                        except StopIteration:
                            alive.remove(t)
                    if bg is not None:
                        try:
                            next(bg)
                        except StopIteration:
                            bg = None
            while bg is not None:
                try:
                    next(bg)
                except StopIteration:
                    bg = None

        run([X_task(0), []])
        for m in range(nmt):
            bg = X_task(m + 1) if m + 1 < nmt else None
            load_res(m)
            fl = list(deferred)
            del deferred[:]

            def flush(fl=fl):
                for fn in fl:
                    fn()
                    yield

            run([bg, [head_task(m, h) for h in range(4)] + [flush()], [post_task(m)]])
        for fn in deferred:
            fn()

    def weights_A(self, l):
        dr = self.dr
        slot = self.WA
        slot.reset()
        win = dr["w_in_p"][l]
        spec = (("q", 0, 512), ("k", 512, 128), ("v", 640, 128), ("g", 768, 1536), ("ab", 2304, 8), ("z", 2312, 512))
        w = {}
        ks = {}
        for nm, c0, n in spec:
            w[nm] = slot.bf16(8 * n).rearrange("p (k n) -> p k n", k=8)
            ks[nm] = self.load_weight(w[nm], win[:, c0:c0 + n], "WA" + nm)
        w["o"] = slot.bf16(8 * 1024).rearrange("p (k n) -> p k n", k=8)
        ks["o"] = self.load_weight(w["o"], dr["w_out_p"][l], "WAo")
        return w, ks

    def compute_A(self, l, wts, src, dst, last):
        P, dr = self.P, self.dr
        w, ks = wts
        S = self.seq
        TM = 256
        NS = 2
        nmt = S // TM
        R = Region(self.WB.big, self.WB.base, self.WB.size + self.WORK.size)
        ones, identF = self.ones, self.identF
        Ublk, Lblk, Csel0, Csel1, NEGc, SM = [self.gdnc[:, i, :] for i in range(6)]
        self.rotA = 0

        def bank():
            i = self.rotA % 8
            self.rotA += 1
            return self.psum[i], "psum%d" % i

        def h4(ap):
            return ap.rearrange("p (h d) -> p h d", h=4)

        xb = R.bf16(NS * 1024).rearrange("p (s d) -> p s d", s=NS)
        xT = R.bf16(8 * TM).rearrange("p (k t) -> p k t", k=8)
        aqT = [R.bf16(4 * TM).rearrange("p (c t) -> p c t", c=4) for _ in range(2)]
        akT = [R.bf16(TM) for _ in range(2)]
        vtok = [R.bf16(TM).rearrange("p (b d) -> p b d", b=2) for _ in range(2)]
        qTn = [R.bf16(4 * TM).rearrange("p (h t) -> p h t", h=4) for _ in range(2)]
        kTn = [R.bf16(4 * TM).rearrange("p (h t) -> p h t", h=4) for _ in range(2)]
        vT = [R.bf16(4 * TM).rearrange("p (h t) -> p h t", h=4) for _ in range(2)]
        ab = [R.f32(16).rearrange("p (s c) -> p s c", s=2) for _ in range(2)]
        zg = [[R.bf16(512) for _ in range(2)] for _ in range(2)]
        NG = 4
        gbuf = [R.f32(260) for _ in range(NG)]
        cacc = [R.f32(TM) for _ in range(NG)]
        chalo = R.f32(48).rearrange("p (c j) -> p c j", c=12)
        sq = [R.bf16(TM) for _ in range(NG)]
        zgf = R.f32(512)
        exb = [R.f32(512) for _ in range(2)]
        ptb = [R.bf16(512) for _ in range(4)]
        dtot = [R.f32(512) for _ in range(2)]
        mixT = [R.bf16(8 * 128).rearrange("p (c t) -> p c t", c=8) for _ in range(2)]
        oall = [h4(R.f32(512)) for _ in range(2)]
        og = h4(R.bf16(512))
        Sst = h4(R.f32(512))
        Sb = h4(R.bf16(512))
        zb = [R.f32(1024) for _ in range(2)]
        self.ln_small = [R.f32(32) for _ in range(2)]
        convw = R.f32(48).rearrange("p (c j) -> p c j", c=12)
        esk = R.f32(4)
        esink_b = R.f32(512)
        nA = R.f32(4)
        dtb = R.f32(4)
        ng4 = R.f32(512)
        smalls = [[R.f32(64) for _ in range(2)] for _ in range(2)]
        Hs = []
        for s_ in range(2):
            d = {}
            for nm in ("fA", "Ec", "Nf", "Pf"):
                d[nm] = h4(R.f32(512))
            d["Nb"] = [h4(R.bf16(512)) for _ in range(2)]
            d["NTb"] = [h4(R.bf16(512)) for _ in range(2)]
            for nm in ("Pb", "intraT", "kd", "r", "vnew", "vtf"):
                d[nm] = h4(R.bf16(512))
            Hs.append(d)
        self.arenaA = R.off

        P.dma("sp", convw, dr["conv_wT"][l], writes=["convw"])
        P.dma("sp", esk[0:64, :], dr["attn_sinks"][l:l + 1, 0:4].broadcast_to([64, 4]), writes=["esk0"])
        P.dma("sp", esk[64:128, :], dr["attn_sinks"][l:l + 1, 4:8].broadcast_to([64, 4]), writes=["esk1"])
        P.dma("sp", nA, dr["a_log"][l:l + 1, :].broadcast_to([128, 4]), writes=["nA"])
        P.dma("sp", dtb, dr["dt_bias"][l:l + 1, :].broadcast_to([128, 4]), writes=["dtb"])
        ngcol = ng4[:, 0:1]
        P.dma("sp", ngcol, dr["gdn_norm_g"][l].rearrange("(p o) -> p o", o=1), writes=["ngcol"])
        self.load_ln(l, 0)
        for c in range(4):
            self.act(esink_b[:, c * 128:(c + 1) * 128], identF, AF.Exp, ["esk0", "esk1", "c_idf"], ["esink_b"], bias=esk[:, c:c + 1], scale=0.0)
        self.act(nA, nA, AF.Exp, ["nA"], ["nA"])
        self.ts("dve", nA, nA, -1.0, None, ALU.mult, None, ["nA"], ["nA"])
        P.op("pool", lambda e: e.memset(Sst, 0.0), writes=["S"])
        P.op("pool", lambda e: e.memset(Sb, 0.0), writes=["Sb"])
        P.op("pool", lambda e: e.memset(chalo, 0.0), writes=["chalo%d" % c for c in range(12)])

        emask = self.emask
        srcv = src.rearrange("(m s p) d -> m p s d", s=NS, p=128)
        kq, kk, kv, kg, kab, kz, ko = ks["q"], ks["k"], ks["v"], ks["g"], ks["ab"], ks["z"], ks["o"]
        wq, wk, wv, wg, wab, wz, wo = w["q"], w["k"], w["v"], w["g"], w["ab"], w["z"], w["o"]
        deferredA = []

        def bc_h(ap4, n=128, rows=slice(0, 128)):
            a = ap4[rows, :]
            return a.unsqueeze(2).to_broadcast([a.shape[0], 4, n])

        def bc_m(ap, rows=slice(0, 128)):
            a = ap[rows, :]
            return a.unsqueeze(1).to_broadcast([a.shape[0], 4, a.shape[1]])

        def small_task(p, s):
            pk_ = "p%d" % p
            sm = smalls[p][s]
            smk = "sm%d" % s + pk_
            abk = "ab%d" % s + pk_
            x4, ax, e4, l4, sp4, g4, eb4, beta = [sm[:, 4 * i:4 * i + 4] for i in range(8)]
            edec = sm[:, 32:48]
            necum = sm[:, 48:52]
            self.tt("dve", x4, ab[p][:, s, 0:4], dtb, ALU.add, [abk, "dtb"], [smk + "x"])
            self.stt(ax, x4, -1.0, x4, ALU.mult, ALU.max, [smk + "x"], [smk + "ax"])
            self.act(e4, ax, AF.Exp, [smk + "ax"], [smk + "e"], scale=-1.0)
            self.act(l4, e4, AF.Ln, [smk + "e"], [smk + "l"], bias=1.0)
            self.stt(sp4, x4, 0.0, l4, ALU.max, ALU.add, [smk + "x", smk + "l"], [smk + "sp"])
            self.tt("dve", g4, sp4, nA, ALU.mult, [smk + "sp", "nA"], [smk + "g"])
            self.act(eb4, ab[p][:, s, 4:8], AF.Exp, [abk], [smk + "eb"], scale=-1.0)
            self.ts("dve", eb4, eb4, 1.0, None, ALU.add, None, [smk + "eb"], [smk + "eb"])
            P.op("dve", lambda e, beta=beta, eb4=eb4: e.reciprocal(out=beta, in_=eb4), reads=[smk + "eb"], writes=[smk + "beta"])
            pb, pk = bank()
            for i, msk in enumerate((Ublk, Lblk, Csel0, Csel1)):
                self.mm(pb[:, 4 * i:4 * i + 4], msk, g4, True, True, ["c_gdn", smk + "g"], [pk])
            self.act(edec, pb[:, 0:16], AF.Exp, [pk], [smk + "edec"])
            self.ts("dve", necum, edec[:, 0:4], -1.0, None, ALU.mult, None, [smk + "edec"], [smk + "necum"])

        def X_task(m):
            p = m % 2
            pk_ = "p%d" % p
            P.dma("pool", xb, srcv[m], writes=["Axb"])
            for s in range(NS):
                pb = self.psum[self.rot_pt % 2 * 0 + 0] if False else None
                pbk, pkk = bank()
                pT = pbk[:].bitcast(BF16).rearrange("p (k t) -> p k t", k=8)
                for k in range(8):
                    self.tr(pT[:, k, :], xb[:, s, k * 128:(k + 1) * 128], ["Axb"], [pkk])
                self.cp("act", xT[:, :, s * 128:(s + 1) * 128], pT, [pkk], ["AxT"])
                yield
            for c in range(4):
                pb, pk = bank()
                for k in range(8):
                    self.mm(pb[:, 0:TM], wq[:, k, c * 128:(c + 1) * 128], xT[:, k, :], k == 0, k == 7, ["AxT"] + kq, [pk])
                self.act(aqT[p][:, c, :], pb[:, 0:TM], AF.Copy, [pk], ["aqT" + pk_], scale=0.125)
                yield
            pb, pk = bank()
            for k in range(8):
                self.mm(pb[:, 0:TM], wk[:, k, :], xT[:, k, :], k == 0, k == 7, ["AxT"] + kk, [pk])
            self.cp("act", akT[p], pb[:, 0:TM], [pk], ["akT" + pk_])
            yield
            for s in range(NS):
                pb, pk = bank()
                for k in range(8):
                    self.mm(pb[:, 0:128], xT[:, k, s * 128:(s + 1) * 128], wv[:, k, :], k == 0, k == 7, ["AxT"] + kv, [pk])
                self.cp("act", vtok[p][:, s, :], pb[:, 0:128], [pk], ["vt" + pk_])
                yield
            for s in range(NS):
                pb, pk = bank()
                for k in range(8):
                    self.mm(pb[:, 0:8], xT[:, k, s * 128:(s + 1) * 128], wab[:, k, :], k == 0, k == 7, ["AxT"] + kab, [pk])
                self.cp("dve", ab[p][:, s, :], pb[:, 0:8], [pk], ["ab%d" % s + pk_])
                yield
                pb, pk = bank()
                for k in range(8):
                    self.mm(pb[:], xT[:, k, s * 128:(s + 1) * 128], wz[:, k, :], k == 0, k == 7, ["AxT"] + kz, [pk])
                self.act(zg[p][s], pb[:], AF.Silu, [pk], ["zg%d" % s + pk_])
                yield
                small_task(p, s)
                yield

            def conv_task(ch, slot):
                gb_ = gbuf[slot]
                gk = "gbuf%d" % slot
                hkk = "chalo%d" % ch
                acc = cacc[slot]
                ak = "cacc%d" % slot
                pb, pk = bank()
                for k in range(8):
                    self.mm(pb[:, 0:TM], wg[:, k, ch * 128:(ch + 1) * 128], xT[:, k, :], k == 0, k == 7, ["AxT"] + kg, [pk])
                self.cp("pool", gb_[:, 0:3], chalo[:, ch, 0:3], [hkk], [gk + "h"])
                self.cp("act", gb_[:, 3:259], pb[:, 0:TM], [pk], [gk])
                self.cp("pool", chalo[:, ch, 0:3], gb_[:, 256:259], [gk], [hkk])
                yield
                self.ts("dve", acc, gb_[:, 3:259], convw[:, ch, 3:4], None, ALU.mult, None, [gk, "convw"], [ak])
                yield
                for j in (2, 1, 0):
                    self.stt(acc, gb_[:, j:j + TM], convw[:, ch, j:j + 1], acc, ALU.mult, ALU.add, [gk, gk + "h", "convw", ak], [ak])
                    yield
                if ch < 8:
                    self.act(acc, acc, AF.Silu, [ak], [ak])
                    yield
                    sqb = sq[slot]
                    sqk = "sq%d" % slot
                    self.act(sqb, acc, AF.Square, [ak], [sqk])
                    yield
                    pb2, pk2 = bank()
                    self.mm(pb2[:, 0:TM], ones, sqb, True, True, ["c_ones", sqk], [pk2])
                    lr = gb_[:, 0:TM]
                    gkk = [gk, gk + "h"]
                    self.ts("dve", lr, pb2[:, 0:TM], RMS_EPS, None, ALU.add, None, [pk2], gkk)
                    yield
                    self.act(lr, lr, AF.Ln, gkk, gkk)
                    yield
                    self.act(lr, lr, AF.Exp, gkk, gkk, scale=-0.5)
                    yield
                    if ch < 4:
                        self.stt(qTn[p][:, ch, :], acc, float(128 ** -0.5), lr, ALU.mult, ALU.mult, [ak] + gkk, ["qTn" + pk_])
                    else:
                        self.tt("dve", kTn[p][:, ch - 4, :], acc, lr, ALU.mult, [ak] + gkk, ["kTn" + pk_])
                else:
                    self.act(vT[p][:, ch - 8, :], acc, AF.Silu, [ak], ["vT" + pk_])
                yield

            for grp in range(3):
                alive = [conv_task(grp * NG + i, i) for i in range(NG)]
                while alive:
                    for t in list(alive):
                        try:
                            next(t)
                        except StopIteration:
                            alive.remove(t)
                    yield

        def make_Y(m):
            p = m % 2
            pk_ = "p%d" % p
            q_ = "p%d" % (1 - p)

            def attn_task(s):
                blk = m * NS + s
                tok = slice(s * 128, (s + 1) * 128)
                mx = mixT[blk % 2]
                mka = "mixTa%d" % (blk % 2)
                dt = dtot[s]
                for g in range(2):
                    gp = slice(g * 64, (g + 1) * 64)
                    dk = "dtot%d_%d" % (s, g)
                    kbs = ([0] if blk > 0 else []) + [1]
                    pts = []
                    for kb in kbs:
                        pb, pk = bank()
                        if kb == 1:
                            kap, kkey = akT[p][gp, s * 128:(s + 1) * 128], "akT" + pk_
                        elif s == 1:
                            kap, kkey = akT[p][gp, 0:128], "akT" + pk_
                        else:
                            kap, kkey = akT[1 - p][gp, 128:256], "akT" + q_
                        self.mm(pb[:], kap, aqT[p][gp, :, tok], True, True, [kkey, "aqT" + pk_], [pk])
                        ei = self.rot_ex % 2
                        pi = self.rot_ex % 4
                        self.rot_ex += 1
                        self.act(exb[ei], pb[:], AF.Exp, [pk], ["exb%d" % ei])
                        self.tt("dve", ptb[pi], exb[ei], emask[:, kb, g, :], ALU.mult, ["exb%d" % ei, "c_emask"], ["ptb%d" % pi])
                        pts.append((kb, ptb[pi], "ptb%d" % pi))
                        yield
                    pbo, pko = bank()
                    pbd, pkd = bank()
                    for idx, (kb, pt, ptk) in enumerate(pts):
                        if kb == 1:
                            vap, vkey = vtok[p][:, s, gp], "vt" + pk_
                        elif s == 1:
                            vap, vkey = vtok[p][:, 0, gp], "vt" + pk_
                        else:
                            vap, vkey = vtok[1 - p][:, 1, gp], "vt" + q_
                        self.mm(pbo[gp, :], vap, pt, idx == 0, idx == len(pts) - 1, [vkey, ptk], [pko])
                    for idx, (kb, pt, ptk) in enumerate(pts):
                        self.mm(pbd[gp, :], ones[:, 0:64], pt, idx == 0, idx == len(pts) - 1, ["c_ones", ptk], [pkd])
                    self.tt("dve", dt[gp, :], pbd[gp, :], esink_b[gp, :], ALU.add, [pkd, "esink_b"], [dk])
                    self.act(dt[gp, :], dt[gp, :], AF.Ln, [dk], [dk])
                    self.act(dt[gp, :], dt[gp, :], AF.Exp, [dk], [dk], scale=-1.0)
                    self.tt("dve", mx[gp, 0:4, :], pbo[gp, :].rearrange("p (c q) -> p c q", c=4), dt[gp, :].rearrange("p (c q) -> p c q", c=4),
                            ALU.mult, [pko, dk], [mka + "_%d" % g])
                    yield

            def small_task_unused(s):
                sm = smalls[p][s]
                smk = "sm%d" % s + pk_
                abk = "ab%d" % s + pk_
                x4, ax, e4, l4, sp4, g4, eb4, beta = [sm[:, 4 * i:4 * i + 4] for i in range(8)]
                edec = sm[:, 32:48]
                necum = sm[:, 48:52]
                self.tt("dve", x4, ab[p][:, s, 0:4], dtb, ALU.add, [abk, "dtb"], [smk + "x"])
                self.stt(ax, x4, -1.0, x4, ALU.mult, ALU.max, [smk + "x"], [smk + "ax"])
                self.act(e4, ax, AF.Exp, [smk + "ax"], [smk + "e"], scale=-1.0)
                self.act(l4, e4, AF.Ln, [smk + "e"], [smk + "l"], bias=1.0)
                self.stt(sp4, x4, 0.0, l4, ALU.max, ALU.add, [smk + "x", smk + "l"], [smk + "sp"])
                self.tt("dve", g4, sp4, nA, ALU.mult, [smk + "sp", "nA"], [smk + "g"])
                self.act(eb4, ab[p][:, s, 4:8], AF.Exp, [abk], [smk + "eb"], scale=-1.0)
                self.ts("dve", eb4, eb4, 1.0, None, ALU.add, None, [smk + "eb"], [smk + "eb"])
                P.op("dve", lambda e, beta=beta, eb4=eb4: e.reciprocal(out=beta, in_=eb4), reads=[smk + "eb"], writes=[smk + "beta"])
                pb, pk = bank()
                for i, msk in enumerate((Ublk, Lblk, Csel0, Csel1)):
                    self.mm(pb[:, 4 * i:4 * i + 4], msk, g4, True, True, ["c_gdn", smk + "g"], [pk])
                self.act(edec, pb[:, 0:16], AF.Exp, [pk], [smk + "edec"])
                self.ts("dve", necum, edec[:, 0:4], -1.0, None, ALU.mult, None, [smk + "edec"], [smk + "necum"])

            def pre_task(s):
                d = Hs[s]
                sm = smalls[p][s]
                smk = "sm%d" % s + pk_
                g4 = sm[:, 20:24]
                beta = sm[:, 28:32]
                edec = sm[:, 32:48]
                tok = slice(s * 128, (s + 1) * 128)
                K_ = lambda nm: "H%d%s" % (s, nm)
                kT_keys = ["kTn" + pk_]
                qT_keys = ["qTn" + pk_]
                vT_keys = ["vT" + pk_]
                kT, qT, vTt = kTn[p], qTn[p], vT[p]
                self.tt("pool", d["fA"], bc_m(Ublk), bc_h(g4), ALU.mult, ["c_gdn", smk + "g"], [K_("fA")])
                pbt, pkt = bank()
                ptv = pbt[:].bitcast(BF16)
                for h in range(4):
                    self.tr(ptv[:, h * 128:(h + 1) * 128], kT[:, h, tok], kT_keys, [pkt])
                for h in range(4):
                    self.tr(ptv[:, 512 + h * 128:512 + (h + 1) * 128], vTt[:, h, tok], vT_keys, [pkt])
                self.tt("dve", d["kd"], h4(ptv[:, 0:512]), bc_h(edec[:, 4:8]), ALU.mult, [pkt, smk + "edec"], [K_("kd")])
                self.cp("act", d["vtf"], h4(ptv[:, 512:1024]), [pkt], [K_("vtf")])
                pbE, pkE = bank()
                for h in range(4):
                    self.mm(pbE[:, h * 128:(h + 1) * 128], Lblk, d["fA"][:, h, :], True, False, ["c_gdn", K_("fA")], [pkE])
                    self.mm(pbE[:, h * 128:(h + 1) * 128], identF, NEGc, False, True, ["c_gdn", "c_idf"], [pkE])
                self.act(d["Ec"], h4(pbE[:]), AF.Exp, [pkE], [K_("Ec")])
                self.tt("pool", d["fA"], d["Ec"], bc_m(SM), ALU.mult, [K_("Ec"), "c_gdn"], [K_("fA")])
                self.tt("pool", d["fA"], d["fA"], bc_h(beta), ALU.mult, [K_("fA"), smk + "beta"], [K_("fA")])
                yield
                pbG, pkG = bank()
                for h in range(4):
                    self.mm(pbG[:, h * 128:(h + 1) * 128], kT[:, h, tok], kT[:, h, tok], True, True, kT_keys, [pkG])
                self.tt("dve", d["Nf"], h4(pbG[:]), d["fA"], ALU.mult, [pkG, K_("fA")], [K_("Nf")])
                self.cp("act", d["Nb"][0], d["Nf"], [K_("Nf")], [K_("Nb0")])
                self.stt(d["Pf"], d["Nf"], -1.0, bc_m(identF), ALU.mult, ALU.add, [K_("Nf"), "c_idf"], [K_("Pf")])
                self.cp("act", d["Pb"], d["Pf"], [K_("Pf")], [K_("Pb")])
                pbI, pkI = bank()
                for h in range(4):
                    self.mm(pbI[:, h * 128:(h + 1) * 128], kT[:, h, tok], qT[:, h, tok], True, True, kT_keys + qT_keys, [pkI])
                self.tt("dve", d["intraT"], h4(pbI[:]), d["Ec"], ALU.mult, [pkI, K_("Ec")], [K_("intraT")])
                yield
                pbt, pkt = bank()
                ptv = pbt[:].bitcast(BF16)
                for h in range(4):
                    self.tr(ptv[:, h * 128:(h + 1) * 128], d["Nb"][0][:, h, :], [K_("Nb0")], [pkt])
                self.cp("act", d["NTb"][0], h4(ptv[:, 0:512]), [pkt], [K_("NTb0")])
                yield

                def square(lev):
                    cur = (lev - 1) % 2
                    nxt = lev % 2
                    kN = [K_("NTb%d" % cur), K_("Nb%d" % cur)]
                    if lev < 5:
                        pbn, pkn = bank()
                        for h in range(4):
                            self.mm(pbn[:, h * 128:(h + 1) * 128], d["NTb"][cur][:, h, :], d["Nb"][cur][:, h, :], True, True, kN, [pkn])
                        self.cp("act", d["Nb"][nxt], h4(pbn[:]), [pkn], [K_("Nb%d" % nxt)])
                    pbn2, pkn2 = bank()
                    for h in range(4):
                        self.mm(pbn2[:, h * 128:(h + 1) * 128], d["Nb"][cur][:, h, :], d["NTb"][cur][:, h, :], True, True, kN, [pkn2])
                    self.cp("act", d["NTb"][nxt], h4(pbn2[:]), [pkn2], [K_("NTb%d" % nxt)])

                def pupd(lev):
                    nxt = lev % 2
                    pbp, pkp = bank()
                    for h in range(4):
                        self.mm(pbp[:, h * 128:(h + 1) * 128], d["NTb"][nxt][:, h, :], d["Pb"][:, h, :], True, True, [K_("NTb%d" % nxt), K_("Pb")], [pkp])
                    self.tt("dve", d["Pf"], d["Pf"], h4(pbp[:]), ALU.add, [pkp, K_("Pf")], [K_("Pf")])
                    self.cp("act", d["Pb"], d["Pf"], [K_("Pf")], [K_("Pb")])

                square(1)
                yield
                for lev in range(2, 6):
                    pupd(lev - 1)
                    square(lev)
                    yield
                pupd(5)
                yield

            def scan_task(s):
                d = Hs[s]
                sm = smalls[p][s]
                smk = "sm%d" % s + pk_
                beta = sm[:, 28:32]
                edec = sm[:, 32:48]
                necum = sm[:, 48:52]
                K_ = lambda nm: "H%d%s" % (s, nm)
                oa = oall[s]
                kT, qT = kTn[p], qTn[p]
                kT_keys = ["kTn" + pk_]
                qT_keys = ["qTn" + pk_]
                okey = "oall%d" % s
                for c in range(2):
                    Rr = slice(c * 64, (c + 1) * 64)
                    ctok = slice(s * 128 + c * 64, s * 128 + (c + 1) * 64)
                    v4 = lambda ap: h4(ap[Rr, :])
                    pb1, pk1 = bank()
                    for h in range(4):
                        self.mm(pb1[Rr, h * 128:(h + 1) * 128], kT[:, h, ctok], Sb[:, h, :], True, True, kT_keys + ["Sb"], [pk1])
                    for h in range(4):
                        self.stt(d["r"][Rr, h, :], pb1[Rr, h * 128:(h + 1) * 128], necum[Rr, h:h + 1], d["vtf"][Rr, h, :], ALU.mult, ALU.add,
                                 [pk1, smk + "necum", K_("vtf")], [K_("r")])
                    pb3, pk3 = bank()
                    for h in range(4):
                        self.mm(pb3[Rr, h * 128:(h + 1) * 128], qT[:, h, ctok], Sb[:, h, :], True, True, qT_keys + ["Sb"], [pk3])
                    self.tt("dve", d["Nf"][Rr], v4(pb3), bc_h(edec[:, 0:4], rows=Rr), ALU.mult, [pk3, smk + "edec"], [K_("Nf")])
                    yield
                    pb2, pk2 = bank()
                    for h in range(4):
                        self.mm(pb2[Rr, h * 128:(h + 1) * 128], d["Pb"][Rr, h, c * 64:(c + 1) * 64], d["r"][Rr, h, :], True, True, [K_("Pb"), K_("r")], [pk2])
                    self.tt("dve", d["vnew"][Rr], v4(pb2), bc_h(beta, rows=Rr), ALU.mult, [pk2, smk + "beta"], [K_("vnew")])
                    yield
                    pb5, pk5 = bank()
                    for h in range(4):
                        self.mm(pb5[:, h * 128:(h + 1) * 128], d["kd"][Rr, h, :], d["vnew"][Rr, h, :], True, True, [K_("kd"), K_("vnew")], [pk5])
                    for h in range(4):
                        self.stt(Sst[:, h, :], Sst[:, h, :], edec[:, 8 + 4 * c + h:9 + 4 * c + h], pb5[:, h * 128:(h + 1) * 128], ALU.mult, ALU.add,
                                 [pk5, "S", smk + "edec"], ["S"])
                    self.cp("act", Sb, Sst, ["S"], ["Sb"])
                    pb4, pk4 = bank()
                    for h in range(4):
                        self.mm(pb4[Rr, h * 128:(h + 1) * 128], d["intraT"][Rr, h, c * 64:(c + 1) * 64], d["vnew"][Rr, h, :], True, True, [K_("intraT"), K_("vnew")], [pk4])
                    self.tt("dve", oa[Rr], d["Nf"][Rr], v4(pb4), ALU.add, [pk4, K_("Nf")], [okey])
                    yield

            def post_task(s):
                blk = m * NS + s
                d = Hs[s]
                sm = smalls[p][s]
                smk = "sm%d" % s + pk_
                ss4 = sm[:, 52:56]
                rstd4 = sm[:, 56:60]
                oa = oall[s]
                okey = "oall%d" % s
                K_ = lambda nm: "H%d%s" % (s, nm)
                mx = mixT[blk % 2]
                mka = "mixTa%d" % (blk % 2)
                mkg = "mixTg%d" % (blk % 2)
                rows = slice(m * TM + s * 128, m * TM + (s + 1) * 128)
                r = blk % 2
                z = zb[r]
                zk = "zb%d" % r
                P.dma("sp", z, src[rows, :], writes=[zk])
                self.act(d["Nf"], oa, AF.Square, [okey], [K_("Nf")])
                P.op("dve", lambda e, ss4=ss4, src_=d["Nf"]: e.tensor_reduce(out=ss4, in_=src_, axis=mybir.AxisListType.X, op=ALU.add),
                     reads=[K_("Nf")], writes=[smk + "ss"])
                yield
                self.ts("dve", rstd4, ss4, 1.0 / 128.0, RMS_EPS, ALU.mult, ALU.add, [smk + "ss"], [smk + "rstd"])
                self.act(rstd4, rstd4, AF.Ln, [smk + "rstd"], [smk + "rstd"])
                self.act(rstd4, rstd4, AF.Exp, [smk + "rstd"], [smk + "rstd"], scale=-0.5)
                yield
                for h in range(4):
                    self.stt(og[:, h, :], oa[:, h, :], rstd4[:, h:h + 1], zg[p][s][:, h * 128:(h + 1) * 128], ALU.mult, ALU.mult,
                             [okey, smk + "rstd", "zg%d" % s + pk_], ["og"])
                yield
                pbt, pkt = bank()
                ptv = pbt[:].bitcast(BF16)
                for h in range(4):
                    self.tr(ptv[:, h * 128:(h + 1) * 128], og[:, h, :], ["og"], [pkt])
                self.act(mx[:, 4:8, :], ptv[:, 0:512].rearrange("p (c t) -> p c t", c=4), AF.Copy, [pkt, "ngcol"], [mkg], scale=ngcol)
                yield
                for n in range(2):
                    pbn, pkn = bank()
                    for c in range(8):
                        self.mm(pbn[:], mx[:, c, :], wo[:, c, n * 512:(n + 1) * 512], c == 0, c == 7, [mka + "_0", mka + "_1", mkg] + ko, [pkn])
                    self.stt(z[:, n * 512:(n + 1) * 512], z[:, n * 512:(n + 1) * 512], ALPHA, pbn[:], ALU.mult, ALU.add, [zk, pkn], [zk])
                    yield

                def tail(z=z, zk=zk, rows=rows):
                    dd = self.ln_tail(z, zk, dst[rows, :])
                    if last:
                        self.finals.append(dd)
                deferredA.append(tail)
                yield

            return {"attn0": (attn_task(0), []), "attn1": (attn_task(1), []),
                    "pre0": (pre_task(0), []), "pre1": (pre_task(1), ["pre0@2"]),
                    "scan0": (scan_task(0), ["pre0"]), "scan1": (scan_task(1), ["pre1", "scan0"]),
                    "post0": (post_task(0), ["scan0", "attn0"]), "post1": (post_task(1), ["scan1", "attn1", "post0"])}

        def run(tasks):
            done = set()
            steps = {n: 0 for n in tasks}
            active = []
            pending = dict(tasks)

            def ok(dep):
                if "@" in dep:
                    n, k = dep.split("@")
                    return n in done or steps.get(n, 0) >= int(k)
                return dep in done or dep not in tasks

            while pending or active:
                for n in list(pending):
                    if all(ok(dp) for dp in pending[n][1]):
                        active.append(n)
                        del pending[n]
                assert active, ("deadlock", list(pending))
                for n in list(active):
                    try:
                        next(tasks[n][0])
                        steps[n] += 1
                    except StopIteration:
                        active.remove(n)
                        done.add(n)
                if deferredA and "X" in tasks and steps.get("X", 0) >= 6:
                    for fn in deferredA:
                        fn()
                    del deferredA[:]

        run({"X": (X_task(0), [])})
        for m in range(nmt):
            tasks = make_Y(m)
            if m + 1 < nmt:
                tasks["X"] = (X_task(m + 1), [])
            run(tasks)
        for fn in deferredA:
            fn()
        del deferredA[:]


def make_consts():
    c = {"c_ident": np.eye(128, dtype=np.float32)}
    j = np.arange(128)[:, None].astype(np.float64)
    i = np.arange(128)[None, :].astype(np.float64)
    em = np.zeros((128, 2, 2, 4, 128), np.float64)
    for g in range(2):
        for cc in range(4):
            h = 4 * g + cc
            slope = 2.0 ** (-8.0 * (h + 1) / 8.0)
            d_cur = i - j
            em[:, 1, g, cc, :] = np.where(d_cur >= 0, np.exp(-slope * d_cur), 0.0)
            d_prev = i + 128 - j
            em[:, 0, g, cc, :] = np.where(d_prev < 128, np.exp(-slope * d_prev), 0.0)
    c["c_emask"] = np.ascontiguousarray(em.reshape(128, 2, 2, 512), dtype=np.float32)
    a = np.arange(128)[:, None]
    b = np.arange(128)[None, :]
    same = (a // 64) == (b // 64)
    gd = np.zeros((128, 6, 128), np.float32)
    gd[:, 0, :] = same & (a <= b)
    gd[:, 1, :] = same & (a > b)
    gd[:, 2, :] = (a // 64 == 0) & (b >= 0)
    gd[:, 3, :] = (a // 64 == 1) & (b >= 0)
    gd[:, 4, :] = np.where(same & (b >= a), 0.0, -1.0e4)
    gd[:, 5, :] = same & (b > a)
    c["c_gdn"] = gd
    return c


def prep_weights(inputs):
    out = {}
    w_in = np.asarray(inputs["w_in"], dtype=np.float32)
    perm = np.empty(512, np.int64)
    for cc in range(4):
        for g in range(2):
            perm[cc * 128 + g * 64: cc * 128 + (g + 1) * 64] = (4 * g + cc) * 64 + np.arange(64)
    w_in_p = w_in.copy()
    w_in_p[:, :, 0:512] = w_in[:, :, perm]
    out["w_in_p"] = np.ascontiguousarray(w_in_p)
    w_out = np.asarray(inputs["w_mix_out"], dtype=np.float32)
    w_out_p = w_out.copy()
    w_out_p[:, 0:512, :] = w_out[:, perm, :]
    out["w_out_p"] = np.ascontiguousarray(w_out_p)
    cw = np.asarray(inputs["conv_w"], dtype=np.float32)
    out["conv_wT"] = np.ascontiguousarray(cw.reshape(cw.shape[0], 4, 12, 128).transpose(0, 3, 2, 1))
    for k, v in inputs.items():
        if k not in ("x", "mem", "w_in", "w_mix_out", "conv_w"):
            out[k] = np.ascontiguousarray(v, dtype=np.float32)
    return out


_CACHE = {}


def kernel(**inputs):
    key = "full"
    if key not in _CACHE:
        _CACHE[key] = Builder().build()
    nc = _CACHE[key]
    shared = prep_weights(inputs)
    shared.update(make_consts())
    in_maps = []
    for c in range(NCORES):
        m = dict(shared)
        m["x"] = np.ascontiguousarray(inputs["x"][c], dtype=np.float32)
        m["mem"] = np.ascontiguousarray(inputs["mem"][c], dtype=np.float32)
        in_maps.append(m)
    res = run_bass_kernel_spmd(nc, in_maps, core_ids=list(range(NCORES)))
    return np.stack([np.asarray(r["out"], dtype=np.float32) for r in res.results], axis=0)
```

```python
import contextlib
import numpy as np
import concourse.bass as bass
import concourse.mybir as mybir
from concourse.bass_utils import run_bass_kernel_spmd

F32 = mybir.dt.float32
BF16 = mybir.dt.bfloat16
AF = mybir.ActivationFunctionType
ALU = mybir.AluOpType

D = 1024
SEQ = 4096
DEPTH = 2
MEM = 256
DIN = 2824
DFF = 4096
ALPHA = float((2 * DEPTH) ** 0.25)
LN_EPS = 1e-5
RMS_EPS = 1e-6
NCORES = 8

ENGS = ("pe", "act", "dve", "pool", "sp")
SEM_ROLL = 30000


class _Op:
    __slots__ = ("eng", "fn", "deps", "is_dma", "sem", "val", "signal", "key", "prefetch")


class Prog:
    def __init__(self, nc):
        self.nc = nc
        self.ops = {e: [] for e in ENGS}
        self.last_w = {}
        self.readers = {}
        self.dma_cnt = {}
        self.all_ops = []

    def _deps(self, op, reads, writes, extra=()):
        pr = [r for r in reads if isinstance(r, str) and r.startswith("psum")]
        if pr:
            reads = [r for r in reads if r not in pr]
            writes = list(writes) + [r for r in pr if r not in writes]
        deps = []
        for r in reads:
            w = self.last_w.get(r)
            if w is not None:
                deps.append((w, "raw"))
        for w_ in writes:
            w = self.last_w.get(w_)
            if w is not None:
                deps.append((w, "waw"))
            for rd in self.readers.get(w_, ()):
                deps.append((rd, "war"))
        for d in extra:
            deps.append((d, "raw"))
        for r in reads:
            self.readers.setdefault(r, []).append(op)
        for w_ in writes:
            self.last_w[w_] = op
            self.readers[w_] = []
        out = []
        seen = set()
        for d, kind in deps:
            if d is op or id(d) in seen:
                continue
            if (not d.is_dma) and (not op.is_dma) and d.eng == op.eng:
                if op.eng == "pe" or (kind != "raw" and op.eng != "pool"):
                    continue
            seen.add(id(d))
            out.append(d)
        op.deps = out

    def op(self, eng, fn, reads=(), writes=(), extra=()):
        o = _Op()
        o.eng = eng
        o.fn = fn
        o.is_dma = False
        o.signal = False
        o.sem = None
        o.val = 0
        o.key = None
        o.prefetch = False
        self._deps(o, reads, writes, extra)
        self.ops[eng].append(o)
        self.all_ops.append(o)
        return o

    def dma(self, eng, out, in_, reads=(), writes=(), key=None, prefetch=False, extra=()):
        o = _Op()
        o.eng = eng
        o.is_dma = True
        o.fn = (out, in_)
        o.signal = True
        o.prefetch = prefetch
        o.key = key if key is not None else tuple(writes)[0]
        self.dma_cnt[o.key] = self.dma_cnt.get(o.key, 0) + 1
        o.val = 16 * self.dma_cnt[o.key]
        o.sem = None
        self._deps(o, reads, writes, extra)
        self.ops[eng].append(o)
        self.all_ops.append(o)
        return o

    def fence(self):
        lasts = []
        for e in ENGS:
            last = None
            for o in reversed(self.ops[e]):
                if not o.is_dma and o.fn is not None:
                    last = o
                    break
            if last is not None:
                lasts.append(last)
        dmas = {}
        for o in self.all_ops:
            if o.is_dma and not o.prefetch:
                dmas[o.key] = o
        deps = lasts + list(dmas.values())
        for e in ENGS:
            o = _Op()
            o.eng = e
            o.fn = None
            o.is_dma = False
            o.signal = False
            o.sem = None
            o.val = 0
            o.key = None
            o.prefetch = False
            o.deps = list(deps)
            self.ops[e].append(o)
            self.all_ops.append(o)

    def emit(self, final_wait_ops=()):
        nc = self.nc
        for o in self.all_ops:
            for d in o.deps:
                d.signal = True
        for o in final_wait_ops:
            o.signal = True
        eng_sems = {}
        for e in ENGS:
            cnt = 0
            k = 0
            for o in self.ops[e]:
                if o.is_dma or not o.signal:
                    continue
                if cnt >= SEM_ROLL:
                    k += 1
                    cnt = 0
                cnt += 1
                o.sem = ("eng", e, k)
                o.val = cnt
                eng_sems[(e, k)] = True
        for o in self.all_ops:
            if o.is_dma:
                o.sem = ("dma", o.key)
        all_sem_keys = [("eng", e, k) for (e, k) in eng_sems] + [("dma", k) for k in self.dma_cnt]
        self.n_sems = len(all_sem_keys)
        stats = {"waits": 0, "ops": 0}
        with contextlib.ExitStack() as st:
            semh = {}
            for i, sk in enumerate(all_sem_keys):
                semh[sk] = st.enter_context(nc.semaphore("s%d" % i))
            block = st.enter_context(nc.Block())
            engmap = {"pe": block.tensor, "act": block.scalar, "dve": block.vector,
                      "pool": block.gpsimd, "sp": block.sync}
            for e in ENGS:
                ops = self.ops[e]
                finals = list(final_wait_ops) if e == "sp" else []
                if not ops and not finals:
                    continue

                def body(engine, ops=ops, finals=finals):
                    waited = {}

                    def wait_for(d):
                        if waited.get(d.sem, 0) >= d.val:
                            return
                        engine.wait_ge(semh[d.sem], d.val)
                        waited[d.sem] = d.val
                        stats["waits"] += 1

                    for o in ops:
                        for d in o.deps:
                            wait_for(d)
                        if o.is_dma:
                            out, in_ = o.fn
                            ins = engine.dma_start(out=out, in_=in_)
                        elif o.fn is None:
                            continue
                        else:
                            ins = o.fn(engine)
                        if o.signal:
                            ins.then_inc(semh[o.sem], 16 if o.is_dma else 1)
                        stats["ops"] += 1
                    for d in finals:
                        wait_for(d)

                engmap[e](body)
        self.stats = stats


class Region:
    def __init__(self, big, base, size):
        self.big, self.base, self.size, self.off = big, base, size, 0

    def reset(self):
        self.off = 0

    def f32(self, n):
        assert self.off + n <= self.size, ("arena overflow", self.off, n, self.size)
        v = self.big[:, self.base + self.off: self.base + self.off + n]
        self.off += n
        return v

    def bf16(self, n):
        assert n % 2 == 0
        return self.f32(n // 2).bitcast(BF16)


KB = 256


class Builder:
    def __init__(self, seq=SEQ, phases=("A", "B", "C"), depth=DEPTH, debug=()):
        self.seq = seq
        self.depth = depth
        self.phases = phases
        self.debug = debug

    def build(self):
        nc = bass.Bass("TRN2", target_bir_lowering=False)
        self.nc = nc
        S = self.seq
        dr = {}

        def din(name, shape):
            dr[name] = nc.dram_tensor(name, list(shape), F32, kind="ExternalInput").ap()

        din("x", [S, D])
        din("mem", [MEM, D])
        din("attn_sinks", [DEPTH, 8])
        din("a_log", [DEPTH, 4])
        din("dt_bias", [DEPTH, 4])
        din("gdn_norm_g", [DEPTH, 128])
        din("wq_mem", [DEPTH, D, D])
        din("wk_mem", [DEPTH, D, D])
        din("wv_mem", [DEPTH, D, D])
        din("wo_mem", [DEPTH, D, D])
        din("w_ff1", [DEPTH, D, DFF])
        din("w_ff2", [DEPTH, DFF, D])
        din("ln_g", [DEPTH, 3, D])
        din("ln_b", [DEPTH, 3, D])
        din("c_ident", [128, 128])
        din("c_emask", [128, 2, 2, 512])
        din("c_gdn", [128, 6, 128])
        din("w_in_p", [DEPTH, D, DIN])
        din("w_out_p", [DEPTH, D, D])
        din("conv_wT", [DEPTH, 128, 12, 4])
        dr["out"] = nc.dram_tensor("out", [S, D], F32, kind="ExternalOutput").ap()
        for nm in ("xs0", "xs1", "ypart"):
            dr[nm] = nc.dram_tensor(nm, [S, D], F32).ap()
        for nm, shp in self.debug:
            dr[nm] = nc.dram_tensor(nm, list(shp), F32, kind="ExternalOutput").ap()
        self.dr = dr

        with contextlib.ExitStack() as st:
            self.st = st
            NF = 207 * KB
            big = st.enter_context(nc.sbuf_tensor("big", [128, NF], F32))
            self.CONST = Region(big, 0, 21 * KB)
            self.WA = Region(big, 21 * KB, 64 * KB)
            self.WB = Region(big, 85 * KB, 64 * KB)
            self.WORK = Region(big, 149 * KB, NF - 149 * KB)
            self.psum = [st.enter_context(nc.psum_tensor("pb%d" % i, [128, 512], F32)) for i in range(8)]
            P = Prog(nc)
            self.P = P
            self.finals = []
            self.setup_consts()
            self.program()
            P.emit(final_wait_ops=self.finals)
            self.stats = dict(P.stats, sems=P.n_sems)
        return nc

    def setup_consts(self):
        P, dr = self.P, self.dr
        C = self.CONST
        idf = C.f32(128)
        self.ident = C.bf16(128)
        P.dma("sp", idf, dr["c_ident"], writes=["c_idf"])
        P.op("dve", lambda e: e.tensor_copy(out=self.ident, in_=idf), reads=["c_idf"], writes=["c_ident"])
        self.gb = [C.f32(1024), C.f32(1024)]
        self.ones = C.bf16(128)
        P.op("pool", lambda e: e.memset(self.ones, 1.0), writes=["c_ones"])
        self.identF = idf
        self.emask = C.f32(2048).rearrange("p (a b q) -> p a b q", a=2, b=2)
        P.dma("sp", self.emask, dr["c_emask"], writes=["c_emask"])
        self.gdnc = C.f32(768).rearrange("p (a q) -> p a q", a=6)
        P.dma("sp", self.gdnc, dr["c_gdn"], writes=["c_gdn"])
        self.rot_ex = 0
        self.tap_mix = None

    def tap(self, name, ap, key):
        if name not in [d[0] for d in self.debug]:
            return
        d = self.P.dma("pool", self.dr[name], ap, reads=list(key) if isinstance(key, list) else [key], writes=[("tap", name)])
        self.finals.append(d)

    def mm(self, out, lhsT, rhs, start, stop, reads, writes):
        return self.P.op("pe", lambda e: e.matmul(out, lhsT=lhsT, rhs=rhs, start=start, stop=stop), reads=reads, writes=writes)

    def tr(self, out, in_, reads, writes):
        return self.P.op("pe", lambda e: e.transpose(out=out, in_=in_, identity=self.ident), reads=list(reads) + ["c_ident"], writes=writes)

    def act(self, out, in_, func, reads, writes, bias=None, scale=None, accum_out=None):
        kw = {}
        if bias is not None:
            kw["bias"] = bias
        if scale is not None:
            kw["scale"] = scale
        if accum_out is not None:
            kw["accum_out"] = accum_out
        return self.P.op("act", lambda e: e.activation(out=out, in_=in_, func=func, **kw), reads=reads, writes=writes)

    def cp(self, eng, out, in_, reads, writes):
        if eng == "act":
            return self.P.op("act", lambda e: e.copy(out=out, in_=in_), reads=reads, writes=writes)
        return self.P.op(eng, lambda e: e.tensor_copy(out=out, in_=in_), reads=reads, writes=writes)

    def tt(self, eng, out, in0, in1, op, reads, writes):
        return self.P.op(eng, lambda e: e.tensor_tensor(out=out, in0=in0, in1=in1, op=op), reads=reads, writes=writes)

    def ts(self, eng, out, in0, s1, s2, op0, op1, reads, writes, accum_out=None):
        if op1 is None:
            return self.P.op(eng, lambda e: e.tensor_scalar(out=out, in0=in0, scalar1=s1, scalar2=None, op0=op0), reads=reads, writes=writes)
        if accum_out is not None:
            return self.P.op(eng, lambda e: e.tensor_scalar(out=out, in0=in0, scalar1=s1, scalar2=s2, op0=op0, op1=op1, accum_out=accum_out), reads=reads, writes=writes)
        return self.P.op(eng, lambda e: e.tensor_scalar(out=out, in0=in0, scalar1=s1, scalar2=s2, op0=op0, op1=op1), reads=reads, writes=writes)

    def stt(self, out, in0, scalar, in1, op0, op1, reads, writes):
        return self.P.op("dve", lambda e: e.scalar_tensor_tensor(out=out, in0=in0, scalar=scalar, in1=in1, op0=op0, op1=op1), reads=reads, writes=writes)

    def load_ln(self, l, i):
        P, dr = self.P, self.dr
        P.dma("sp", self.gb[0], dr["ln_g"][l, i:i + 1, :].broadcast_to([128, D]), writes=["ln_g"])
        P.dma("sp", self.gb[1], dr["ln_b"][l, i:i + 1, :].broadcast_to([128, D]), writes=["ln_b"])

    def load_weight(self, dst, src, key, rows_per_part_chunk=128, prefetch=True):
        P = self.P
        kc = dst.shape[1]
        h = max(1, kc // 2)
        srcv = src.rearrange("(k p) n -> p k n", p=128)
        for j, (a, b) in enumerate(((0, h), (h, kc))):
            if a == b:
                continue
            P.dma("pool", dst[:, a:b, :], srcv[:, a:b, :], writes=[(key, j)], prefetch=prefetch)
        return [(key, 0), (key, 1)] if kc > 1 else [(key, 0)]

    def x_to_xT(self, xin, xin_key, xb, xT, nsub, tag):
        self.cp("act", xb, xin, [xin_key], [tag + "xb"])
        for s in range(nsub):
            pb = self.psum[self.rot_pt % 2]
            pkey = "psum%d" % (self.rot_pt % 2)
            self.rot_pt += 1
            pT = pb[:].bitcast(BF16).rearrange("p (k t) -> p k t", k=8)
            for k in range(8):
                self.tr(pT[:, k, :], xb[:, s, k * 128:(k + 1) * 128], [tag + "xb"], [pkey])
            self.cp("dve", xT[:, :, s * 128:(s + 1) * 128], pT, [pkey], [tag + "xT"])

    def ln_tail(self, z, zkey, dst_rows):
        P = self.P
        r = self.ln_rot % 2
        self.ln_rot += 1
        sm = self.ln_small[r]
        st6 = sm[:, 0:12]
        mv = sm[:, 12:14]
        rstd = sm[:, 14:15]
        nmr = sm[:, 15:16]
        lnv = sm[:, 16:17]
        sk = "ln_small%d" % r
        P.op("dve", lambda e: e.bn_stats(out=st6[:, 0:6], in_=z[:, 0:512]), reads=[zkey], writes=[sk + "a"])
        P.op("dve", lambda e: e.bn_stats(out=st6[:, 6:12], in_=z[:, 512:1024]), reads=[zkey], writes=[sk + "b"])
        P.op("dve", lambda e: e.bn_aggr(out=mv, in_=st6.rearrange("p (a b) -> p a b", a=2)), reads=[sk + "a", sk + "b"], writes=[sk + "mv"])
        self.ts("dve", lnv, mv[:, 1:2], LN_EPS, None, ALU.add, None, [sk + "mv"], [sk + "lnv"])
        self.act(lnv, lnv, AF.Ln, [sk + "lnv"], [sk + "lnv"])
        self.act(rstd, lnv, AF.Exp, [sk + "lnv"], [sk + "rstd"], scale=-0.5)
        self.stt(nmr, mv[:, 0:1], -1.0, rstd, ALU.mult, ALU.mult, [sk + "mv", sk + "rstd"], [sk + "nmr"])
        self.act(z, z, AF.Identity, [zkey, sk + "rstd", sk + "nmr"], [zkey], bias=nmr, scale=rstd)
        self.tt("dve", z, z, self.gb[0], ALU.mult, [zkey, "ln_g"], [zkey])
        self.tt("dve", z, z, self.gb[1], ALU.add, [zkey, "ln_b"], [zkey])
        d = P.dma("sp", dst_rows, z, reads=[zkey], writes=[("st", zkey)])
        return d

    def program(self):
        P, dr = self.P, self.dr
        self.rot_pt = 0
        self.ln_rot = 0
        self.rot_ps = 0
        cur = dr["x"]
        bufs = [dr["xs0"], dr["xs1"]]
        bi = 0
        nl = self.depth
        order = [(l, ph) for l in range(nl) for ph in ("A", "B", "C1", "C2") if ph[0] in self.phases]
        wts = {}

        def load(i):
            if i < len(order) and i not in wts:
                l, ph = order[i]
                wts[i] = getattr(self, "weights_" + ph)(l)

        load(0)
        for i, (l, ph) in enumerate(order):
            last = i == len(order) - 1
            if ph == "A":
                dst = dr["out"] if last else bufs[bi]
                self.compute_A(l, wts[i], cur, dst, last)
                P.fence()
                load(i + 1)
                load(i + 2)
            else:
                if ph == "C1":
                    dst = None
                else:
                    dst = dr["out"] if last else bufs[bi]
                getattr(self, "compute_" + ph)(l, wts[i], cur, dst, last)
                P.fence()
                load(i + 1)
                if i + 1 < len(order) and order[i + 1][1] != "A":
                    load(i + 2)
            if dst is not None:
                cur = dst
                bi ^= 1

    def weights_C1(self, l):
        return self._weights_C(l, 0, self.WA, "WA")

    def weights_C2(self, l):
        return self._weights_C(l, 1, self.WB, "WB")

    def _weights_C(self, l, half, slot, slotname):
        dr = self.dr
        slot.reset()
        w1 = slot.bf16(8 * 2048).rearrange("p (k n) -> p k n", k=8)
        w2 = slot.bf16(16 * 1024).rearrange("p (k n) -> p k n", k=16)
        k1 = self.load_weight(w1, dr["w_ff1"][l][:, half * 2048:(half + 1) * 2048], slotname + "a")
        k2 = self.load_weight(w2, dr["w_ff2"][l][half * 2048:(half + 1) * 2048, :], slotname + "b")
        return (w1, w2, k1, k2)

    def compute_C1(self, l, wts, src, dst, last):
        self._compute_C(l, wts, src, dst, last, 0)

    def compute_C2(self, l, wts, src, dst, last):
        self._compute_C(l, wts, src, dst, last, 1)

    def _compute_C(self, l, wts, src, dst, last, half):
        P, dr = self.P, self.dr
        w1, w2, k1, k2 = wts
        S = self.seq
        TM = 256
        NS = TM // 128
        nmt = S // TM
        W = self.WORK
        W.reset()
        if half == 1:
            self.load_ln(l, 2)
        xin = [W.f32(NS * 1024).rearrange("p (s d) -> p s d", s=NS) for _ in range(2)]
        xb = W.bf16(NS * 1024).rearrange("p (s d) -> p s d", s=NS)
        xTs = [W.bf16(8 * TM).rearrange("p (k t) -> p k t", k=8) for _ in range(2)]
        hTs = [W.bf16(16 * TM).rearrange("p (f t) -> p f t", f=16) for _ in range(2)]
        rtmp = [W.f32(TM) for _ in range(2)]
        zb = [W.f32(1024) for _ in range(2)]
        self.ln_small = [W.f32(32) for _ in range(2)]
        srcv = src.rearrange("(m s p) d -> m p s d", s=NS, p=128)
        deferred = []

        def prep(m):
            self.x_to_xT(xin[m % 2], "xin%d" % (m % 2), xb, xTs[m % 2], NS, "C%d" % (m % 2))

        P.dma("sp", xin[0], srcv[0], writes=["xin0"])
        prep(0)
        for m in range(nmt):
            xi = xin[m % 2]
            xk = "xin%d" % (m % 2)
            xT = xTs[m % 2]
            xtag = "C%d" % (m % 2)
            hT = hTs[m % 2]
            hk = "hT%d" % (m % 2)
            if m + 1 < nmt:
                P.dma("sp", xin[(m + 1) % 2], srcv[m + 1], writes=["xin%d" % ((m + 1) % 2)])
            for f in range(16):
                pb = self.psum[2 + f % 2]
                pk = "psum%d" % (2 + f % 2)
                for k in range(8):
                    self.mm(pb[:, 0:TM], w1[:, k, f * 128:(f + 1) * 128], xT[:, k, :], k == 0, k == 7, [xtag + "xT"] + k1, [pk])
                rt = rtmp[f % 2]
                rk = "rtmp%d" % (f % 2)
                self.act(rt, pb[:, 0:TM], AF.Relu, [pk], [rk])
                self.tt("dve", hT[:, f, :], rt, rt, ALU.mult, [rk], [hk])
            if m + 1 < nmt:
                prep(m + 1)
            for fn in deferred:
                fn()
            deferred = []
            for s in range(NS):
                r = (m * NS + s) % 2
                pys = [self.psum[4 + 2 * r], self.psum[5 + 2 * r]]
                pyk = ["psum%d" % (4 + 2 * r), "psum%d" % (5 + 2 * r)]
                rows = slice(m * TM + s * 128, m * TM + (s + 1) * 128)
                z = zb[r]
                zk = "zb%d" % r
                if half == 1:
                    P.dma("sp", z, dr["ypart"][rows, :], writes=[zk])
                for n in range(2):
                    for f in range(16):
                        self.mm(pys[n][:], hT[:, f, s * 128:(s + 1) * 128], w2[:, f, n * 512:(n + 1) * 512],
                                f == 0, f == 15, [hk] + k2, [pyk[n]])
                if half == 0:
                    for n in range(2):
                        self.stt(z[:, n * 512:(n + 1) * 512], xi[:, s, n * 512:(n + 1) * 512], ALPHA, pys[n][:], ALU.mult, ALU.add, [xk, pyk[n]], [zk])
                    P.dma("sp", dr["ypart"][rows, :], z, reads=[zk], writes=[("st", zk)])
                else:
                    for n in range(2):
                        self.tt("dve", z[:, n * 512:(n + 1) * 512], z[:, n * 512:(n + 1) * 512], pys[n][:], ALU.add, [pyk[n], zk], [zk])

                    def tail(z=z, zk=zk, rows=rows):
                        d = self.ln_tail(z, zk, dst[rows, :])
                        if last:
                            self.finals.append(d)
                    deferred.append(tail)
        for fn in deferred:
            fn()

    def weights_B(self, l):
        dr = self.dr
        slot = self.WB
        slot.reset()
        ws = []
        ks = []
        views = {nm: slot.bf16(8 * 1024).rearrange("p (k n) -> p k n", k=8) for nm in ("wq_mem", "wk_mem", "wv_mem", "wo_mem")}
        keys = {}
        for nm in ("wk_mem", "wv_mem", "wq_mem", "wo_mem"):
            keys[nm] = self.load_weight(views[nm], dr[nm][l], "WB" + nm[1])
        for nm in ("wq_mem", "wk_mem", "wv_mem", "wo_mem"):
            ws.append(views[nm])
            ks.append(keys[nm])
        return ws, ks

    def bank(self):
        i = 2 + self.rot_ps % 4
        self.rot_ps += 1
        return self.psum[i], "psum%d" % i

    def compute_B(self, l, wts, src, dst, last):
        P, dr = self.P, self.dr
        (wq, wk, wv, wo), (kq, kk, kv, ko) = wts
        S = self.seq
        TM = 256
        NS = 2
        nmt = S // TM
        W = self.WORK
        W.reset()
        self.load_ln(l, 1)
        xb = W.bf16(NS * 1024).rearrange("p (s d) -> p s d", s=NS)
        xT = W.bf16(8 * TM).rearrange("p (k t) -> p k t", k=8)
        kTm = W.bf16(8 * MEM).rearrange("p (c m) -> p c m", c=8)
        vm = W.bf16(2 * 1024).rearrange("p (c n) -> p c n", c=2)
        qTs = [W.bf16(8 * TM).rearrange("p (c t) -> p c t", c=8) for _ in range(2)]
        pT = [W.bf16(2 * TM).rearrange("p (c t) -> p c t", c=2) for _ in range(4)]
        rden = [W.f32(TM) for _ in range(4)]
        oTn = W.bf16(8 * TM).rearrange("p (c t) -> p c t", c=8)
        zb = [W.f32(1024) for _ in range(4)]
        self.ln_small = [W.f32(32) for _ in range(2)]
        ones = self.ones
        deferred = []

        def transposes(tag):
            for s in range(NS):
                pb = self.psum[self.rot_pt % 2]
                pkey = "psum%d" % (self.rot_pt % 2)
                self.rot_pt += 1
                pTt = pb[:].bitcast(BF16).rearrange("p (k t) -> p k t", k=8)
                for k in range(8):
                    self.tr(pTt[:, k, :], xb[:, s, k * 128:(k + 1) * 128], ["Bxb"], [pkey])
                self.cp("dve", xT[:, :, s * 128:(s + 1) * 128], pTt, [pkey], ["BxT"])

        P.dma("pool", xb, dr["mem"].rearrange("(s p) d -> p s d", p=128), writes=["Bxb"])
        transposes("B")
        for c in range(8):
            pb, pk = self.bank()
            for k in range(8):
                self.mm(pb[:, 0:MEM], wk[:, k, c * 128:(c + 1) * 128], xT[:, k, :], k == 0, k == 7, ["BxT"] + kk, [pk])
            self.cp("act" if c % 2 else "dve", kTm[:, c, :], pb[:, 0:MEM], [pk], ["kTm"])
        for mc in range(2):
            for n in range(2):
                pb, pk = self.bank()
                for k in range(8):
                    self.mm(pb[:], xT[:, k, mc * 128:(mc + 1) * 128], wv[:, k, n * 512:(n + 1) * 512], k == 0, k == 7, ["BxT"] + kv, [pk])
                self.cp("act" if n % 2 else "dve", vm[:, mc, n * 512:(n + 1) * 512], pb[:], [pk], ["vm"])
        srcv = src.rearrange("(m s p) d -> m p s d", s=NS, p=128)

        def X_task(m):
            qT = qTs[m % 2]
            qk = "qT%d" % (m % 2)
            if m == 0:
                P.dma("pool", xb, srcv[0], writes=["Bxb"])
            transposes("B")
            if m + 1 < nmt:
                P.dma("pool", xb, srcv[m + 1], writes=["Bxb"])
            yield
            for c in range(8):
                pb, pk = self.bank()
                for k in range(8):
                    self.mm(pb[:, 0:TM], wq[:, k, c * 128:(c + 1) * 128], xT[:, k, :], k == 0, k == 7, ["BxT"] + kq, [pk])
                self.act(qT[:, c, :], pb[:, 0:TM], AF.Copy, [pk], [qk], scale=1.0 / 16.0)
                yield

        def head_task(m, h):
            qT = qTs[m % 2]
            qk = "qT%d" % (m % 2)
            pb, pk = self.bank()
            sT = pb[:].rearrange("p (c t) -> p c t", c=2)
            for mc in range(2):
                for dc in range(2):
                    self.mm(sT[:, mc, :], kTm[:, 2 * h + dc, mc * 128:(mc + 1) * 128], qT[:, 2 * h + dc, :], dc == 0, dc == 1, ["kTm", qk], [pk])
            pt = pT[h]
            ptk = "pT%d" % h
            self.act(pt, sT, AF.Exp, [pk], [ptk])
            yield
            pbo, pko = self.bank()
            oT = pbo[:].rearrange("p (c t) -> p c t", c=2)
            for dc in range(2):
                for mc in range(2):
                    self.mm(oT[:, dc, :], vm[:, mc, h * 256 + dc * 128: h * 256 + (dc + 1) * 128], pt[:, mc, :], mc == 0, mc == 1, ["vm", ptk], [pko])
            pbd, pkd = self.bank()
            for mc in range(2):
                self.mm(pbd[:, 0:TM], ones, pt[:, mc, :], mc == 0, mc == 1, ["c_ones", ptk], [pkd])
            rd = rden[h]
            rdk = "rden%d" % h
            self.act(rd, pbd[:, 0:TM], AF.Ln, [pkd], [rdk])
            self.act(rd, rd, AF.Exp, [rdk], [rdk], scale=-1.0)
            for dc in range(2):
                self.tt("dve", oTn[:, 2 * h + dc, :], oT[:, dc, :], rd, ALU.mult, [pko, rdk], ["oTn%d" % h])
            yield

        def load_res(m):
            for s in range(NS):
                rows = slice(m * TM + s * 128, m * TM + (s + 1) * 128)
                r = (m * NS + s) % 4
                P.dma("sp", zb[r], src[rows, :], writes=["zb%d" % r])

        def post_task(m):
            for s in range(NS):
                rows = slice(m * TM + s * 128, m * TM + (s + 1) * 128)
                r = (m * NS + s) % 4
                z = zb[r]
                zk = "zb%d" % r
                for n in range(2):
                    pbn = self.psum[6 + n]
                    pkn = "psum%d" % (6 + n)
                    for c in range(8):
                        self.mm(pbn[:], oTn[:, c, s * 128:(s + 1) * 128], wo[:, c, n * 512:(n + 1) * 512], c == 0, c == 7,
                                ["oTn%d" % (c // 2)] + ko, [pkn])
                    self.stt(z[:, n * 512:(n + 1) * 512], z[:, n * 512:(n + 1) * 512], ALPHA, pbn[:], ALU.mult, ALU.add, [zk, pkn], [zk])
                    yield

                def tail(z=z, zk=zk, rows=rows):
                    d = self.ln_tail(z, zk, dst[rows, :])
                    if last:
                        self.finals.append(d)
                deferred.append(tail)

        def run(groups):
            bg = groups.pop(0)
            for grp in groups:
                alive = list(grp)
                while alive:
                    for t in list(alive):
                        try:
                            next(t)
                        except StopIteration:
                            alive.remove(t)
                    if bg is not None:
                        try:
                            next(bg)
                        except StopIteration:
                            bg = None
            while bg is not None:
                try:
                    next(bg)
                except StopIteration:
                    bg = None

        run([X_task(0), []])
        for m in range(nmt):
            bg = X_task(m + 1) if m + 1 < nmt else None
            load_res(m)
            fl = list(deferred)
            del deferred[:]

            def flush(fl=fl):
                for fn in fl:
                    fn()
                    yield

            run([bg, [head_task(m, h) for h in range(4)] + [flush()], [post_task(m)]])
        for fn in deferred:
            fn()

    def weights_A(self, l):
        dr = self.dr
        slot = self.WA
        slot.reset()
        win = dr["w_in_p"][l]
        spec = (("q", 0, 512), ("k", 512, 128), ("v", 640, 128), ("g", 768, 1536), ("ab", 2304, 8), ("z", 2312, 512))
        w = {}
        ks = {}
        for nm, c0, n in spec:
            w[nm] = slot.bf16(8 * n).rearrange("p (k n) -> p k n", k=8)
            ks[nm] = self.load_weight(w[nm], win[:, c0:c0 + n], "WA" + nm)
        w["o"] = slot.bf16(8 * 1024).rearrange("p (k n) -> p k n", k=8)
        ks["o"] = self.load_weight(w["o"], dr["w_out_p"][l], "WAo")
        return w, ks

    def compute_A(self, l, wts, src, dst, last):
        P, dr = self.P, self.dr
        w, ks = wts
        S = self.seq
        TM = 256
        NS = 2
        nmt = S // TM
        R = Region(self.WB.big, self.WB.base, self.WB.size + self.WORK.size)
        ones, identF = self.ones, self.identF
        Ublk, Lblk, Csel0, Csel1, NEGc, SM = [self.gdnc[:, i, :] for i in range(6)]
        self.rotA = 0

        def bank():
            i = self.rotA % 8
            self.rotA += 1
            return self.psum[i], "psum%d" % i

        def h4(ap):
            return ap.rearrange("p (h d) -> p h d", h=4)

        xb = R.bf16(NS * 1024).rearrange("p (s d) -> p s d", s=NS)
        xT = R.bf16(8 * TM).rearrange("p (k t) -> p k t", k=8)
        aqT = [R.bf16(4 * TM).rearrange("p (c t) -> p c t", c=4) for _ in range(2)]
        akT = [R.bf16(TM) for _ in range(2)]
        vtok = [R.bf16(TM).rearrange("p (b d) -> p b d", b=2) for _ in range(2)]
        qTn = [R.bf16(4 * TM).rearrange("p (h t) -> p h t", h=4) for _ in range(2)]
        kTn = [R.bf16(4 * TM).rearrange("p (h t) -> p h t", h=4) for _ in range(2)]
        vT = [R.bf16(4 * TM).rearrange("p (h t) -> p h t", h=4) for _ in range(2)]
        ab = [R.f32(16).rearrange("p (s c) -> p s c", s=2) for _ in range(2)]
        zg = [[R.bf16(512) for _ in range(2)] for _ in range(2)]
        NG = 4
        gbuf = [R.f32(260) for _ in range(NG)]
        cacc = [R.f32(TM) for _ in range(NG)]
        chalo = R.f32(48).rearrange("p (c j) -> p c j", c=12)
        sq = [R.bf16(TM) for _ in range(NG)]
        zgf = R.f32(512)
        exb = [R.f32(512) for _ in range(2)]
        ptb = [R.bf16(512) for _ in range(4)]
        dtot = [R.f32(512) for _ in range(2)]
        mixT = [R.bf16(8 * 128).rearrange("p (c t) -> p c t", c=8) for _ in range(2)]
        oall = [h4(R.f32(512)) for _ in range(2)]
        og = h4(R.bf16(512))
        Sst = h4(R.f32(512))
        Sb = h4(R.bf16(512))
        zb = [R.f32(1024) for _ in range(2)]
        self.ln_small = [R.f32(32) for _ in range(2)]
        convw = R.f32(48).rearrange("p (c j) -> p c j", c=12)
        esk = R.f32(4)
        esink_b = R.f32(512)
        nA = R.f32(4)
        dtb = R.f32(4)
        ng4 = R.f32(512)
        smalls = [[R.f32(64) for _ in range(2)] for _ in range(2)]
        Hs = []
        for s_ in range(2):
            d = {}
            for nm in ("fA", "Ec", "Nf", "Pf"):
                d[nm] = h4(R.f32(512))
            d["Nb"] = [h4(R.bf16(512)) for _ in range(2)]
            d["NTb"] = [h4(R.bf16(512)) for _ in range(2)]
            for nm in ("Pb", "intraT", "kd", "r", "vnew", "vtf"):
                d[nm] = h4(R.bf16(512))
            Hs.append(d)
        self.arenaA = R.off

        P.dma("sp", convw, dr["conv_wT"][l], writes=["convw"])
        P.dma("sp", esk[0:64, :], dr["attn_sinks"][l:l + 1, 0:4].broadcast_to([64, 4]), writes=["esk0"])
        P.dma("sp", esk[64:128, :], dr["attn_sinks"][l:l + 1, 4:8].broadcast_to([64, 4]), writes=["esk1"])
        P.dma("sp", nA, dr["a_log"][l:l + 1, :].broadcast_to([128, 4]), writes=["nA"])
        P.dma("sp", dtb, dr["dt_bias"][l:l + 1, :].broadcast_to([128, 4]), writes=["dtb"])
        ngcol = ng4[:, 0:1]
        P.dma("sp", ngcol, dr["gdn_norm_g"][l].rearrange("(p o) -> p o", o=1), writes=["ngcol"])
        self.load_ln(l, 0)
        for c in range(4):
            self.act(esink_b[:, c * 128:(c + 1) * 128], identF, AF.Exp, ["esk0", "esk1", "c_idf"], ["esink_b"], bias=esk[:, c:c + 1], scale=0.0)
        self.act(nA, nA, AF.Exp, ["nA"], ["nA"])
        self.ts("dve", nA, nA, -1.0, None, ALU.mult, None, ["nA"], ["nA"])
        P.op("pool", lambda e: e.memset(Sst, 0.0), writes=["S"])
        P.op("pool", lambda e: e.memset(Sb, 0.0), writes=["Sb"])
        P.op("pool", lambda e: e.memset(chalo, 0.0), writes=["chalo%d" % c for c in range(12)])

        emask = self.emask
        srcv = src.rearrange("(m s p) d -> m p s d", s=NS, p=128)
        kq, kk, kv, kg, kab, kz, ko = ks["q"], ks["k"], ks["v"], ks["g"], ks["ab"], ks["z"], ks["o"]
        wq, wk, wv, wg, wab, wz, wo = w["q"], w["k"], w["v"], w["g"], w["ab"], w["z"], w["o"]
        deferredA = []

        def bc_h(ap4, n=128, rows=slice(0, 128)):
            a = ap4[rows, :]
            return a.unsqueeze(2).to_broadcast([a.shape[0], 4, n])

        def bc_m(ap, rows=slice(0, 128)):
            a = ap[rows, :]
            return a.unsqueeze(1).to_broadcast([a.shape[0], 4, a.shape[1]])

        def small_task(p, s):
            pk_ = "p%d" % p
            sm = smalls[p][s]
            smk = "sm%d" % s + pk_
            abk = "ab%d" % s + pk_
            x4, ax, e4, l4, sp4, g4, eb4, beta = [sm[:, 4 * i:4 * i + 4] for i in range(8)]
            edec = sm[:, 32:48]
            necum = sm[:, 48:52]
            self.tt("dve", x4, ab[p][:, s, 0:4], dtb, ALU.add, [abk, "dtb"], [smk + "x"])
            self.stt(ax, x4, -1.0, x4, ALU.mult, ALU.max, [smk + "x"], [smk + "ax"])
            self.act(e4, ax, AF.Exp, [smk + "ax"], [smk + "e"], scale=-1.0)
            self.act(l4, e4, AF.Ln, [smk + "e"], [smk + "l"], bias=1.0)
            self.stt(sp4, x4, 0.0, l4, ALU.max, ALU.add, [smk + "x", smk + "l"], [smk + "sp"])
            self.tt("dve", g4, sp4, nA, ALU.mult, [smk + "sp", "nA"], [smk + "g"])
            self.act(eb4, ab[p][:, s, 4:8], AF.Exp, [abk], [smk + "eb"], scale=-1.0)
            self.ts("dve", eb4, eb4, 1.0, None, ALU.add, None, [smk + "eb"], [smk + "eb"])
            P.op("dve", lambda e, beta=beta, eb4=eb4: e.reciprocal(out=beta, in_=eb4), reads=[smk + "eb"], writes=[smk + "beta"])
            pb, pk = bank()
            for i, msk in enumerate((Ublk, Lblk, Csel0, Csel1)):
                self.mm(pb[:, 4 * i:4 * i + 4], msk, g4, True, True, ["c_gdn", smk + "g"], [pk])
            self.act(edec, pb[:, 0:16], AF.Exp, [pk], [smk + "edec"])
            self.ts("dve", necum, edec[:, 0:4], -1.0, None, ALU.mult, None, [smk + "edec"], [smk + "necum"])

        def X_task(m):
            p = m % 2
            pk_ = "p%d" % p
            P.dma("pool", xb, srcv[m], writes=["Axb"])
            for s in range(NS):
                pb = self.psum[self.rot_pt % 2 * 0 + 0] if False else None
                pbk, pkk = bank()
                pT = pbk[:].bitcast(BF16).rearrange("p (k t) -> p k t", k=8)
                for k in range(8):
                    self.tr(pT[:, k, :], xb[:, s, k * 128:(k + 1) * 128], ["Axb"], [pkk])
                self.cp("act", xT[:, :, s * 128:(s + 1) * 128], pT, [pkk], ["AxT"])
                yield
            for c in range(4):
                pb, pk = bank()
                for k in range(8):
                    self.mm(pb[:, 0:TM], wq[:, k, c * 128:(c + 1) * 128], xT[:, k, :], k == 0, k == 7, ["AxT"] + kq, [pk])
                self.act(aqT[p][:, c, :], pb[:, 0:TM], AF.Copy, [pk], ["aqT" + pk_], scale=0.125)
                yield
            pb, pk = bank()
            for k in range(8):
                self.mm(pb[:, 0:TM], wk[:, k, :], xT[:, k, :], k == 0, k == 7, ["AxT"] + kk, [pk])
            self.cp("act", akT[p], pb[:, 0:TM], [pk], ["akT" + pk_])
            yield
            for s in range(NS):
                pb, pk = bank()
                for k in range(8):
                    self.mm(pb[:, 0:128], xT[:, k, s * 128:(s + 1) * 128], wv[:, k, :], k == 0, k == 7, ["AxT"] + kv, [pk])
                self.cp("act", vtok[p][:, s, :], pb[:, 0:128], [pk], ["vt" + pk_])
                yield
            for s in range(NS):
                pb, pk = bank()
                for k in range(8):
                    self.mm(pb[:, 0:8], xT[:, k, s * 128:(s + 1) * 128], wab[:, k, :], k == 0, k == 7, ["AxT"] + kab, [pk])
                self.cp("dve", ab[p][:, s, :], pb[:, 0:8], [pk], ["ab%d" % s + pk_])
                yield
                pb, pk = bank()
                for k in range(8):
                    self.mm(pb[:], xT[:, k, s * 128:(s + 1) * 128], wz[:, k, :], k == 0, k == 7, ["AxT"] + kz, [pk])
                self.act(zg[p][s], pb[:], AF.Silu, [pk], ["zg%d" % s + pk_])
                yield
                small_task(p, s)
                yield

            def conv_task(ch, slot):
                gb_ = gbuf[slot]
                gk = "gbuf%d" % slot
                hkk = "chalo%d" % ch
                acc = cacc[slot]
                ak = "cacc%d" % slot
                pb, pk = bank()
                for k in range(8):
                    self.mm(pb[:, 0:TM], wg[:, k, ch * 128:(ch + 1) * 128], xT[:, k, :], k == 0, k == 7, ["AxT"] + kg, [pk])
                self.cp("pool", gb_[:, 0:3], chalo[:, ch, 0:3], [hkk], [gk + "h"])
                self.cp("act", gb_[:, 3:259], pb[:, 0:TM], [pk], [gk])
                self.cp("pool", chalo[:, ch, 0:3], gb_[:, 256:259], [gk], [hkk])
                yield
                self.ts("dve", acc, gb_[:, 3:259], convw[:, ch, 3:4], None, ALU.mult, None, [gk, "convw"], [ak])
                yield
                for j in (2, 1, 0):
                    self.stt(acc, gb_[:, j:j + TM], convw[:, ch, j:j + 1], acc, ALU.mult, ALU.add, [gk, gk + "h", "convw", ak], [ak])
                    yield
                if ch < 8:
                    self.act(acc, acc, AF.Silu, [ak], [ak])
                    yield
                    sqb = sq[slot]
                    sqk = "sq%d" % slot
                    self.act(sqb, acc, AF.Square, [ak], [sqk])
                    yield
                    pb2, pk2 = bank()
                    self.mm(pb2[:, 0:TM], ones, sqb, True, True, ["c_ones", sqk], [pk2])
                    lr = gb_[:, 0:TM]
                    gkk = [gk, gk + "h"]
                    self.ts("dve", lr, pb2[:, 0:TM], RMS_EPS, None, ALU.add, None, [pk2], gkk)
                    yield
                    self.act(lr, lr, AF.Ln, gkk, gkk)
                    yield
                    self.act(lr, lr, AF.Exp, gkk, gkk, scale=-0.5)
                    yield
                    if ch < 4:
                        self.stt(qTn[p][:, ch, :], acc, float(128 ** -0.5), lr, ALU.mult, ALU.mult, [ak] + gkk, ["qTn" + pk_])
                    else:
                        self.tt("dve", kTn[p][:, ch - 4, :], acc, lr, ALU.mult, [ak] + gkk, ["kTn" + pk_])
                else:
                    self.act(vT[p][:, ch - 8, :], acc, AF.Silu, [ak], ["vT" + pk_])
                yield

            for grp in range(3):
                alive = [conv_task(grp * NG + i, i) for i in range(NG)]
                while alive:
                    for t in list(alive):
                        try:
                            next(t)
                        except StopIteration:
                            alive.remove(t)
                    yield

        def make_Y(m):
            p = m % 2
            pk_ = "p%d" % p
            q_ = "p%d" % (1 - p)

            def attn_task(s):
                blk = m * NS + s
                tok = slice(s * 128, (s + 1) * 128)
                mx = mixT[blk % 2]
                mka = "mixTa%d" % (blk % 2)
                dt = dtot[s]
                for g in range(2):
                    gp = slice(g * 64, (g + 1) * 64)
                    dk = "dtot%d_%d" % (s, g)
                    kbs = ([0] if blk > 0 else []) + [1]
                    pts = []
                    for kb in kbs:
                        pb, pk = bank()
                        if kb == 1:
                            kap, kkey = akT[p][gp, s * 128:(s + 1) * 128], "akT" + pk_
                        elif s == 1:
                            kap, kkey = akT[p][gp, 0:128], "akT" + pk_
                        else:
                            kap, kkey = akT[1 - p][gp, 128:256], "akT" + q_
                        self.mm(pb[:], kap, aqT[p][gp, :, tok], True, True, [kkey, "aqT" + pk_], [pk])
                        ei = self.rot_ex % 2
                        pi = self.rot_ex % 4
                        self.rot_ex += 1
                        self.act(exb[ei], pb[:], AF.Exp, [pk], ["exb%d" % ei])
                        self.tt("pool", ptb[pi], exb[ei], emask[:, kb, g, :], ALU.mult, ["exb%d" % ei, "c_emask"], ["ptb%d" % pi])
                        pts.append((kb, ptb[pi], "ptb%d" % pi))
                        yield
                    pbo, pko = bank()
                    pbd, pkd = bank()
                    for idx, (kb, pt, ptk) in enumerate(pts):
                        if kb == 1:
                            vap, vkey = vtok[p][:, s, gp], "vt" + pk_
                        elif s == 1:
                            vap, vkey = vtok[p][:, 0, gp], "vt" + pk_
                        else:
                            vap, vkey = vtok[1 - p][:, 1, gp], "vt" + q_
                        self.mm(pbo[gp, :], vap, pt, idx == 0, idx == len(pts) - 1, [vkey, ptk], [pko])
                    for idx, (kb, pt, ptk) in enumerate(pts):
                        self.mm(pbd[gp, :], ones[:, 0:64], pt, idx == 0, idx == len(pts) - 1, ["c_ones", ptk], [pkd])
                    self.tt("dve", dt[gp, :], pbd[gp, :], esink_b[gp, :], ALU.add, [pkd, "esink_b"], [dk])
                    self.act(dt[gp, :], dt[gp, :], AF.Ln, [dk], [dk])
                    self.act(dt[gp, :], dt[gp, :], AF.Exp, [dk], [dk], scale=-1.0)
                    self.tt("dve", mx[gp, 0:4, :], pbo[gp, :].rearrange("p (c q) -> p c q", c=4), dt[gp, :].rearrange("p (c q) -> p c q", c=4),
                            ALU.mult, [pko, dk], [mka + "_%d" % g])
                    yield

            def small_task_unused(s):
                sm = smalls[p][s]
                smk = "sm%d" % s + pk_
                abk = "ab%d" % s + pk_
                x4, ax, e4, l4, sp4, g4, eb4, beta = [sm[:, 4 * i:4 * i + 4] for i in range(8)]
                edec = sm[:, 32:48]
                necum = sm[:, 48:52]
                self.tt("dve", x4, ab[p][:, s, 0:4], dtb, ALU.add, [abk, "dtb"], [smk + "x"])
                self.stt(ax, x4, -1.0, x4, ALU.mult, ALU.max, [smk + "x"], [smk + "ax"])
                self.act(e4, ax, AF.Exp, [smk + "ax"], [smk + "e"], scale=-1.0)
                self.act(l4, e4, AF.Ln, [smk + "e"], [smk + "l"], bias=1.0)
                self.stt(sp4, x4, 0.0, l4, ALU.max, ALU.add, [smk + "x", smk + "l"], [smk + "sp"])
                self.tt("dve", g4, sp4, nA, ALU.mult, [smk + "sp", "nA"], [smk + "g"])
                self.act(eb4, ab[p][:, s, 4:8], AF.Exp, [abk], [smk + "eb"], scale=-1.0)
                self.ts("dve", eb4, eb4, 1.0, None, ALU.add, None, [smk + "eb"], [smk + "eb"])
                P.op("dve", lambda e, beta=beta, eb4=eb4: e.reciprocal(out=beta, in_=eb4), reads=[smk + "eb"], writes=[smk + "beta"])
                pb, pk = bank()
                for i, msk in enumerate((Ublk, Lblk, Csel0, Csel1)):
                    self.mm(pb[:, 4 * i:4 * i + 4], msk, g4, True, True, ["c_gdn", smk + "g"], [pk])
                self.act(edec, pb[:, 0:16], AF.Exp, [pk], [smk + "edec"])
                self.ts("dve", necum, edec[:, 0:4], -1.0, None, ALU.mult, None, [smk + "edec"], [smk + "necum"])

            def pre_task(s):
                d = Hs[s]
                sm = smalls[p][s]
                smk = "sm%d" % s + pk_
                g4 = sm[:, 20:24]
                beta = sm[:, 28:32]
                edec = sm[:, 32:48]
                tok = slice(s * 128, (s + 1) * 128)
                K_ = lambda nm: "H%d%s" % (s, nm)
                kT_keys = ["kTn" + pk_]
                qT_keys = ["qTn" + pk_]
                vT_keys = ["vT" + pk_]
                kT, qT, vTt = kTn[p], qTn[p], vT[p]
                self.tt("pool", d["fA"], bc_m(Ublk), bc_h(g4), ALU.mult, ["c_gdn", smk + "g"], [K_("fA")])
                pbt, pkt = bank()
                ptv = pbt[:].bitcast(BF16)
                for h in range(4):
                    self.tr(ptv[:, h * 128:(h + 1) * 128], kT[:, h, tok], kT_keys, [pkt])
                for h in range(4):
                    self.tr(ptv[:, 512 + h * 128:512 + (h + 1) * 128], vTt[:, h, tok], vT_keys, [pkt])
                self.tt("dve", d["kd"], h4(ptv[:, 0:512]), bc_h(edec[:, 4:8]), ALU.mult, [pkt, smk + "edec"], [K_("kd")])
                self.cp("act", d["vtf"], h4(ptv[:, 512:1024]), [pkt], [K_("vtf")])
                pbE, pkE = bank()
                for h in range(4):
                    self.mm(pbE[:, h * 128:(h + 1) * 128], Lblk, d["fA"][:, h, :], True, False, ["c_gdn", K_("fA")], [pkE])
                    self.mm(pbE[:, h * 128:(h + 1) * 128], identF, NEGc, False, True, ["c_gdn", "c_idf"], [pkE])
                self.act(d["Ec"], h4(pbE[:]), AF.Exp, [pkE], [K_("Ec")])
                self.tt("pool", d["fA"], d["Ec"], bc_m(SM), ALU.mult, [K_("Ec"), "c_gdn"], [K_("fA")])
                self.tt("pool", d["fA"], d["fA"], bc_h(beta), ALU.mult, [K_("fA"), smk + "beta"], [K_("fA")])
                yield
                pbG, pkG = bank()
                for h in range(4):
                    self.mm(pbG[:, h * 128:(h + 1) * 128], kT[:, h, tok], kT[:, h, tok], True, True, kT_keys, [pkG])
                self.tt("dve", d["Nf"], h4(pbG[:]), d["fA"], ALU.mult, [pkG, K_("fA")], [K_("Nf")])
                self.cp("act", d["Nb"][0], d["Nf"], [K_("Nf")], [K_("Nb0")])
                self.stt(d["Pf"], d["Nf"], -1.0, bc_m(identF), ALU.mult, ALU.add, [K_("Nf"), "c_idf"], [K_("Pf")])
                self.cp("act", d["Pb"], d["Pf"], [K_("Pf")], [K_("Pb")])
                pbI, pkI = bank()
                for h in range(4):
                    self.mm(pbI[:, h * 128:(h + 1) * 128], kT[:, h, tok], qT[:, h, tok], True, True, kT_keys + qT_keys, [pkI])
                self.tt("dve", d["intraT"], h4(pbI[:]), d["Ec"], ALU.mult, [pkI, K_("Ec")], [K_("intraT")])
                yield
                pbt, pkt = bank()
                ptv = pbt[:].bitcast(BF16)
                for h in range(4):
                    self.tr(ptv[:, h * 128:(h + 1) * 128], d["Nb"][0][:, h, :], [K_("Nb0")], [pkt])
                self.cp("act", d["NTb"][0], h4(ptv[:, 0:512]), [pkt], [K_("NTb0")])
                yield

                def square(lev):
                    cur = (lev - 1) % 2
                    nxt = lev % 2
                    kN = [K_("NTb%d" % cur), K_("Nb%d" % cur)]
                    if lev < 5:
                        pbn, pkn = bank()
                        for h in range(4):
                            self.mm(pbn[:, h * 128:(h + 1) * 128], d["NTb"][cur][:, h, :], d["Nb"][cur][:, h, :], True, True, kN, [pkn])
                        self.cp("act", d["Nb"][nxt], h4(pbn[:]), [pkn], [K_("Nb%d" % nxt)])
                    pbn2, pkn2 = bank()
                    for h in range(4):
                        self.mm(pbn2[:, h * 128:(h + 1) * 128], d["Nb"][cur][:, h, :], d["NTb"][cur][:, h, :], True, True, kN, [pkn2])
                    self.cp("act", d["NTb"][nxt], h4(pbn2[:]), [pkn2], [K_("NTb%d" % nxt)])

                def pupd(lev):
                    nxt = lev % 2
                    pbp, pkp = bank()
                    for h in range(4):
                        self.mm(pbp[:, h * 128:(h + 1) * 128], d["NTb"][nxt][:, h, :], d["Pb"][:, h, :], True, True, [K_("NTb%d" % nxt), K_("Pb")], [pkp])
                    self.tt("dve", d["Pf"], d["Pf"], h4(pbp[:]), ALU.add, [pkp, K_("Pf")], [K_("Pf")])
                    self.cp("act", d["Pb"], d["Pf"], [K_("Pf")], [K_("Pb")])

                square(1)
                yield
                for lev in range(2, 6):
                    pupd(lev - 1)
                    square(lev)
                    yield
                pupd(5)
                yield

            def scan_task(s):
                d = Hs[s]
                sm = smalls[p][s]
                smk = "sm%d" % s + pk_
                beta = sm[:, 28:32]
                edec = sm[:, 32:48]
                necum = sm[:, 48:52]
                K_ = lambda nm: "H%d%s" % (s, nm)
                oa = oall[s]
                kT, qT = kTn[p], qTn[p]
                kT_keys = ["kTn" + pk_]
                qT_keys = ["qTn" + pk_]
                okey = "oall%d" % s
                for c in range(2):
                    Rr = slice(c * 64, (c + 1) * 64)
                    ctok = slice(s * 128 + c * 64, s * 128 + (c + 1) * 64)
                    v4 = lambda ap: h4(ap[Rr, :])
                    pb1, pk1 = bank()
                    for h in range(4):
                        self.mm(pb1[Rr, h * 128:(h + 1) * 128], kT[:, h, ctok], Sb[:, h, :], True, True, kT_keys + ["Sb"], [pk1])
                    for h in range(4):
                        self.stt(d["r"][Rr, h, :], pb1[Rr, h * 128:(h + 1) * 128], necum[Rr, h:h + 1], d["vtf"][Rr, h, :], ALU.mult, ALU.add,
                                 [pk1, smk + "necum", K_("vtf")], [K_("r")])
                    pb3, pk3 = bank()
                    for h in range(4):
                        self.mm(pb3[Rr, h * 128:(h + 1) * 128], qT[:, h, ctok], Sb[:, h, :], True, True, qT_keys + ["Sb"], [pk3])
                    self.tt("dve", d["Nf"][Rr], v4(pb3), bc_h(edec[:, 0:4], rows=Rr), ALU.mult, [pk3, smk + "edec"], [K_("Nf")])
                    yield
                    pb2, pk2 = bank()
                    for h in range(4):
                        self.mm(pb2[Rr, h * 128:(h + 1) * 128], d["Pb"][Rr, h, c * 64:(c + 1) * 64], d["r"][Rr, h, :], True, True, [K_("Pb"), K_("r")], [pk2])
                    self.tt("dve", d["vnew"][Rr], v4(pb2), bc_h(beta, rows=Rr), ALU.mult, [pk2, smk + "beta"], [K_("vnew")])
                    yield
                    pb5, pk5 = bank()
                    for h in range(4):
                        self.mm(pb5[:, h * 128:(h + 1) * 128], d["kd"][Rr, h, :], d["vnew"][Rr, h, :], True, True, [K_("kd"), K_("vnew")], [pk5])
                    for h in range(4):
                        self.stt(Sst[:, h, :], Sst[:, h, :], edec[:, 8 + 4 * c + h:9 + 4 * c + h], pb5[:, h * 128:(h + 1) * 128], ALU.mult, ALU.add,
                                 [pk5, "S", smk + "edec"], ["S"])
                    self.cp("act", Sb, Sst, ["S"], ["Sb"])
                    pb4, pk4 = bank()
                    for h in range(4):
                        self.mm(pb4[Rr, h * 128:(h + 1) * 128], d["intraT"][Rr, h, c * 64:(c + 1) * 64], d["vnew"][Rr, h, :], True, True, [K_("intraT"), K_("vnew")], [pk4])
                    self.tt("dve", oa[Rr], d["Nf"][Rr], v4(pb4), ALU.add, [pk4, K_("Nf")], [okey])
                    yield

            def post_task(s):
                blk = m * NS + s
                d = Hs[s]
                sm = smalls[p][s]
                smk = "sm%d" % s + pk_
                ss4 = sm[:, 52:56]
                rstd4 = sm[:, 56:60]
                oa = oall[s]
                okey = "oall%d" % s
                K_ = lambda nm: "H%d%s" % (s, nm)
                mx = mixT[blk % 2]
                mka = "mixTa%d" % (blk % 2)
                mkg = "mixTg%d" % (blk % 2)
                rows = slice(m * TM + s * 128, m * TM + (s + 1) * 128)
                r = blk % 2
                z = zb[r]
                zk = "zb%d" % r
                P.dma("sp", z, src[rows, :], writes=[zk])
                self.act(d["Nf"], oa, AF.Square, [okey], [K_("Nf")])
                P.op("dve", lambda e, ss4=ss4, src_=d["Nf"]: e.tensor_reduce(out=ss4, in_=src_, axis=mybir.AxisListType.X, op=ALU.add),
                     reads=[K_("Nf")], writes=[smk + "ss"])
                yield
                self.ts("dve", rstd4, ss4, 1.0 / 128.0, RMS_EPS, ALU.mult, ALU.add, [smk + "ss"], [smk + "rstd"])
                self.act(rstd4, rstd4, AF.Ln, [smk + "rstd"], [smk + "rstd"])
                self.act(rstd4, rstd4, AF.Exp, [smk + "rstd"], [smk + "rstd"], scale=-0.5)
                yield
                for h in range(4):
                    self.stt(og[:, h, :], oa[:, h, :], rstd4[:, h:h + 1], zg[p][s][:, h * 128:(h + 1) * 128], ALU.mult, ALU.mult,
                             [okey, smk + "rstd", "zg%d" % s + pk_], ["og"])
                yield
                pbt, pkt = bank()
                ptv = pbt[:].bitcast(BF16)
                for h in range(4):
                    self.tr(ptv[:, h * 128:(h + 1) * 128], og[:, h, :], ["og"], [pkt])
                self.act(mx[:, 4:8, :], ptv[:, 0:512].rearrange("p (c t) -> p c t", c=4), AF.Copy, [pkt, "ngcol"], [mkg], scale=ngcol)
                yield
                for n in range(2):
                    pbn, pkn = bank()
                    for c in range(8):
                        self.mm(pbn[:], mx[:, c, :], wo[:, c, n * 512:(n + 1) * 512], c == 0, c == 7, [mka + "_0", mka + "_1", mkg] + ko, [pkn])
                    self.stt(z[:, n * 512:(n + 1) * 512], z[:, n * 512:(n + 1) * 512], ALPHA, pbn[:], ALU.mult, ALU.add, [zk, pkn], [zk])
                    yield

                def tail(z=z, zk=zk, rows=rows):
                    dd = self.ln_tail(z, zk, dst[rows, :])
                    if last:
                        self.finals.append(dd)
                deferredA.append(tail)
                yield

            return {"attn0": (attn_task(0), []), "attn1": (attn_task(1), []),
                    "pre0": (pre_task(0), []), "pre1": (pre_task(1), ["pre0@2"]),
                    "scan0": (scan_task(0), ["pre0"]), "scan1": (scan_task(1), ["pre1", "scan0"]),
                    "post0": (post_task(0), ["scan0", "attn0"]), "post1": (post_task(1), ["scan1", "attn1", "post0"])}

        def run(tasks):
            done = set()
            steps = {n: 0 for n in tasks}
            active = []
            pending = dict(tasks)

            def ok(dep):
                if "@" in dep:
                    n, k = dep.split("@")
                    return n in done or steps.get(n, 0) >= int(k)
                return dep in done or dep not in tasks

            while pending or active:
                for n in list(pending):
                    if all(ok(dp) for dp in pending[n][1]):
                        active.append(n)
                        del pending[n]
                assert active, ("deadlock", list(pending))
                for n in list(active):
                    for _ in range(2 if n == "X" else 1):
                        try:
                            next(tasks[n][0])
                            steps[n] += 1
                        except StopIteration:
                            active.remove(n)
                            done.add(n)
                            break
                if deferredA and "X" in tasks and steps.get("X", 0) >= 6:
                    for fn in deferredA:
                        fn()
                    del deferredA[:]

        run({"X": (X_task(0), [])})
        for m in range(nmt):
            tasks = make_Y(m)
            if m + 1 < nmt:
                tasks["X"] = (X_task(m + 1), ["attn0"])
            run(tasks)
        for fn in deferredA:
            fn()
        del deferredA[:]


def make_consts():
    c = {"c_ident": np.eye(128, dtype=np.float32)}
    j = np.arange(128)[:, None].astype(np.float64)
    i = np.arange(128)[None, :].astype(np.float64)
    em = np.zeros((128, 2, 2, 4, 128), np.float64)
    for g in range(2):
        for cc in range(4):
            h = 4 * g + cc
            slope = 2.0 ** (-8.0 * (h + 1) / 8.0)
            d_cur = i - j
            em[:, 1, g, cc, :] = np.where(d_cur >= 0, np.exp(-slope * d_cur), 0.0)
            d_prev = i + 128 - j
            em[:, 0, g, cc, :] = np.where(d_prev < 128, np.exp(-slope * d_prev), 0.0)
    c["c_emask"] = np.ascontiguousarray(em.reshape(128, 2, 2, 512), dtype=np.float32)
    a = np.arange(128)[:, None]
    b = np.arange(128)[None, :]
    same = (a // 64) == (b // 64)
    gd = np.zeros((128, 6, 128), np.float32)
    gd[:, 0, :] = same & (a <= b)
    gd[:, 1, :] = same & (a > b)
    gd[:, 2, :] = (a // 64 == 0) & (b >= 0)
    gd[:, 3, :] = (a // 64 == 1) & (b >= 0)
    gd[:, 4, :] = np.where(same & (b >= a), 0.0, -1.0e4)
    gd[:, 5, :] = same & (b > a)
    c["c_gdn"] = gd
    return c


def prep_weights(inputs):
    out = {}
    w_in = np.asarray(inputs["w_in"], dtype=np.float32)
    perm = np.empty(512, np.int64)
    for cc in range(4):
        for g in range(2):
            perm[cc * 128 + g * 64: cc * 128 + (g + 1) * 64] = (4 * g + cc) * 64 + np.arange(64)
    w_in_p = w_in.copy()
    w_in_p[:, :, 0:512] = w_in[:, :, perm]
    out["w_in_p"] = np.ascontiguousarray(w_in_p)
    w_out = np.asarray(inputs["w_mix_out"], dtype=np.float32)
    w_out_p = w_out.copy()
    w_out_p[:, 0:512, :] = w_out[:, perm, :]
    out["w_out_p"] = np.ascontiguousarray(w_out_p)
    cw = np.asarray(inputs["conv_w"], dtype=np.float32)
    out["conv_wT"] = np.ascontiguousarray(cw.reshape(cw.shape[0], 4, 12, 128).transpose(0, 3, 2, 1))
    for k, v in inputs.items():
        if k not in ("x", "mem", "w_in", "w_mix_out", "conv_w"):
            out[k] = np.ascontiguousarray(v, dtype=np.float32)
    return out


_CACHE = {}


def kernel(**inputs):
    key = "full"
    if key not in _CACHE:
        _CACHE[key] = Builder().build()
    nc = _CACHE[key]
    shared = prep_weights(inputs)
    shared.update(make_consts())
    in_maps = []
    for c in range(NCORES):
        m = dict(shared)
        m["x"] = np.ascontiguousarray(inputs["x"][c], dtype=np.float32)
        m["mem"] = np.ascontiguousarray(inputs["mem"][c], dtype=np.float32)
        in_maps.append(m)
    res = run_bass_kernel_spmd(nc, in_maps, core_ids=list(range(NCORES)))
    return np.stack([np.asarray(r["out"], dtype=np.float32) for r in res.results], axis=0)
```
